# Optimizing a Trainium2 kernel written in Bass

```python
import math
import jax, jax.numpy as jnp
from jax import lax
import numpy as np

D_MODEL = 2048
BATCH = 1
SEQ = 8192
DEPTH = 4

CHUNK = 64
N_MIXERS = 3
Q_BLOCK = 128
TOKEN_BLOCK = 128
EPS = 1e-6

CONV_WIDTH = 31

HG_HEADS = 16
HG_DK = D_MODEL // HG_HEADS
HG_DV = D_MODEL // HG_HEADS

MLA_HEADS = 16
Q_LORA = 512
KV_LORA = 512
QK_NOPE = 128
QK_ROPE = 64
V_HEAD = 128
ROPE_THETA = 10000.0

PEER_HEADS = 8
N_KEYS = 128
N_EXPERTS = N_KEYS * N_KEYS
PEER_DK = 256
PEER_TOPK = 16

N_CONV = (DEPTH + 2) // 3
N_HGRN = (DEPTH + 1) // 3
N_MLA = DEPTH // 3

kernel_name = "hybrid_conv_hgrn2_mla_peer_trunk"

F32 = jnp.float32


def rms_norm(x, g):
    xf = x.astype(F32)
    y = xf * lax.rsqrt(jnp.mean(xf * xf, axis=-1, keepdims=True) + EPS)
    return (y * g.astype(F32)).astype(x.dtype)


def layer_norm(x, g, b):
    xf = x.astype(F32)
    mu = jnp.mean(xf, axis=-1, keepdims=True)
    var = jnp.mean(jnp.square(xf - mu), axis=-1, keepdims=True)
    y = (xf - mu) * lax.rsqrt(var + EPS)
    return (y * g.astype(F32) + b.astype(F32)).astype(x.dtype)


def causal_depthwise_conv(x, w, b):
    y = lax.conv_general_dilated(
        x, w[:, None, :], window_strides=(1,), padding=[(CONV_WIDTH - 1, 0)],
        dimension_numbers=('NWC', 'WIO', 'NWC'), feature_group_count=x.shape[-1])
    return y + b


def conformer_conv(h, w_pw1, b_pw1, w_dw, b_dw, ln_g, ln_b, w_pw2, b_pw2):
    y = h @ w_pw1 + b_pw1
    val, gate = jnp.split(y, 2, axis=-1)
    y = val * jax.nn.sigmoid(gate)
    y = causal_depthwise_conv(y, w_dw, b_dw)
    y = jax.nn.silu(layer_norm(y, ln_g, ln_b))
    return y @ w_pw2 + b_pw2


def hgrn_lower_bounds(logits):
    p = jax.nn.softmax(logits.astype(F32), axis=0)
    return jnp.cumsum(p, axis=0) - p[0]


def gla_chunk_scan(q, k, v, logf):
    _, B, H, C, DK = q.shape
    DV = v.shape[-1]
    causal = jnp.tril(jnp.ones((C, C), dtype=bool))[:, :, None]

    def step(state, inp):
        qc, kc, vc, lc = inp
        b = jnp.cumsum(lc, axis=-2)
        diff = b[..., :, None, :] - b[..., None, :, :]
        decay = jnp.exp(jnp.where(causal, diff, -jnp.inf))
        attn = jnp.einsum('bhtd,bhsd,bhtsd->bhts', qc, kc, decay)
        o = (jnp.einsum('bhts,bhsv->bhtv', attn, vc)
             + jnp.einsum('bhtd,bhdv->bhtv', qc * jnp.exp(b), state))
        b_last = b[..., -1:, :]
        state = (jnp.exp(b_last[..., 0, :])[..., None] * state
                 + jnp.einsum('bhsd,bhsv->bhdv', kc * jnp.exp(b_last - b), vc))
        return state, o

    s0 = jnp.zeros((B, H, DK, DV), F32)
    _, o = lax.scan(step, s0, (q, k, v, logf))
    return o


def hgrn2_mixer(h, w_in, lb, norm_g, w_out):
    B, S, _ = h.shape
    q, f, i, g = jnp.split(h @ w_in, 4, axis=-1)
    lb = lb.astype(F32)
    logf = jnp.logaddexp(jnp.log(lb), jnp.log1p(-lb) + jax.nn.log_sigmoid(f.astype(F32)))
    k = -jnp.expm1(logf)

    def heads(t, d):
        return t.reshape(B, S // CHUNK, CHUNK, HG_HEADS, d).transpose(1, 0, 3, 2, 4)

    qc = heads(q.astype(F32) * (HG_DK ** -0.5), HG_DK)
    o = gla_chunk_scan(qc, heads(k, HG_DK), heads(i.astype(F32), HG_DV), heads(logf, HG_DK))
    o = o.transpose(1, 0, 3, 2, 4).reshape(B, S, HG_HEADS, HG_DV)
    o = rms_norm(o, norm_g) * jax.nn.silu(g.astype(F32).reshape(B, S, HG_HEADS, HG_DV))
    return o.reshape(B, S, D_MODEL).astype(h.dtype) @ w_out


def rope_tables(pos):
    inv_freq = ROPE_THETA ** (-jnp.arange(0, QK_ROPE, 2, dtype=F32) / QK_ROPE)
    ang = pos.astype(F32)[..., None] * inv_freq
    return jnp.cos(ang)[:, :, None, :], jnp.sin(ang)[:, :, None, :]


def apply_rope(x, cos, sin):
    x1, x2 = jnp.split(x.astype(F32), 2, axis=-1)
    return jnp.concatenate([x1 * cos - x2 * sin, x2 * cos + x1 * sin], axis=-1).astype(x.dtype)


def chunk_causal_attention(q, k, v):
    B, S, H, DH = q.shape
    nb = S // Q_BLOCK
    scale = DH ** -0.5
    key_chunk = jnp.arange(S) // CHUNK
    qb = q.reshape(B, nb, Q_BLOCK, H, DH).transpose(1, 0, 2, 3, 4)

    def one_block(args):
        q_blk, blk = args
        q_chunk = (blk * Q_BLOCK + jnp.arange(Q_BLOCK)) // CHUNK
        s = jnp.einsum('bqhd,bkhd->bhqk', q_blk, k, preferred_element_type=F32) * scale
        s = jnp.where(key_chunk[None, :] <= q_chunk[:, None], s, -jnp.inf)
        p = jax.nn.softmax(s, axis=-1)
        return jnp.einsum('bhqk,bkhd->bqhd', p.astype(v.dtype), v)

    o = lax.map(one_block, (qb, jnp.arange(nb)))
    return o.transpose(1, 0, 2, 3, 4).reshape(B, S, H, v.shape[-1])


def mla_mixer(h, pos, w_in, q_norm_g, kv_norm_g, w_uq, w_ukv, w_o):
    B, S, _ = h.shape
    c_q, c_kv, k_rope = jnp.split(h @ w_in, [Q_LORA, Q_LORA + KV_LORA], axis=-1)
    q = (rms_norm(c_q, q_norm_g) @ w_uq).reshape(B, S, MLA_HEADS, QK_NOPE + QK_ROPE)
    kv = (rms_norm(c_kv, kv_norm_g) @ w_ukv).reshape(B, S, MLA_HEADS, QK_NOPE + V_HEAD)
    q_nope, q_rope = jnp.split(q, [QK_NOPE], axis=-1)
    k_nope, v = jnp.split(kv, [QK_NOPE], axis=-1)
    cos, sin = rope_tables(pos)
    q_rope = apply_rope(q_rope, cos, sin)
    k_rope = apply_rope(k_rope[:, :, None, :], cos, sin)
    q = jnp.concatenate([q_nope, q_rope], axis=-1)
    k = jnp.concatenate([k_nope, jnp.broadcast_to(k_rope, (B, S, MLA_HEADS, QK_ROPE))], axis=-1)
    o = chunk_causal_attention(q, k, v)
    return o.reshape(B, S, MLA_HEADS * V_HEAD) @ w_o


def peer_ffn(h, w_q, sub_keys, u, v):
    B, S, D = h.shape
    q = (h @ w_q).astype(F32).reshape(B, S, PEER_HEADS, 2, PEER_DK // 2)
    scores = jnp.einsum('bshpd,hpnd->bshpn', q, sub_keys.astype(F32))
    top_s, top_i = lax.top_k(scores, PEER_TOPK)
    cand_s = top_s[..., 0, :, None] + top_s[..., 1, None, :]
    cand_i = top_i[..., 0, :, None] * N_KEYS + top_i[..., 1, None, :]
    cand_s = cand_s.reshape(B, S, PEER_HEADS, PEER_TOPK * PEER_TOPK)
    cand_i = cand_i.reshape(B, S, PEER_HEADS, PEER_TOPK * PEER_TOPK)
    best_s, best_pos = lax.top_k(cand_s, PEER_TOPK)
    expert_idx = jnp.take_along_axis(cand_i, best_pos, axis=-1)
    gates = jax.nn.softmax(best_s, axis=-1)

    T = B * S
    nblk = T // TOKEN_BLOCK
    E = PEER_HEADS * PEER_TOPK
    h_b = h.reshape(nblk, TOKEN_BLOCK, D)
    idx_b = expert_idx.reshape(nblk, TOKEN_BLOCK, E)
    g_b = gates.reshape(nblk, TOKEN_BLOCK, E)

    def block(args):
        h_blk, idx_blk, g_blk = args
        u_sel = jnp.take(u, idx_blk, axis=0)
        a = jnp.einsum('td,ted->te', h_blk, u_sel, preferred_element_type=F32)
        w = (jax.nn.gelu(a) * g_blk).astype(v.dtype)
        v_sel = jnp.take(v, idx_blk, axis=0)
        return jnp.einsum('te,ted->td', w, v_sel)

    out = lax.map(block, (h_b, idx_b, g_b))
    return out.reshape(B, S, D)


def setup_inputs(seed: int = 0) -> dict:
    key = jax.random.key(seed)
    ks = iter(jax.random.split(key, 40))
    D = D_MODEL

    def nrm(shape, scale):
        return jax.random.normal(next(ks), shape, F32) * scale

    def gain(shape):
        return 1.0 + nrm(shape, 0.02)

    x = nrm((BATCH, SEQ, D), 1.0)
    c = nrm((BATCH, D), 1.0)
    offsets = jax.random.randint(next(ks), (BATCH, 1), 0, 4096, dtype=jnp.int32)
    positions = offsets + jnp.arange(SEQ, dtype=jnp.int32)[None, :]

    inp = {
        "x": x, "c": c, "positions": positions,
        "ada_w": nrm((DEPTH, D, 6 * D), 0.5 * D ** -0.5),
        "ada_b": nrm((DEPTH, 6 * D), 0.01),
        "norm_g": gain((DEPTH, 2, D)),
        "conv_w_pw1": nrm((N_CONV, D, 2 * D), D ** -0.5),
        "conv_b_pw1": nrm((N_CONV, 2 * D), 0.01),
        "conv_w_dw": nrm((N_CONV, CONV_WIDTH, D), CONV_WIDTH ** -0.5),
        "conv_b_dw": nrm((N_CONV, D), 0.01),
        "conv_ln_g": gain((N_CONV, D)),
        "conv_ln_b": nrm((N_CONV, D), 0.01),
        "conv_w_pw2": nrm((N_CONV, D, D), D ** -0.5),
        "conv_b_pw2": nrm((N_CONV, D), 0.01),
        "hg_w_in": nrm((N_HGRN, D, 4 * D), D ** -0.5),
        "hg_lb_logits": nrm((DEPTH, D), 0.1),
        "hg_norm_g": gain((N_HGRN, HG_DV)),
        "hg_w_out": nrm((N_HGRN, D, D), D ** -0.5),
        "mla_w_in": nrm((N_MLA, D, Q_LORA + KV_LORA + QK_ROPE), D ** -0.5),
        "mla_q_norm_g": gain((N_MLA, Q_LORA)),
        "mla_kv_norm_g": gain((N_MLA, KV_LORA)),
        "mla_w_uq": nrm((N_MLA, Q_LORA, MLA_HEADS * (QK_NOPE + QK_ROPE)), Q_LORA ** -0.5),
        "mla_w_ukv": nrm((N_MLA, KV_LORA, MLA_HEADS * (QK_NOPE + V_HEAD)), KV_LORA ** -0.5),
        "mla_w_o": nrm((N_MLA, MLA_HEADS * V_HEAD, D), (MLA_HEADS * V_HEAD) ** -0.5),
        "peer_w_q": nrm((DEPTH, D, PEER_HEADS * PEER_DK), D ** -0.5),
        "peer_sub_keys": nrm((DEPTH, PEER_HEADS, 2, N_KEYS, PEER_DK // 2), (PEER_DK // 2) ** -0.5),
        "peer_u": nrm((DEPTH, N_EXPERTS, D), D ** -0.5),
        "peer_v": nrm((DEPTH, N_EXPERTS, D), PEER_TOPK ** -0.5),
        "final_g": gain((D,)),
    }
    return inp


def reference(x, c, positions, ada_w, ada_b, norm_g,
              conv_w_pw1, conv_b_pw1, conv_w_dw, conv_b_dw, conv_ln_g, conv_ln_b, conv_w_pw2, conv_b_pw2,
              hg_w_in, hg_lb_logits, hg_norm_g, hg_w_out,
              mla_w_in, mla_q_norm_g, mla_kv_norm_g, mla_w_uq, mla_w_ukv, mla_w_o,
              peer_w_q, peer_sub_keys, peer_u, peer_v, final_g):
    lb_all = hgrn_lower_bounds(hg_lb_logits)
    cond = jax.nn.silu(c)
    h = x
    for i in range(DEPTH):
        mod = (cond @ ada_w[i] + ada_b[i])[:, None, :]
        sh1, sc1, g1, sh2, sc2, g2 = jnp.split(mod, 6, axis=-1)
        kind = i % N_MIXERS
        slot = i // N_MIXERS

        u = rms_norm(h, norm_g[i, 0]) * (1.0 + sc1) + sh1
        if kind == 0:
            y = conformer_conv(u, conv_w_pw1[slot], conv_b_pw1[slot], conv_w_dw[slot], conv_b_dw[slot],
                               conv_ln_g[slot], conv_ln_b[slot], conv_w_pw2[slot], conv_b_pw2[slot])
        elif kind == 1:
            y = hgrn2_mixer(u, hg_w_in[slot], lb_all[i], hg_norm_g[slot], hg_w_out[slot])
        else:
            y = mla_mixer(u, positions, mla_w_in[slot], mla_q_norm_g[slot], mla_kv_norm_g[slot],
                          mla_w_uq[slot], mla_w_ukv[slot], mla_w_o[slot])
        h = h + g1 * y

        u = rms_norm(h, norm_g[i, 1]) * (1.0 + sc2) + sh2
        h = h + g2 * peer_ffn(u, peer_w_q[i], peer_sub_keys[i], peer_u[i], peer_v[i])
    return rms_norm(h, final_g)
```

```python
import math
from contextlib import ExitStack

import numpy as np
import concourse.bass as bass
import concourse.mybir as mybir
from concourse.bass_utils import run_bass_kernel_spmd

F32 = mybir.dt.float32
BF16 = mybir.dt.bfloat16
I32 = mybir.dt.int32
U32 = mybir.dt.uint32
AF = mybir.ActivationFunctionType
ALU = mybir.AluOpType
AX = mybir.AxisListType

D = 2048
KC = 16
DEPTH = 4
EPS = 1e-6
CONV_W = 31
NEG = -1.0e30


class Buf:
    __slots__ = ("w", "r")

    def __init__(self):
        self.w = None
        self.r = {}


class Prog:
    NDMA = 8

    def __init__(self, nc, es):
        self.nc = nc
        self.es = es
        self.E = {"pe": nc.tensor, "dve": nc.vector, "act": nc.scalar, "pool": nc.gpsimd, "sp": nc.sync}
        self.sems = {}
        self.cnt = {}
        for e in self.E:
            self.sems[e] = es.enter_context(nc.semaphore("s_" + e))
            self.cnt[e] = 0
        self.dq = {}
        for q in ("sp", "pool", "act"):
            slots = []
            for i in range(self.NDMA):
                k = "d_%s_%d" % (q, i)
                self.sems[k] = es.enter_context(nc.semaphore(k))
                slots.append(k)
            self.dq[q] = [slots, 0, [0] * self.NDMA]
        self.seen = {e: {} for e in self.E}
        self.ninst = 0

    def sb(self, name, shape, dt):
        return self.es.enter_context(self.nc.sbuf_tensor(name, list(shape), dt))

    def ps(self, name, shape, dt=F32):
        return self.es.enter_context(self.nc.psum_tensor(name, list(shape), dt))

    def _need(self, eng, k, v):
        if eng == "pe" and k == "pe":
            return
        if self.seen[eng].get(k, 0) < v:
            self.E[eng].wait_ge(self.sems[k], v)
            self.seen[eng][k] = v
            self.ninst += 1

    def _deps(self, eng, reads, writes):
        for b in reads:
            if b.w is not None:
                self._need(eng, b.w[0], b.w[1])
        for b in writes:
            if b.w is not None and not b.r:
                self._need(eng, b.w[0], b.w[1])
            for k, v in b.r.items():
                self._need(eng, k, v)

    def _record(self, tok, reads, writes):
        k, v = tok
        for b in reads:
            if b.r.get(k, 0) < v:
                b.r[k] = v
        for b in writes:
            b.w = tok
            b.r = {}

    def op(self, eng, fn, reads=(), writes=()):
        self._deps(eng, reads, writes)
        inst = fn(self.E[eng])
        self.cnt[eng] += 1
        inst.then_inc(self.sems[eng], 1)
        self.ninst += 1
        self._record((eng, self.cnt[eng]), reads, writes)

    def dma(self, q, fn, reads=(), writes=(), slot=None, implied_slot_free=False):
        slots, nxt, counts = self.dq[q]
        if slot is None:
            s = nxt % self.NDMA
            self.dq[q][1] = nxt + 1
        else:
            s = slot
        k = slots[s]
        self._deps(q, reads, writes)
        if implied_slot_free:
            self.seen[q][k] = max(self.seen[q].get(k, 0), counts[s] * 16)
        else:
            self._need(q, k, counts[s] * 16)
        inst = fn(self.E[q])
        counts[s] += 1
        inst.then_inc(self.sems[k], 16)
        self.ninst += 1
        self._record((k, counts[s] * 16), reads, writes)

    def wait_all(self, eng, bufs):
        self._deps(eng, bufs, bufs)

    def barrier(self):
        for e in self.E:
            for k in self.E:
                if k != e:
                    self._need(e, k, self.cnt[k])
            for q in self.dq:
                slots, _, counts = self.dq[q]
                for s, k in enumerate(slots):
                    self._need(e, k, counts[s] * 16)


def _bc(ap, shape):
    return ap.to_broadcast(list(shape))


class Model:
    def __init__(self, S, stages):
        self.S = S
        self.stages = stages
        self.nc = bass.Bass("TRN2", target_bir_lowering=False)
        self.din = {}

    def inp(self, name, shape, dt=F32):
        t = self.nc.dram_tensor(name, list(shape), dt, kind="ExternalInput")
        self.din[name] = t
        return t

    def build(self):
        nc = self.nc
        S = self.S
        NB = S // 128
        with ExitStack() as es:
            P = Prog(nc, es)
            self.P = P
            xT = self.inp("xT", [128, KC, S])
            cT = self.inp("cT", [128, KC])
            pos = self.inp("pos", [1, S], I32)
            ada_w = self.inp("ada_w", [DEPTH, KC, 128, 6 * D])
            vecs = self.inp("vecs", [128, self.NV])
            ident_d = self.inp("ident", [128, 128])
            outT = nc.dram_tensor("outT", [128, KC, S], F32, kind="ExternalOutput")
            hS = nc.dram_tensor("hS", [128, KC, S], F32)
            self.hbufs = {"x": [Buf() for _ in range(NB)], "h": [Buf() for _ in range(NB)],
                          "o": [Buf() for _ in range(NB)]}
            self.hten = {"x": xT, "h": hS, "o": outT}
            self.ident = P.sb("ident_s", [128, 128], F32)
            self.b_const = Buf()
            P.dma("sp", lambda e: e.dma_start(out=self.ident[:], in_=ident_d.ap()), writes=[self.b_const])
            self.ones = P.sb("ones", [128, 128], F32)
            P.op("pool", lambda e: e.memset(self.ones[:], 1.0), writes=[self.b_const])
            self.epsb = P.sb("epsb", [128, 1], F32)
            P.op("pool", lambda e: e.memset(self.epsb[:], EPS), writes=[self.b_const])
            self.vec = P.sb("vec", [128, self.NV], F32)
            self.b_vec = Buf()
            P.dma("sp", lambda e: e.dma_start(out=self.vec[:], in_=vecs.ap()), writes=[self.b_vec])
            self.pb = [P.ps("pb%d" % i, [128, 512]) for i in range(8)]
            self.pbb = [Buf() for _ in range(8)]
            self.mods = P.sb("mods", [128, DEPTH, 96], F32)
            self.b_mods = Buf()
            self._stage_mods(cT, ada_w)
            for st in self.stages:
                kind, li = st[0], st[1]
                if kind == "conv":
                    s = li // 3
                    if "conv_w_pw1_%d" % s not in self.din:
                        self.inp("conv_w_pw1_%d" % s, [32, 128, KC, 128])
                        self.inp("conv_w_pw2_%d" % s, [16, 128, KC, 128])
                if kind == "peer":
                    if "peer_wq_%d" % li not in self.din:
                        self.inp("peer_wq_%d" % li, [16, 128, KC, 128])
                        self.inp("peer_skT_%d" % li, [128, 16, 128])
                        self.inp("peer_u_%d" % li, [16384, D])
                        self.inp("peer_v_%d" % li, [16384, D])
                if kind == "hgrn":
                    self.inp("hg_w_qfg", [48, 128, KC, 128])
                    self.inp("hg_w_i", [16, 128, KC, 128])
                    self.inp("hg_w_out", [16, 128, KC, 128])
                if kind in ("hgrn", "mla") and "tri" not in self.din:
                    self.inp("tri", [128, 128])
                if kind == "mla":
                    self.declare_mla()
            for st in self.stages:
                kind, li, src, dst = st
                getattr(self, "_stage_" + kind)(li, src, dst)
            P.wait_all("sp", self.hbufs["o"])
            self.ninst = P.ninst
        return nc

    NV = 0
    VOFF = {}

    def v(self, name, j=0, n=1):
        o = self.VOFF[name] + j
        return self.vec[:, o:o + n]

    def _stage_mods(self, cT, ada_w):
        P = self.P
        with ExitStack() as es:
            P2 = P
            nc = self.nc
            sc = es.enter_context(nc.sbuf_tensor("m_sc", [128, KC, 2], F32))
            c0 = es.enter_context(nc.sbuf_tensor("m_c0", [128, KC], F32))
            wt = [es.enter_context(nc.sbuf_tensor("m_wt%d" % i, [128, KC, 512], F32)) for i in range(2)]
            bsc, bc0 = Buf(), Buf()
            bwt = [Buf(), Buf()]
            P.dma("sp", lambda e: e.dma_start(out=c0[:], in_=cT.ap()), writes=[bc0])
            for j in range(2):
                P.op("act", lambda e, j=j: e.activation(out=sc[:, :, j], in_=c0[:], func=AF.Silu),
                     reads=[bc0], writes=[bsc])
            n = 0
            for li in range(DEPTH):
                for cb in range(6 * D // 512):
                    w = wt[n % 2]
                    bw = bwt[n % 2]
                    n += 1
                    P.dma("sp", lambda e, w=w, li=li, cb=cb: e.dma_start(
                        out=w[:], in_=ada_w.ap()[li, :, :, cb * 512:(cb + 1) * 512].rearrange("k p c -> p k c")),
                        writes=[bw])
                    bank = self.pb[n % 2]
                    bb = self.pbb[n % 2]
                    for j in range(4):
                        for kc in range(KC):
                            P.op("pe", lambda e, w=w, j=j, kc=kc, bank=bank: e.matmul(
                                bank[:, 2 * j:2 * j + 2], lhsT=w[:, kc, j * 128:(j + 1) * 128], rhs=sc[:, kc, :],
                                start=(kc == 0), stop=(kc == KC - 1)),
                                reads=[bw, bsc], writes=[bb])
                    for j in range(4):
                        col = cb * 4 + j
                        P.op("dve", lambda e, j=j, col=col, li=li, bank=bank: e.tensor_tensor(
                            out=self.mods[:, li, col:col + 1], in0=bank[:, 2 * j:2 * j + 1],
                            in1=self.v("ada_b", li * 96 + col), op=ALU.add),
                            reads=[bb, self.b_vec], writes=[self.b_mods])
            P.barrier()

    def mod(self, li, s, kc):
        return self.mods[:, li, s * 16 + kc:s * 16 + kc + 1]

    debug = False
    dumps = None

    def dump(self, name, ap, shape, dt, reads):
        if not self.debug:
            return
        if self.dumps is None:
            self.dumps = {}
        if name in self.dumps:
            return
        t = self.nc.dram_tensor("dbg_" + name, list(shape), dt, kind="ExternalOutput")
        self.dumps[name] = t
        b = Buf()
        self.P.dma("sp", lambda e: e.dma_start(out=t.ap(), in_=ap), reads=reads, writes=[b])
        self.hbufs["o"].append(b)

    def hbl(self, which, t0, n):
        return self.hbufs[which][t0 // 128:(t0 + n) // 128]

    def load_h(self, which, t0, N, h, bh):
        ten = self.hten[which]
        self.P.dma("sp", lambda e: e.dma_start(out=h[:, :, :N], in_=ten.ap()[:, :, t0:t0 + N]),
                   reads=self.hbl(which, t0, N), writes=[bh])

    def store_h(self, which, t0, N, h, bh):
        ten = self.hten[which]
        self.P.dma("sp", lambda e: e.dma_start(out=ten.ap()[:, :, t0:t0 + N], in_=h[:, :, :N]),
                   reads=[bh], writes=self.hbl(which, t0, N))

    def make_A(self, es, li, half):
        P = self.P
        A = es.enter_context(self.nc.sbuf_tensor("A_%d_%d" % (li, half), [128, KC], F32))
        bA = Buf()
        s_sc = 1 + 3 * half
        P.op("dve", lambda e: e.tensor_scalar(out=A[:], in0=self.mods[:, li, s_sc * 16:(s_sc + 1) * 16],
                                              scalar1=1.0, scalar2=None, op0=ALU.add),
             reads=[self.b_mods], writes=[bA])
        P.op("dve", lambda e: e.tensor_tensor(out=A[:], in0=A[:], in1=self.v("norm_g", (li * 2 + half) * 16, 16),
                                              op=ALU.mult),
             reads=[bA, self.b_vec], writes=[bA])
        return A, bA

    def colsum(self, srcs, nk, N, bank, bbank, sq, bsq, square=True, extra_reads=()):
        P = self.P
        for kc in range(nk):
            s = sq[kc % 2]
            bs = bsq[kc % 2]
            if square:
                P.op("act", lambda e, s=s, kc=kc: e.activation(out=s[:, :N], in_=srcs(kc), func=AF.Square),
                     reads=list(extra_reads), writes=[bs])
                rhs = s[:, :N]
                rd = [bs, self.b_const]
            else:
                rhs = srcs(kc)
                rd = list(extra_reads) + [self.b_const]
            P.op("pe", lambda e, rhs=rhs, kc=kc: e.matmul(bank[:, :N], lhsT=self.ones[:], rhs=rhs,
                                                          start=(kc == 0), stop=(kc == nk - 1)),
                 reads=rd, writes=[bbank])

    def norm_mod(self, es, h, bh, N, A, bA, li, half, out, bout, tag):
        P = self.P
        nc = self.nc
        sq = [es.enter_context(nc.sbuf_tensor("%s_sq%d" % (tag, i), [128, 512], F32)) for i in range(2)]
        bsq = [Buf(), Buf()]
        r = es.enter_context(nc.sbuf_tensor("%s_r" % tag, [128, 512], F32))
        br = Buf()
        s_sh = 3 * half

        def run():
            bank, bb = self.pb[7], self.pbb[7]
            self.colsum(lambda kc: h[:, kc, :N], KC, N, bank, bb, sq, bsq, extra_reads=[bh])
            P.op("act", lambda e: e.activation(out=r[:, :N], in_=bank[:, :N], func=AF.Sqrt, bias=self.epsb[:],
                                               scale=1.0 / D), reads=[bb, self.b_const], writes=[br])
            P.op("dve", lambda e: e.reciprocal(out=r[:, :N], in_=r[:, :N]), reads=[br], writes=[br])
            for kc in range(KC):
                s = sq[kc % 2]
                bs = bsq[kc % 2]
                P.op("dve", lambda e, s=s, kc=kc: e.scalar_tensor_tensor(
                    out=s[:, :N], in0=h[:, kc, :N], scalar=A[:, kc:kc + 1], in1=r[:, :N], op0=ALU.mult, op1=ALU.mult),
                    reads=[bh, bA, br], writes=[bs])
                P.op("act", lambda e, s=s, kc=kc: e.activation(
                    out=out[:, kc, :N], in_=s[:, :N], func=AF.Identity, bias=self.mod(li, s_sh, kc), scale=1.0),
                    reads=[bs, self.b_mods], writes=[bout])
        return run

    def linear_fm(self, es, wd, KCin, mlist, x, bx, N, epi, tag, banks=(0, 1, 2, 3), f32=False, x_col0=0):
        P = self.P
        nc = self.nc
        wst = [es.enter_context(nc.sbuf_tensor("%s_ws%d" % (tag, i), [128, KCin, 128], F32)) for i in range(2)]
        bws = [Buf(), Buf()]
        if not f32:
            wbf = [es.enter_context(nc.sbuf_tensor("%s_wb%d" % (tag, i), [128, KCin, 128], BF16)) for i in range(2)]
            bwb = [Buf(), Buf()]

        def run(mlist=mlist, x=x, bx=bx, N=N, epi=epi, x_col0=x_col0):
            for i, mc in enumerate(mlist):
                st = self._wrot
                self._wrot += 1
                w, bw = wst[st % 2], bws[st % 2]
                P.dma("sp", lambda e, w=w, mc=mc: e.dma_start(out=w[:], in_=wd.ap()[mc]), writes=[bw])
                if f32:
                    wm, bm = w, bw
                else:
                    wm, bm = wbf[st % 2], bwb[st % 2]
                    P.op("pool", lambda e, w=w, wm=wm: e.tensor_copy(out=wm[:], in_=w[:]), reads=[bw], writes=[bm])
                bk = banks[st % len(banks)]
                bank, bb = self.pb[bk], self.pbb[bk]
                for kc in range(KCin):
                    P.op("pe", lambda e, wm=wm, kc=kc, bank=bank: e.matmul(
                        bank[:, :N], lhsT=wm[:, kc, :], rhs=x[:, kc, x_col0:x_col0 + N],
                        start=(kc == 0), stop=(kc == KCin - 1)),
                        reads=[bm, bx], writes=[bb])
                epi(i, mc, bank, bb)
        return run

    _wrot = 0

    def _stage_conv(self, li, src, dst):
        P = self.P
        nc = self.nc
        S = self.S
        slot = li // 3
        TG = 512
        wd1 = self.din["conv_w_pw1_%d" % slot]
        wd2 = self.din["conv_w_pw2_%d" % slot]
        pre = "conv%d_" % slot
        with ExitStack() as es:
            sbt = lambda n, s, d: es.enter_context(nc.sbuf_tensor("cv%d_" % li + n, list(s), d))
            h = sbt("h", [128, KC, TG], F32); bh = Buf()
            u = sbt("u", [128, KC, TG], BF16); bu = Buf()
            glu = sbt("glu", [128, KC, 30 + TG], F32); bglu = [Buf() for _ in range(KC)]
            y = sbt("y", [128, KC, TG], F32); by = [Buf() for _ in range(KC)]
            z = sbt("z", [128, KC, TG], BF16); bz = Buf()
            sig = [sbt("sig%d" % i, [128, TG], F32) for i in range(2)]; bsig = [Buf(), Buf()]
            mean = sbt("mean", [128, TG], F32); bmean = Buf()
            rstd = sbt("rstd", [128, TG], F32); brstd = Buf()
            tmp = [sbt("tmp%d" % i, [128, TG], F32) for i in range(2)]; btmp = [Buf(), Buf()]
            gb = sbt("gb", [128, KC], F32); bgb = Buf()
            A, bA = self.make_A(es, li, 0)
            P.op("dve", lambda e: e.tensor_tensor(out=gb[:], in0=self.mods[:, li, 32:48], in1=self.v(pre + "b_pw2", 0, 16),
                                                  op=ALU.mult), reads=[self.b_mods, self.b_vec], writes=[bgb])
            P.op("pool", lambda e: e.memset(glu[:, :, 0:30], 0.0), writes=bglu)
            nm = self.norm_mod(es, h, bh, TG, A, bA, li, 0, u, bu, "cvn%d" % li)

            def epi1(i, mc, bank, bb):
                if mc >= KC:
                    j = mc - KC
                    s, bs = sig[j % 2], bsig[j % 2]
                    P.op("act", lambda e: e.activation(out=s[:], in_=bank[:, :TG], func=AF.Sigmoid,
                                                       bias=self.v(pre + "b_pw1", mc), scale=1.0),
                         reads=[bb, self.b_vec], writes=[bs])
                else:
                    j = mc
                    s, bs = sig[j % 2], bsig[j % 2]
                    P.op("dve", lambda e: e.scalar_tensor_tensor(
                        out=glu[:, j, 30:30 + TG], in0=bank[:, :TG], scalar=self.v(pre + "b_pw1", mc), in1=s[:],
                        op0=ALU.add, op1=ALU.mult), reads=[bb, bs, self.b_vec], writes=[bglu[j]])
            ml1 = []
            for j in range(KC):
                ml1 += [KC + j, j]
            lin1 = self.linear_fm(es, wd1, KC, ml1, u, bu, TG, epi1, "cv1_%d" % li)

            def epi2(i, mc, bank, bb):
                P.op("dve", lambda e: e.scalar_tensor_tensor(
                    out=h[:, mc, :], in0=bank[:, :TG], scalar=self.mod(li, 2, mc), in1=h[:, mc, :],
                    op0=ALU.mult, op1=ALU.add), reads=[bb, self.b_mods, bh], writes=[bh])
                P.op("dve", lambda e: e.tensor_scalar(out=h[:, mc, :], in0=h[:, mc, :], scalar1=gb[:, mc:mc + 1],
                                                      scalar2=None, op0=ALU.add), reads=[bh, bgb], writes=[bh])
            lin2 = self.linear_fm(es, wd2, KC, list(range(KC)), z, bz, TG, epi2, "cv2_%d" % li)

            for g in range(S // TG):
                t0 = g * TG
                self.load_h(src, t0, TG, h, bh)
                nm()
                lin1()
                for j in range(KC):
                    P.op("dve", lambda e, j=j: e.tensor_scalar(
                        out=y[:, j, :], in0=glu[:, j, 0:TG], scalar1=self.v(pre + "w_dw", j * CONV_W + 0),
                        scalar2=self.v(pre + "b_dw", j), op0=ALU.mult, op1=ALU.add),
                        reads=[bglu[j], self.b_vec], writes=[by[j]])
                    for w in range(1, CONV_W):
                        P.op("dve", lambda e, j=j, w=w: e.scalar_tensor_tensor(
                            out=y[:, j, :], in0=glu[:, j, w:w + TG], scalar=self.v(pre + "w_dw", j * CONV_W + w),
                            in1=y[:, j, :], op0=ALU.mult, op1=ALU.add),
                            reads=[bglu[j], by[j], self.b_vec], writes=[by[j]])
                    P.op("pool", lambda e, j=j: e.tensor_copy(out=glu[:, j, 0:30], in_=glu[:, j, TG:TG + 30]),
                         reads=[bglu[j]], writes=[bglu[j]])
                self.colsum(lambda kc: y[:, kc, :], KC, TG, self.pb[4], self.pbb[4], tmp, btmp, square=False,
                            extra_reads=by)
                self.colsum(lambda kc: y[:, kc, :], KC, TG, self.pb[5], self.pbb[5], tmp, btmp, square=True,
                            extra_reads=by)
                P.op("act", lambda e: e.activation(out=mean[:], in_=self.pb[4][:, :TG], func=AF.Copy, scale=1.0 / D),
                     reads=[self.pbb[4]], writes=[bmean])
                P.op("dve", lambda e: e.tensor_tensor(out=rstd[:], in0=mean[:], in1=mean[:], op=ALU.mult),
                     reads=[bmean], writes=[brstd])
                P.op("dve", lambda e: e.scalar_tensor_tensor(out=rstd[:], in0=self.pb[5][:, :TG], scalar=1.0 / D,
                                                             in1=rstd[:], op0=ALU.mult, op1=ALU.subtract),
                     reads=[self.pbb[5], brstd], writes=[brstd])
                P.op("act", lambda e: e.activation(out=rstd[:], in_=rstd[:], func=AF.Sqrt, bias=self.epsb[:], scale=1.0),
                     reads=[brstd, self.b_const], writes=[brstd])
                P.op("dve", lambda e: e.reciprocal(out=rstd[:], in_=rstd[:]), reads=[brstd], writes=[brstd])
                for j in range(KC):
                    t, bt = tmp[j % 2], btmp[j % 2]
                    P.op("dve", lambda e, j=j, t=t: e.tensor_tensor(out=t[:], in0=y[:, j, :], in1=mean[:], op=ALU.subtract),
                         reads=[by[j], bmean], writes=[bt])
                    P.op("dve", lambda e, t=t: e.tensor_tensor(out=t[:], in0=t[:], in1=rstd[:], op=ALU.mult),
                         reads=[bt, brstd], writes=[bt])
                    P.op("act", lambda e, j=j, t=t: e.activation(out=z[:, j, :], in_=t[:], func=AF.Silu,
                                                                 bias=self.v(pre + "ln_b", j), scale=self.v(pre + "ln_g", j)),
                         reads=[bt, self.b_vec], writes=[bz])
                lin2()
                self.store_h(dst, t0, TG, h, bh)
            P.barrier()

    def _stage_final(self, li, src, dst):
        P = self.P
        nc = self.nc
        S = self.S
        TG = 512
        with ExitStack() as es:
            sbt = lambda n, s, d: es.enter_context(nc.sbuf_tensor("fn_" + n, list(s), d))
            h = sbt("h", [128, KC, TG], F32); bh = Buf()
            o = sbt("o", [128, KC, TG], F32); bo = Buf()
            sq = [sbt("sq%d" % i, [128, TG], F32) for i in range(2)]; bsq = [Buf(), Buf()]
            r = sbt("r", [128, TG], F32); br = Buf()
            for g in range(S // TG):
                t0 = g * TG
                self.load_h(src, t0, TG, h, bh)
                bank, bb = self.pb[7], self.pbb[7]
                self.colsum(lambda kc: h[:, kc, :], KC, TG, bank, bb, sq, bsq, extra_reads=[bh])
                P.op("act", lambda e: e.activation(out=r[:], in_=bank[:, :TG], func=AF.Sqrt, bias=self.epsb[:],
                                                   scale=1.0 / D), reads=[bb, self.b_const], writes=[br])
                P.op("dve", lambda e: e.reciprocal(out=r[:], in_=r[:]), reads=[br], writes=[br])
                for kc in range(KC):
                    P.op("dve", lambda e, kc=kc: e.scalar_tensor_tensor(
                        out=o[:, kc, :], in0=h[:, kc, :], scalar=self.v("final_g", kc), in1=r[:],
                        op0=ALU.mult, op1=ALU.mult), reads=[bh, br, self.b_vec], writes=[bo])
                self.store_h(dst, t0, TG, o, bo)
            P.barrier()


_VEC_SPEC = [("ada_b", 4 * 96), ("norm_g", 4 * 2 * 16), ("final_g", 16),
             ("conv0_b_pw1", 32), ("conv0_w_dw", 16 * CONV_W), ("conv0_b_dw", 16), ("conv0_ln_g", 16),
             ("conv0_ln_b", 16), ("conv0_b_pw2", 16),
             ("conv1_b_pw1", 32), ("conv1_w_dw", 16 * CONV_W), ("conv1_b_dw", 16), ("conv1_ln_g", 16),
             ("conv1_ln_b", 16), ("conv1_b_pw2", 16),
             ("hg_lb", 4 * 16), ("hg_norm_g", 1), ("mla_qg", 4), ("mla_kvg", 4), ("inv_freq", 1)]
_o = 0
for _n, _w in _VEC_SPEC:
    Model.VOFF[_n] = _o
    _o += _w
Model.NV = _o


def _fvec(v):
    v = np.asarray(v, np.float32).reshape(-1)
    return v.reshape(-1, 128).T


def _wtiles(W):
    K, M = W.shape
    return np.ascontiguousarray(W.reshape(K // 128, 128, M // 128, 128).transpose(2, 1, 0, 3))


def pack_vecs(inp):
    parts = {}
    parts["ada_b"] = np.concatenate([_fvec(inp["ada_b"][i]) for i in range(4)], axis=1)
    parts["norm_g"] = np.concatenate([_fvec(inp["norm_g"][i, j]) for i in range(4) for j in range(2)], axis=1)
    parts["final_g"] = _fvec(inp["final_g"])
    for s in range(2):
        p = "conv%d_" % s
        parts[p + "b_pw1"] = _fvec(inp["conv_b_pw1"][s])
        wdw = np.asarray(inp["conv_w_dw"][s], np.float32)
        parts[p + "w_dw"] = wdw.T.reshape(16, 128, CONV_W).transpose(1, 0, 2).reshape(128, 16 * CONV_W)
        parts[p + "b_dw"] = _fvec(inp["conv_b_dw"][s])
        parts[p + "ln_g"] = _fvec(inp["conv_ln_g"][s])
        parts[p + "ln_b"] = _fvec(inp["conv_ln_b"][s])
        parts[p + "b_pw2"] = _fvec(inp["conv_b_pw2"][s])
    parts["hg_lb"] = np.concatenate([_fvec(inp["hg_lb_logits"][i]) for i in range(4)], axis=1)
    parts["hg_norm_g"] = np.asarray(inp["hg_norm_g"][0], np.float32).reshape(128, 1)
    parts["mla_qg"] = _fvec(inp["mla_q_norm_g"][0])
    parts["mla_kvg"] = _fvec(inp["mla_kv_norm_g"][0])
    invf = np.zeros((128, 1), np.float32)
    invf[:32, 0] = (10000.0 ** (-np.arange(0, 64, 2, dtype=np.float32) / 64.0)).astype(np.float32)
    invf[32:64, 0] = invf[:32, 0]
    parts["inv_freq"] = invf
    cols = []
    for n, w in _VEC_SPEC:
        a = np.asarray(parts[n], np.float32)
        assert a.shape == (128, w), (n, a.shape, w)
        cols.append(a)
    return np.ascontiguousarray(np.concatenate(cols, axis=1))


def pack_inputs(inp, S, needed):
    m = {}
    x = np.asarray(inp["x"], np.float32)[0, :S]
    m["xT"] = np.ascontiguousarray(x.T.reshape(KC, 128, S).transpose(1, 0, 2))
    m["cT"] = np.ascontiguousarray(_fvec(inp["c"][0]))
    m["pos"] = np.ascontiguousarray(np.asarray(inp["positions"], np.int32)[:, :S])
    m["ada_w"] = np.asarray(inp["ada_w"], np.float32).reshape(DEPTH, KC, 128, 6 * D)
    m["vecs"] = pack_vecs(inp)
    m["ident"] = np.eye(128, dtype=np.float32)
    for s in range(2):
        if "conv_w_pw1_%d" % s in needed:
            m["conv_w_pw1_%d" % s] = _wtiles(np.asarray(inp["conv_w_pw1"][s], np.float32))
            m["conv_w_pw2_%d" % s] = _wtiles(np.asarray(inp["conv_w_pw2"][s], np.float32))
    if "tri" in needed:
        m["tri"] = np.triu(np.ones((128, 128), np.float32))
    if "mla_w_in" in needed:
        w = np.asarray(inp["mla_w_in"][0], np.float32)
        wp = np.zeros((D, 9 * 128), np.float32)
        wp[:, :1088] = w
        m["mla_w_in"] = _wtiles(wp)
        wq = np.asarray(inp["mla_w_uq"][0], np.float32).reshape(512, 16, 192)
        wqp = np.zeros((512, 16, 2, 128), np.float32)
        wqp[:, :, 0, :] = wq[:, :, 0:128]
        wqp[:, :, 1, 0:64] = wq[:, :, 128:192]
        m["mla_w_uq"] = _wtiles(wqp.reshape(512, 32 * 128))
        m["mla_w_ukv"] = _wtiles(np.asarray(inp["mla_w_ukv"][0], np.float32))
        m["mla_w_o"] = _wtiles(np.asarray(inp["mla_w_o"][0], np.float32))
        ps = np.zeros((64, 64), np.float32)
        ps[np.arange(32) + 32, np.arange(32)] = 1.0
        ps[np.arange(32), np.arange(32) + 32] = 1.0
        m["pswap"] = ps
        dmk = np.ones((128, 128), np.float32)
        dmk[64:, :64] = 0.0
        m["dmask"] = dmk
        sg = np.ones((128, 1), np.float32)
        sg[:32] = -1.0
        m["rsgn"] = sg
    if "hg_w_qfg" in needed:
        w_in = np.asarray(inp["hg_w_in"][0], np.float32)
        m["hg_w_qfg"] = _wtiles(np.concatenate([w_in[:, 0:D], w_in[:, D:2 * D], w_in[:, 3 * D:4 * D]], axis=1))
        m["hg_w_i"] = _wtiles(w_in[:, 2 * D:3 * D])
        m["hg_w_out"] = _wtiles(np.asarray(inp["hg_w_out"][0], np.float32))
    for li in range(DEPTH):
        if "peer_wq_%d" % li in needed:
            m["peer_wq_%d" % li] = _wtiles(np.asarray(inp["peer_w_q"][li], np.float32))
            sk = np.asarray(inp["peer_sub_keys"][li], np.float32).reshape(16, 128, 128)
            m["peer_skT_%d" % li] = np.ascontiguousarray(sk.transpose(2, 0, 1))
            m["peer_u_%d" % li] = np.asarray(inp["peer_u"][li], np.float32)
            m["peer_v_%d" % li] = np.asarray(inp["peer_v"][li], np.float32)
    return m


def unpack_out(outT, S):
    return np.ascontiguousarray(outT.transpose(2, 1, 0).reshape(S, D))[None]


def run_model(inputs, S, stages, debug=False):
    m = Model(S, stages)
    m.debug = debug
    nc = m.build()
    in_map = pack_inputs(inputs, S, set(m.din.keys()))
    in_map = {k: in_map[k] for k in m.din}
    res = run_bass_kernel_spmd(nc, [in_map], core_ids=[0])
    m.results = res.results[0]
    return unpack_out(res.results[0]["outT"], S), m


FULL_STAGES = [("conv", 0, "x", "h"), ("peer", 0, "h", "h"),
               ("hgrn", 1, "h", "h"), ("peer", 1, "h", "h"),
               ("mla", 2, "h", "h"), ("peer", 2, "h", "h"),
               ("conv", 3, "h", "h"), ("peer", 3, "h", "h"),
               ("final", 0, "h", "o")]


def kernel(**inputs):
    out, _ = run_model(inputs, 8192, FULL_STAGES)
    return out.astype(np.float32)


def _peer_stage(self, li, src, dst):
    P = self.P
    nc = self.nc
    S = self.S
    TG = 256
    NBLK = TG // 128
    R = 8
    wq = self.din["peer_wq_%d" % li]
    skd = self.din["peer_skT_%d" % li]
    utab_f = self.din["peer_u_%d" % li]
    vtab_f = self.din["peer_v_%d" % li]
    if getattr(self, "ubf", None) is None:
        self.ubf = nc.dram_tensor("peer_ubf", [16384, D], BF16)
        self.vbf = nc.dram_tensor("peer_vbf", [16384, D], BF16)
    utab, vtab = self.ubf, self.vbf
    btab = Buf()
    with ExitStack() as es:
        JR = 2
        stg = [es.enter_context(nc.sbuf_tensor("pc%d_s%d" % (li, i), [128, JR, D], F32)) for i in range(2)]
        bstg = [Buf(), Buf()]
        cvt = [es.enter_context(nc.sbuf_tensor("pc%d_c%d" % (li, i), [128, JR, D], BF16)) for i in range(2)]
        bcvt = [Buf(), Buf()]
        n = 0
        for src_t, dst_t in ((utab_f, utab), (vtab_f, vtab)):
            for rt_ in range(16384 // (128 * JR)):
                k = n % 2
                r0 = rt_ * 128 * JR
                P.dma("sp", lambda e, k=k, r0=r0, src_t=src_t: e.dma_start(
                    out=stg[k][:], in_=src_t.ap()[r0:r0 + 128 * JR, :].rearrange("(p j) c -> p j c", j=JR)),
                    writes=[bstg[k]])
                eng = "act" if n % 2 == 0 else "dve"
                if eng == "act":
                    P.op("act", lambda e, k=k: e.activation(out=cvt[k][:], in_=stg[k][:], func=AF.Copy),
                         reads=[bstg[k]], writes=[bcvt[k]])
                else:
                    P.op("dve", lambda e, k=k: e.tensor_copy(out=cvt[k][:], in_=stg[k][:]),
                         reads=[bstg[k]], writes=[bcvt[k]])
                P.dma("sp", lambda e, k=k, r0=r0, dst_t=dst_t: e.dma_start(
                    out=dst_t.ap()[r0:r0 + 128 * JR, :].rearrange("(p j) c -> p j c", j=JR), in_=cvt[k][:]),
                    reads=[bcvt[k]], writes=[btab])
                n += 1
        P.barrier()
    with ExitStack() as es:
        sbt = lambda n, s, d: es.enter_context(nc.sbuf_tensor("pr%d_" % li + n, list(s), d))
        h = sbt("h", [128, KC, TG], F32); bh = Buf()
        u = sbt("u", [128, KC, TG], F32); bu = Buf()
        qT = sbt("qT", [128, KC, TG], F32); bq = Buf()
        skT = sbt("skT", [128, 16, 128], F32); bsk = Buf()
        Ssb = sbt("S", [128, 16, 128], F32); bS = Buf()
        S2 = sbt("S2", [128, 256], F32); bS2 = Buf()
        m8 = sbt("m8", [128, 16, 16], F32); bm8 = Buf()
        i8 = sbt("i8", [128, 16, 16], U32); bi8 = Buf()
        i8f = sbt("i8f", [128, 16, 16], F32); bi8f = Buf()
        cand = sbt("cand", [128, 8, 256], F32); bcand = Buf()
        bs = sbt("bs", [128, 8, 16], F32); bbs = Buf()
        bp = sbt("bp", [128, 8, 16], U32); bbp = Buf()
        posf = sbt("posf", [128, 128], F32); bposf = Buf()
        af = sbt("af", [128, 128], F32); baf = Buf()
        bf = sbt("bf", [128, 128], F32); bbf = Buf()
        io16 = sbt("io16", [128, 128, 16], F32); bio = Buf()
        oh = sbt("oh", [128, 128, 16], F32); boh = Buf()
        isel = sbt("isel", [128, 128], F32); bisel = Buf()
        jsel = sbt("jsel", [128, 128], F32); bjsel = Buf()
        eidx = sbt("eidx", [128, 128], I32); beidx = Buf()
        gate = sbt("gate", [128, 8, 16], F32); bgate = Buf()
        gsum = sbt("gsum", [128, 8], F32); bgsum = Buf()
        av = sbt("a", [128, 128], F32); bav = Buf()
        bavh = [Buf() for _ in range(8)]; bt1h = [Buf() for _ in range(8)]; bwvh = [Buf() for _ in range(8)]
        t1 = sbt("t1", [128, 128], F32); bt1 = Buf()
        wv = sbt("w", [128, 128], F32); bwv = Buf()
        utok = sbt("utok", [128, D], F32); butok = Buf()
        acc = sbt("acc", [128, D], F32); bacc = Buf()
        scr = sbt("scr", [128, D], BF16); bscr = Buf()
        gb = [sbt("g%d" % i, [128, D], BF16) for i in range(R)]; bgb = [Buf() for _ in range(R)]
        A, bA = self.make_A(es, li, 1)
        identb = sbt("identb", [128, 128], BF16); bidb = Buf()
        P.op("dve", lambda e: e.tensor_copy(out=identb[:], in_=self.ident[:]), reads=[self.b_const], writes=[bidb])
        dg = [sbt("dg%d" % i, [128, 128], BF16) for i in range(4)]; bdg = [Buf() for _ in range(4)]
        P.dma("sp", lambda e: e.dma_start(out=skT[:], in_=skd.ap()), writes=[bsk])
        P.op("pool", lambda e: e.iota(io16[:], pattern=[[0, 128], [1, 16]], base=0, channel_multiplier=0,
                                      allow_small_or_imprecise_dtypes=True), writes=[bio])
        nm = self.norm_mod(es, h, bh, TG, A, bA, li, 1, u, bu, "prn%d" % li)

        def epiq(i, mc, bank, bb):
            P.op("act", lambda e: e.activation(out=qT[:, mc, :], in_=bank[:, :TG], func=AF.Copy),
                 reads=[bb], writes=[bq])
        linq = self.linear_fm(es, wq, KC, list(range(KC)), u, bu, TG, epiq, "prq%d" % li, banks=(0, 1), f32=True)

        def top16(src_ap, n, vals_ap, idx_ap, rd, bvals, bidx):
            P.op("dve", lambda e: e.max(out=vals_ap[:, 0:8], in_=src_ap), reads=rd, writes=[bvals])
            P.op("dve", lambda e: e.max_index(out=idx_ap[:, 0:8], in_max=vals_ap[:, 0:8], in_values=src_ap),
                 reads=rd + [bvals], writes=[bidx])
            P.op("dve", lambda e: e.match_replace(out=S2[:, :n], in_to_replace=vals_ap[:, 0:8], in_values=src_ap,
                                                  imm_value=NEG), reads=rd + [bvals], writes=[bS2])
            P.op("dve", lambda e: e.max(out=vals_ap[:, 8:16], in_=S2[:, :n]), reads=[bS2], writes=[bvals])
            P.op("dve", lambda e: e.max_index(out=idx_ap[:, 8:16], in_max=vals_ap[:, 8:16], in_values=S2[:, :n]),
                 reads=[bS2, bvals], writes=[bidx])

        gcount = [0]

        GBATCH = 4

        def gather(tab, m):
            k = gcount[0] % R
            if gcount[0] % GBATCH == 0:
                P._deps("pool", [], [bgb[(k + GBATCH - 1) % R]])
            first_use = gcount[0] < R
            gcount[0] += 1
            g, bg = gb[k], bgb[k]
            P.dma("pool", lambda e: e.indirect_dma_start(
                out=g[:, :], out_offset=None, in_=tab[:, :],
                in_offset=bass.IndirectOffsetOnAxis(ap=eidx[:, m:m + 1], axis=0)),
                reads=[beidx], writes=[bg], slot=k, implied_slot_free=not first_use)
            return g, bg

        for gi in range(S // TG):
            t0 = gi * TG
            self.load_h(src, t0, TG, h, bh)
            self.dump("h", h[:], [128, KC, TG], F32, [bh])
            nm()
            self.dump("u", u[:], [128, KC, TG], F32, [bu])
            linq()
            self.dump("qT", qT[:], [128, KC, TG], F32, [bq])
            for blk in range(NBLK):
                c0 = blk * 128
                for g in range(16):
                    bank, bb = self.pb[2 + g // 4], self.pbb[2 + g // 4]
                    P.op("pe", lambda e, g=g, bank=bank: e.matmul(
                        bank[:, (g % 4) * 128:(g % 4 + 1) * 128], lhsT=qT[:, g, c0:c0 + 128], rhs=skT[:, g, :],
                        start=True, stop=True), reads=[bq, bsk], writes=[bb])
                for b4 in range(4):
                    P.op("act", lambda e, b4=b4: e.activation(
                        out=Ssb[:, b4 * 4:(b4 + 1) * 4, :], in_=self.pb[2 + b4][:, :].rearrange("p (g n) -> p g n", g=4),
                        func=AF.Copy), reads=[self.pbb[2 + b4]], writes=[bS])
                for g in range(16):
                    top16(Ssb[:, g, :], 128, m8[:, g, :], i8[:, g, :], [bS], bm8, bi8)
                P.op("dve", lambda e: e.tensor_copy(out=i8f[:], in_=i8[:]), reads=[bi8], writes=[bi8f])
                m8v = m8[:].rearrange("p (h two) k -> p h two k", two=2)
                P.op("dve", lambda e: e.tensor_tensor(
                    out=cand[:].rearrange("p h (a b) -> p h a b", a=16),
                    in0=m8v[:, :, 0, :].unsqueeze(3).to_broadcast([128, 8, 16, 16]),
                    in1=m8v[:, :, 1, :].unsqueeze(2).to_broadcast([128, 8, 16, 16]), op=ALU.add),
                    reads=[bm8], writes=[bcand])
                for hh in range(8):
                    top16(cand[:, hh, :], 256, bs[:, hh, :], bp[:, hh, :], [bcand], bbs, bbp)
                P.op("dve", lambda e: e.tensor_copy(out=posf[:], in_=bp[:].rearrange("p h k -> p (h k)")),
                     reads=[bbp], writes=[bposf])
                P.op("dve", lambda e: e.tensor_scalar(out=af[:], in0=posf[:], scalar1=0.0625, scalar2=0.53125,
                                                      op0=ALU.mult, op1=ALU.add), reads=[bposf], writes=[baf])
                P.op("dve", lambda e: e.tensor_scalar(out=af[:], in0=af[:], scalar1=8388608.0, scalar2=None,
                                                      op0=ALU.add), reads=[baf], writes=[baf])
                P.op("dve", lambda e: e.tensor_scalar(out=af[:], in0=af[:], scalar1=-8388609.0, scalar2=None,
                                                      op0=ALU.add), reads=[baf], writes=[baf])
                P.op("dve", lambda e: e.scalar_tensor_tensor(out=bf[:], in0=af[:], scalar=-16.0, in1=posf[:],
                                                             op0=ALU.mult, op1=ALU.add),
                     reads=[baf, bposf], writes=[bbf])
                i8v = i8f[:].rearrange("p (h two) k -> p h two k", two=2)
                for which, sel_idx, dst_t, bdst in ((0, af, isel, bisel), (1, bf, jsel, bjsel)):
                    bsel = baf if which == 0 else bbf
                    P.op("dve", lambda e, sel_idx=sel_idx: e.tensor_tensor(
                        out=oh[:], in0=io16[:], in1=sel_idx[:].unsqueeze(2).to_broadcast([128, 128, 16]),
                        op=ALU.is_equal), reads=[bio, bsel], writes=[boh])
                    P.op("dve", lambda e, which=which: e.tensor_tensor(
                        out=oh[:].rearrange("p (h k) a -> p h k a", h=8),
                        in0=oh[:].rearrange("p (h k) a -> p h k a", h=8),
                        in1=i8v[:, :, which, :].unsqueeze(2).to_broadcast([128, 8, 16, 16]), op=ALU.mult),
                        reads=[boh, bi8f], writes=[boh])
                    P.op("dve", lambda e, dst_t=dst_t: e.tensor_reduce(out=dst_t[:], in_=oh[:], axis=AX.X, op=ALU.add),
                         reads=[boh], writes=[bdst])
                P.op("dve", lambda e: e.scalar_tensor_tensor(out=af[:], in0=isel[:], scalar=128.0, in1=jsel[:],
                                                             op0=ALU.mult, op1=ALU.add),
                     reads=[bisel, bjsel, baf], writes=[baf])
                P.op("dve", lambda e: e.tensor_copy(out=eidx[:], in_=af[:]), reads=[baf], writes=[beidx])
                self.dump("eidx", eidx[:], [128, 128], I32, [beidx])
                self.dump("m8", m8[:], [128, 16, 16], F32, [bm8])
                self.dump("i8f", i8f[:], [128, 16, 16], F32, [bi8f])
                self.dump("bs", bs[:], [128, 8, 16], F32, [bbs])
                self.dump("posf", posf[:], [128, 128], F32, [bposf])
                self.dump("S", Ssb[:], [128, 16, 128], F32, [bS])
                self.dump("isel", isel[:], [128, 128], F32, [bisel])
                self.dump("jsel", jsel[:], [128, 128], F32, [bjsel])
                P.op("dve", lambda e: e.tensor_tensor(out=gate[:], in0=bs[:], in1=bs[:, :, 0:1].to_broadcast([128, 8, 16]),
                                                      op=ALU.subtract), reads=[bbs], writes=[bgate])
                P.op("act", lambda e: e.activation(out=gate[:], in_=gate[:], func=AF.Exp), reads=[bgate], writes=[bgate])
                P.op("dve", lambda e: e.tensor_reduce(out=gsum[:], in_=gate[:], axis=AX.X, op=ALU.add),
                     reads=[bgate], writes=[bgsum])
                P.op("dve", lambda e: e.reciprocal(out=gsum[:], in_=gsum[:]), reads=[bgsum], writes=[bgsum])
                P.op("dve", lambda e: e.tensor_tensor(out=gate[:], in0=gate[:],
                                                      in1=gsum[:].unsqueeze(2).to_broadcast([128, 8, 16]), op=ALU.mult),
                     reads=[bgate, bgsum], writes=[bgate])
                for kc in range(KC):
                    bank, bb = self.pb[2 + kc // 4], self.pbb[2 + kc // 4]
                    P.op("pe", lambda e, kc=kc, bank=bank: e.transpose(
                        out=bank[:, (kc % 4) * 128:(kc % 4 + 1) * 128], in_=u[:, kc, c0:c0 + 128], identity=self.ident[:]),
                        reads=[bu, self.b_const], writes=[bb])
                for b4 in range(4):
                    P.op("act", lambda e, b4=b4: e.activation(out=utok[:, b4 * 512:(b4 + 1) * 512],
                                                              in_=self.pb[2 + b4][:, :], func=AF.Copy),
                         reads=[self.pbb[2 + b4]], writes=[butok])
                gflat = gate[:].rearrange("p h k -> p (h k)")
                def emit_u(m):
                    k8 = m // 16
                    g, bg = gather(utab, m)
                    P.op("dve", lambda e: e.scalar_tensor_tensor(
                        out=scr[:], in0=utok[:], scalar=1.0, in1=g[:], op0=ALU.mult, op1=ALU.mult,
                        accum_out=av[:, m:m + 1]), reads=[butok, bg], writes=[bscr, bavh[k8]])

                def emit_w(k8):
                    sl = slice(16 * k8, 16 * k8 + 16)
                    P.op("dve", lambda e: e.tensor_tensor(out=t1[:, sl], in0=av[:, sl], in1=av[:, sl], op=ALU.mult),
                         reads=[bavh[k8]], writes=[bt1h[k8]])
                    P.op("dve", lambda e: e.tensor_scalar(out=t1[:, sl], in0=t1[:, sl], scalar1=0.044715, scalar2=1.0,
                                                          op0=ALU.mult, op1=ALU.add), reads=[bt1h[k8]], writes=[bt1h[k8]])
                    P.op("dve", lambda e: e.tensor_tensor(out=t1[:, sl], in0=t1[:, sl], in1=av[:, sl], op=ALU.mult),
                         reads=[bt1h[k8], bavh[k8]], writes=[bt1h[k8]])
                    P.op("act", lambda e: e.activation(out=t1[:, sl], in_=t1[:, sl], func=AF.Tanh, scale=0.7978845608028654),
                         reads=[bt1h[k8]], writes=[bt1h[k8]])
                    P.op("dve", lambda e: e.scalar_tensor_tensor(out=t1[:, sl], in0=t1[:, sl], scalar=1.0, in1=av[:, sl],
                                                                 op0=ALU.add, op1=ALU.mult),
                         reads=[bt1h[k8], bavh[k8]], writes=[bt1h[k8]])
                    P.op("dve", lambda e: e.scalar_tensor_tensor(out=wv[:, sl], in0=t1[:, sl], scalar=0.5, in1=gflat[:, sl],
                                                                 op0=ALU.mult, op1=ALU.mult),
                         reads=[bt1h[k8], bgate], writes=[bwvh[k8]])

                def emit_v(m):
                    k8 = m // 16
                    g, bg = gather(vtab, m)
                    dgt, bdgt = dg[m % 4], bdg[m % 4]
                    P.op("act", lambda e: e.activation(out=dgt[:], in_=identb[:], func=AF.Copy, scale=wv[:, m:m + 1]),
                         reads=[bwvh[k8], bidb], writes=[bdgt])
                    for c4 in range(4):
                        P.op("pe", lambda e, c4=c4: e.matmul(
                            self.pb[2 + c4][:, :], lhsT=dgt[:], rhs=g[:, c4 * 512:(c4 + 1) * 512],
                            start=(m == 0), stop=(m == 127)), reads=[bdgt, bg], writes=[self.pbb[2 + c4]])

                for k8 in range(9):
                    for i16 in range(16):
                        if k8 < 8:
                            emit_u(16 * k8 + i16)
                        if k8 >= 1:
                            emit_v(16 * (k8 - 1) + i16)
                    if k8 < 8:
                        emit_w(k8)
                for c4 in range(4):
                    P.op("act", lambda e, c4=c4: e.activation(out=acc[:, c4 * 512:(c4 + 1) * 512], in_=self.pb[2 + c4][:, :],
                                                              func=AF.Copy), reads=[self.pbb[2 + c4]], writes=[bacc])
                self.dump("acc", acc[:], [128, D], F32, [bacc])
                for kc in range(KC):
                    bank, bb = self.pb[2 + kc // 4], self.pbb[2 + kc // 4]
                    P.op("pe", lambda e, kc=kc, bank=bank: e.transpose(
                        out=bank[:, (kc % 4) * 128:(kc % 4 + 1) * 128], in_=acc[:, kc * 128:(kc + 1) * 128],
                        identity=self.ident[:]), reads=[bacc, self.b_const], writes=[bb])
                for kc in range(KC):
                    bank, bb = self.pb[2 + kc // 4], self.pbb[2 + kc // 4]
                    P.op("dve", lambda e, kc=kc, bank=bank: e.scalar_tensor_tensor(
                        out=h[:, kc, c0:c0 + 128], in0=bank[:, (kc % 4) * 128:(kc % 4 + 1) * 128],
                        scalar=self.mod(li, 5, kc), in1=h[:, kc, c0:c0 + 128], op0=ALU.mult, op1=ALU.add),
                        reads=[bb, self.b_mods, bh], writes=[bh])
            self.store_h(dst, t0, TG, h, bh)
        P.barrier()


Model._stage_peer = _peer_stage


class WS:
    def __init__(self, model, es, KCin, tag, nbuf=2):
        nc = model.nc
        self.m = model
        self.n = nbuf
        self.KCin = KCin
        self.wst = [es.enter_context(nc.sbuf_tensor("%s_ws%d" % (tag, i), [128, KCin, 128], F32)) for i in range(nbuf)]
        self.bws = [Buf() for _ in range(nbuf)]
        self.wbf = [es.enter_context(nc.sbuf_tensor("%s_wb%d" % (tag, i), [128, KCin, 128], BF16)) for i in range(nbuf)]
        self.bwb = [Buf() for _ in range(nbuf)]
        self.i = 0

    def get(self, wd, mc, f32=False, kcs=None):
        P = self.m.P
        k = self.i % self.n
        self.i += 1
        w, bw = self.wst[k], self.bws[k]
        kk = self.KCin if kcs is None else kcs
        P.dma("sp", lambda e: e.dma_start(out=w[:, :kk, :], in_=wd.ap()[mc]), writes=[bw])
        if f32:
            return w, bw
        wm, bm = self.wbf[k], self.bwb[k]
        if self.i % 4 == 0:
            P.op("act", lambda e: e.activation(out=wm[:, :kk, :], in_=w[:, :kk, :], func=AF.Copy), reads=[bw], writes=[bm])
        else:
            P.op("pool", lambda e: e.tensor_copy(out=wm[:, :kk, :], in_=w[:, :kk, :]), reads=[bw], writes=[bm])
        return wm, bm


def _lin(self, ws, wd, mc, kcs, rhs_of, rd, out_ap, bout, f32=False):
    P = self.P
    wm, bm = ws.get(wd, mc, f32=f32, kcs=kcs)
    for kc in range(kcs):
        P.op("pe", lambda e, kc=kc: e.matmul(out_ap, lhsT=wm[:, kc, :], rhs=rhs_of(kc),
                                             start=(kc == 0), stop=(kc == kcs - 1)),
             reads=[bm] + list(rd), writes=[bout])


Model.lin = _lin


def _hgrn_stage(self, li, src, dst):
    P = self.P
    nc = self.nc
    S = self.S
    TG = 256
    NCH = TG // 64
    H = 16
    wqfg = self.din["hg_w_qfg"]
    wi = self.din["hg_w_i"]
    wo = self.din["hg_w_out"]
    with ExitStack() as es:
        sbt = lambda n, s, d: es.enter_context(nc.sbuf_tensor("hg_" + n, list(s), d))
        h = sbt("h", [128, KC, TG], F32); bh = Buf()
        u = sbt("u", [128, KC, TG], BF16); bu = Buf()
        vtok = sbt("vtok", [64, NCH, D], BF16); bvt = [[Buf() for _ in range(H)] for _ in range(NCH)]
        qt = sbt("qt", [128, H, TG], BF16); bqt = [Buf() for _ in range(H)]
        ktb = sbt("ktb", [128, H, TG], BF16); bktb = [Buf() for _ in range(H)]
        ktok = sbt("ktok", [64, NCH, H, 128], BF16); bktok = [[Buf() for _ in range(H)] for _ in range(NCH)]
        gs = sbt("gs", [128, H, TG], BF16); bgs = [Buf() for _ in range(H)]
        oT = sbt("oT", [128, H, TG], F32); boT = [Buf() for _ in range(H)]
        ob = sbt("ob", [128, H, TG], BF16); bob = Buf()
        St = sbt("S", [128, H, 128], F32); bS = [Buf() for _ in range(H)]
        Sb = sbt("Sb", [128, H, 128], BF16); bSb = [Buf() for _ in range(H)]
        elast = sbt("elast", [128, H, NCH], F32); bel = [Buf() for _ in range(H)]
        lb = sbt("lb", [128, H], F32); blb = Buf()
        omlb = sbt("omlb", [128, H], F32)
        ex = sbt("ex", [128, 4, H], F32)
        tf = [sbt("tf%d" % i, [128, TG], F32) for i in range(2)]; btf = [Buf(), Buf()]
        logf = [sbt("logf%d" % i, [128, TG], F32) for i in range(2)]; blogf = [Buf(), Buf()]
        kk = [sbt("kk%d" % i, [128, TG], F32) for i in range(2)]; bkk = [Buf(), Buf()]
        bb_ = [sbt("b%d" % i, [128, TG], F32) for i in range(2)]; bbb = [Buf(), Buf()]
        eb = [sbt("eb%d" % i, [128, TG], F32) for i in range(2)]; beb = [Buf(), Buf()]
        ktf = [sbt("ktf%d" % i, [128, TG], F32) for i in range(2)]; bktf = [Buf(), Buf()]
        att = [sbt("att%d" % i, [64, 64], BF16) for i in range(4)]; batt = [Buf() for _ in range(4)]
        tri = sbt("tri", [64, 64], F32); btri = Buf()
        rs = sbt("rs", [128, TG], F32); brs = Buf()
        A, bA = self.make_A(es, li, 0)
        ws = WS(self, es, KC, "hgw")
        nm = self.norm_mod(es, h, bh, TG, A, bA, li, 0, u, bu, "hg")
        P.dma("sp", lambda e: e.dma_start(out=tri[:], in_=self.din["tri"].ap()[0:64, 0:64]), writes=[btri])
        lg = self.v("hg_lb", 0, 64).rearrange("p (l k) -> p l k", l=4)
        P.op("act", lambda e: e.activation(out=ex[:], in_=lg, func=AF.Exp), reads=[self.b_vec], writes=[blb])
        P.op("dve", lambda e: e.tensor_tensor(out=omlb[:], in0=ex[:, 0, :], in1=ex[:, 1, :], op=ALU.add), reads=[blb], writes=[blb])
        P.op("dve", lambda e: e.tensor_tensor(out=omlb[:], in0=omlb[:], in1=ex[:, 2, :], op=ALU.add), reads=[blb], writes=[blb])
        P.op("dve", lambda e: e.tensor_tensor(out=omlb[:], in0=omlb[:], in1=ex[:, 3, :], op=ALU.add), reads=[blb], writes=[blb])
        P.op("dve", lambda e: e.reciprocal(out=omlb[:], in_=omlb[:]), reads=[blb], writes=[blb])
        P.op("dve", lambda e: e.tensor_copy(out=lb[:], in_=ex[:, 1, :]), reads=[blb], writes=[blb])
        for l in range(2, li + 1):
            P.op("dve", lambda e, l=l: e.tensor_tensor(out=lb[:], in0=lb[:], in1=ex[:, l, :], op=ALU.add), reads=[blb], writes=[blb])
        P.op("dve", lambda e: e.tensor_tensor(out=lb[:], in0=lb[:], in1=omlb[:], op=ALU.mult), reads=[blb], writes=[blb])
        P.op("dve", lambda e: e.tensor_scalar(out=omlb[:], in0=lb[:], scalar1=-1.0, scalar2=1.0, op0=ALU.mult, op1=ALU.add),
             reads=[blb], writes=[blb])
        P.op("pool", lambda e: e.memset(St[:], 0.0), writes=bS)
        P.op("pool", lambda e: e.memset(Sb[:], 0.0), writes=bSb)
        SCALE = 128.0 ** -0.5
        lrot = [0]

        for gi in range(S // TG):
            t0 = gi * TG
            self.load_h(src, t0, TG, h, bh)
            nm()
            for hh in range(H):
                k2 = hh % 2
                bk = lrot[0] % 2; lrot[0] += 1
                bank, bbk = self.pb[bk], self.pbb[bk]
                self.lin(ws, wqfg, 16 + hh, KC, lambda kc: u[:, kc, :], [bu], bank[:, :TG], bbk)
                t, bt = tf[k2], btf[k2]
                P.op("act", lambda e, t=t, bank=bank: e.activation(out=t[:], in_=bank[:, :TG], func=AF.Sigmoid),
                     reads=[bbk], writes=[bt])
                P.op("dve", lambda e, t=t, hh=hh: e.tensor_scalar(out=t[:], in0=t[:], scalar1=omlb[:, hh:hh + 1],
                                                                   scalar2=lb[:, hh:hh + 1], op0=ALU.mult, op1=ALU.add),
                     reads=[bt, blb], writes=[bt])
                lf, blf = logf[k2], blogf[k2]
                P.op("act", lambda e, t=t, lf=lf: e.activation(out=lf[:], in_=t[:], func=AF.Ln), reads=[bt], writes=[blf])
                kx, bkx = kk[k2], bkk[k2]
                P.op("dve", lambda e, t=t, kx=kx: e.tensor_scalar(out=kx[:], in0=t[:], scalar1=-1.0, scalar2=1.0,
                                                                  op0=ALU.mult, op1=ALU.add), reads=[bt], writes=[bkx])
                bc, bbc = bb_[k2], bbb[k2]
                for c in range(NCH):
                    P.op("dve", lambda e, c=c, bc=bc, lf=lf: e.tensor_tensor_scan(
                        out=bc[:, c * 64:(c + 1) * 64], data0=self.ones[:, 0:64], data1=lf[:, c * 64:(c + 1) * 64],
                        initial=0.0, op0=ALU.mult, op1=ALU.add), reads=[blf, self.b_const], writes=[bbc])
                e1, be1 = eb[k2], beb[k2]
                P.op("act", lambda e, e1=e1, bc=bc: e.activation(out=e1[:], in_=bc[:], func=AF.Exp), reads=[bbc], writes=[be1])
                P.op("dve", lambda e, e1=e1, hh=hh: e.tensor_copy(
                    out=elast[:, hh, :], in_=e1[:].rearrange("p (c t) -> p c t", t=64)[:, :, 63]),
                    reads=[be1], writes=[bel[hh]])
                kf, bkf = ktf[k2], bktf[k2]
                P.op("act", lambda e, kf=kf, bc=bc: e.activation(out=kf[:], in_=bc[:], func=AF.Exp, scale=-1.0),
                     reads=[bbc], writes=[bkf])
                P.op("dve", lambda e, kf=kf, kx=kx: e.tensor_tensor(out=kf[:], in0=kf[:], in1=kx[:], op=ALU.mult),
                     reads=[bkf, bkx], writes=[bkf])
                P.op("act", lambda e, kf=kf, hh=hh: e.activation(out=ktb[:, hh, :], in_=kf[:], func=AF.Copy),
                     reads=[bkf], writes=[bktb[hh]])
                for c in range(NCH):
                    sl = (hh * NCH + c) % 4
                    P.op("pe", lambda e, c=c, kf=kf, sl=sl: e.transpose(
                        out=self.pb[6][0:64, sl * 128:(sl + 1) * 128], in_=kf[:, c * 64:(c + 1) * 64], identity=self.ident[:]),
                        reads=[bkf, self.b_const], writes=[self.pbb[6]])
                    P.op("act", lambda e, c=c, hh=hh, sl=sl: e.activation(
                        out=ktok[0:64, c, hh, :], in_=self.pb[6][0:64, sl * 128:(sl + 1) * 128], func=AF.Copy),
                        reads=[self.pbb[6]], writes=[bktok[c][hh]])
                bk = lrot[0] % 2; lrot[0] += 1
                bank, bbk = self.pb[bk], self.pbb[bk]
                self.lin(ws, wqfg, hh, KC, lambda kc: u[:, kc, :], [bu], bank[:, :TG], bbk)
                P.op("dve", lambda e, hh=hh, bank=bank, e1=e1: e.scalar_tensor_tensor(
                    out=qt[:, hh, :], in0=bank[:, :TG], scalar=SCALE, in1=e1[:], op0=ALU.mult, op1=ALU.mult),
                    reads=[bbk, be1], writes=[bqt[hh]])
                bk = lrot[0] % 2; lrot[0] += 1
                bank, bbk = self.pb[bk], self.pbb[bk]
                self.lin(ws, wqfg, 32 + hh, KC, lambda kc: u[:, kc, :], [bu], bank[:, :TG], bbk)
                P.op("act", lambda e, hh=hh, bank=bank: e.activation(out=gs[:, hh, :], in_=bank[:, :TG], func=AF.Silu),
                     reads=[bbk], writes=[bgs[hh]])
                wm, bm = ws.get(wi, hh)
                for c in range(NCH):
                    bk2 = 2 + (hh * NCH + c) % 2
                    bank, bbk = self.pb[bk2], self.pbb[bk2]
                    for kc in range(KC):
                        P.op("pe", lambda e, kc=kc, c=c, bank=bank: e.matmul(
                            bank[0:64, 0:128], lhsT=u[:, kc, c * 64:(c + 1) * 64], rhs=wm[:, kc, :],
                            start=(kc == 0), stop=(kc == KC - 1)), reads=[bu, bm], writes=[bbk])
                    P.op("act", lambda e, c=c, hh=hh, bank=bank: e.activation(
                        out=vtok[0:64, c, hh * 128:(hh + 1) * 128], in_=bank[0:64, 0:128], func=AF.Copy),
                        reads=[bbk], writes=[bvt[c][hh]])
            import os
            HG_STOP = int(os.environ.get("HG_STOP", "9"))
            for c in range(NCH if HG_STOP >= 2 else 0):
                cs = slice(c * 64, (c + 1) * 64)
                for hh in range(H):
                    sa = so = ss = 0
                    pA, pO, pS = self.pb[hh % 2], self.pb[2 + hh % 2], self.pb[4 + hh % 2]
                    bpA = [self.pbb[hh % 2]]; bpO = [self.pbb[2 + hh % 2]]; bpS = [self.pbb[4 + hh % 2]]
                    at, bat = att[hh % 4], batt[hh % 4]
                    P.op("pe", lambda e, hh=hh, sa=sa, pA=pA: e.matmul(
                        pA[0:64, sa * 64:(sa + 1) * 64], lhsT=ktb[:, hh, cs], rhs=qt[:, hh, cs], start=True, stop=True),
                        reads=[bktb[hh], bqt[hh]], writes=[bpA[sa]])
                    P.op("dve", lambda e, at=at, sa=sa, pA=pA: e.tensor_tensor(
                        out=at[:], in0=pA[0:64, sa * 64:(sa + 1) * 64], in1=tri[:], op=ALU.mult),
                        reads=[bpA[sa], btri], writes=[bat])
                    P.op("pe", lambda e, hh=hh, so=so, at=at, pO=pO: e.matmul(
                        pO[:, so * 64:(so + 1) * 64], lhsT=vtok[0:64, c, hh * 128:(hh + 1) * 128], rhs=at[:],
                        start=True, stop=False), reads=[bvt[c][hh], bat], writes=[bpO[so]])
                    P.op("pe", lambda e, hh=hh, so=so, pO=pO: e.matmul(
                        pO[:, so * 64:(so + 1) * 64], lhsT=Sb[:, hh, :], rhs=qt[:, hh, cs],
                        start=False, stop=True), reads=[bSb[hh], bqt[hh]], writes=[bpO[so]])
                    P.op("act", lambda e, hh=hh, so=so, pO=pO: e.activation(
                        out=oT[:, hh, cs], in_=pO[:, so * 64:(so + 1) * 64], func=AF.Copy),
                        reads=[bpO[so]], writes=[boT[hh]])
                    P.op("pe", lambda e, hh=hh, ss=ss, pS=pS: e.matmul(
                        pS[:, ss * 128:(ss + 1) * 128], lhsT=ktok[0:64, c, hh, :], rhs=vtok[0:64, c, hh * 128:(hh + 1) * 128],
                        start=True, stop=True), reads=[bktok[c][hh], bvt[c][hh]], writes=[bpS[ss]])
                    P.op("dve", lambda e, hh=hh, ss=ss, pS=pS: e.tensor_tensor(
                        out=St[:, hh, :], in0=pS[:, ss * 128:(ss + 1) * 128], in1=St[:, hh, :], op=ALU.add),
                        reads=[bpS[ss], bS[hh]], writes=[bS[hh]])
                    P.op("dve", lambda e, hh=hh: e.tensor_scalar(
                        out=St[:, hh, :], in0=St[:, hh, :], scalar1=elast[:, hh, c:c + 1], scalar2=None, op0=ALU.mult),
                        reads=[bS[hh], bel[hh]], writes=[bS[hh]])
                    P.op("act", lambda e, hh=hh: e.activation(out=Sb[:, hh, :], in_=St[:, hh, :], func=AF.Copy),
                         reads=[bS[hh]], writes=[bSb[hh]])
            for hh in range(H if HG_STOP >= 3 else 0):
                t, bt = tf[hh % 2], btf[hh % 2]
                P.op("act", lambda e, t=t, hh=hh: e.activation(out=t[:], in_=oT[:, hh, :], func=AF.Square),
                     reads=[boT[hh]], writes=[bt])
                bank, bbk = self.pb[7], self.pbb[7]
                P.op("pe", lambda e, t=t: e.matmul(bank[:, :TG], lhsT=self.ones[:], rhs=t[:], start=True, stop=True),
                     reads=[bt, self.b_const], writes=[bbk])
                P.op("act", lambda e: e.activation(out=rs[:], in_=bank[:, :TG], func=AF.Sqrt, bias=self.epsb[:],
                                                   scale=1.0 / 128.0), reads=[bbk, self.b_const], writes=[brs])
                P.op("dve", lambda e: e.reciprocal(out=rs[:], in_=rs[:]), reads=[brs], writes=[brs])
                P.op("dve", lambda e, t=t, hh=hh: e.tensor_tensor(out=t[:], in0=oT[:, hh, :], in1=rs[:], op=ALU.mult),
                     reads=[boT[hh], brs, bt], writes=[bt])
                P.op("dve", lambda e, t=t, hh=hh: e.scalar_tensor_tensor(
                    out=ob[:, hh, :], in0=t[:], scalar=self.v("hg_norm_g", 0), in1=gs[:, hh, :], op0=ALU.mult, op1=ALU.mult),
                    reads=[bt, self.b_vec, bgs[hh]], writes=[bob])
            for mc in range(KC if HG_STOP >= 4 else 0):
                bk = lrot[0] % 2; lrot[0] += 1
                bank, bbk = self.pb[bk], self.pbb[bk]
                self.lin(ws, wo, mc, KC, lambda kc: ob[:, kc, :], [bob], bank[:, :TG], bbk)
                P.op("dve", lambda e, mc=mc, bank=bank: e.scalar_tensor_tensor(
                    out=h[:, mc, :], in0=bank[:, :TG], scalar=self.mod(li, 2, mc), in1=h[:, mc, :],
                    op0=ALU.mult, op1=ALU.add), reads=[bbk, self.b_mods, bh], writes=[bh])
            self.store_h(dst, t0, TG, h, bh)
        P.barrier()


Model._stage_hgrn = _hgrn_stage


def _declare_mla(self):
    S = self.S
    nc = self.nc
    self.inp("mla_w_in", [9, 128, KC, 128])
    self.inp("mla_w_uq", [32, 128, 4, 128])
    self.inp("mla_w_ukv", [32, 128, 4, 128])
    self.inp("mla_w_o", [16, 128, KC, 128])
    self.inp("pswap", [64, 64])
    self.inp("dmask", [128, 128])
    self.inp("rsgn", [128, 1])
    self.QN = nc.dram_tensor("mla_QN", [16, 128, S], BF16)
    self.QR = nc.dram_tensor("mla_QR", [16, 64, S], BF16)
    self.KN = nc.dram_tensor("mla_KN", [16, 128, S], BF16)
    self.KR = nc.dram_tensor("mla_KR", [64, S], BF16)
    self.VT = nc.dram_tensor("mla_VT", [16, S, 128], BF16)
    self.OT = nc.dram_tensor("mla_OT", [128, 16, S], BF16)


Model.declare_mla = _declare_mla

TWO_PI_HI = 6.28125
TWO_PI_LO = 0.0019353071795864769


def _mla_stage(self, li, src, dst):
    P = self.P
    nc = self.nc
    S = self.S
    TG = 512
    H = 16
    NG = S // TG
    w_in, w_uq, w_ukv, w_o = (self.din[k] for k in ("mla_w_in", "mla_w_uq", "mla_w_ukv", "mla_w_o"))
    bscr = Buf()
    SCALE = 192.0 ** -0.5
    with ExitStack() as es:
        sbt = lambda n, s, d: es.enter_context(nc.sbuf_tensor("m1_" + n, list(s), d))
        h = sbt("h", [128, KC, TG], F32); bh = Buf()
        u = sbt("u", [128, KC, TG], BF16); bu = Buf()
        cq = sbt("cq", [128, 4, TG], F32); bcq = Buf()
        ckv = sbt("ckv", [128, 4, TG], F32); bckv = Buf()
        cqn = sbt("cqn", [128, 4, TG], BF16); bcqn = Buf()
        ckvn = sbt("ckvn", [128, 4, TG], BF16); bckvn = Buf()
        kr = sbt("kr", [64, TG], F32); bkr = Buf()
        posi = sbt("posi", [64, TG], I32); bpos = Buf()
        ang = sbt("ang", [64, TG], F32); bang = Buf()
        nn = sbt("nn", [64, TG], F32); bnn = Buf()
        cosT = sbt("cos", [64, TG], F32); bcos = Buf()
        sinT = sbt("sin", [64, TG], F32); bsin = Buf()
        psw = sbt("psw", [64, 64], F32); bpsw = Buf()
        sgn = sbt("sgn", [128, 1], F32)
        rt = [sbt("rt%d" % i, [64, TG], F32) for i in range(2)]; brt = [Buf(), Buf()]
        qrf = [sbt("qrf%d" % i, [64, TG], F32) for i in range(2)]; bqrf = [Buf(), Buf()]
        ob16 = [sbt("ob%d" % i, [128, TG], BF16) for i in range(4)]; bob16 = [Buf() for _ in range(4)]
        vt = [sbt("vt%d" % i, [128, 4, 128], BF16) for i in range(2)]; bvt = [Buf(), Buf()]
        sq = [sbt("sq%d" % i, [128, TG], F32) for i in range(2)]; bsq = [Buf(), Buf()]
        rr = sbt("rr", [128, TG], F32); brr = Buf()
        A, bA = self.make_A(es, li, 0)
        ws = WS(self, es, KC, "m1w")
        nm = self.norm_mod(es, h, bh, TG, A, bA, li, 0, u, bu, "m1n")
        P.dma("sp", lambda e: e.dma_start(out=psw[:], in_=self.din["pswap"].ap()), writes=[bpsw])
        P.dma("sp", lambda e: e.dma_start(out=sgn[:], in_=self.din["rsgn"].ap()), writes=[bpsw])
        rot = [0]
        orot = [0]

        def nbank():
            b = rot[0] % 4
            rot[0] += 1
            return self.pb[b], self.pbb[b]

        def nob():
            k = orot[0] % 4
            orot[0] += 1
            return ob16[k], bob16[k]

        def sincos(dst_t, bdst, phase, sign_scale):
            P.op("dve", lambda e: e.tensor_scalar(out=nn[:], in0=ang[:], scalar1=phase, scalar2=1.0 / (2 * math.pi),
                                                  op0=ALU.add, op1=ALU.mult), reads=[bang], writes=[bnn])
            P.op("dve", lambda e: e.tensor_scalar(out=nn[:], in0=nn[:], scalar1=8388608.0, scalar2=None, op0=ALU.add),
                 reads=[bnn], writes=[bnn])
            P.op("dve", lambda e: e.tensor_scalar(out=nn[:], in0=nn[:], scalar1=-8388608.0, scalar2=None, op0=ALU.add),
                 reads=[bnn], writes=[bnn])
            P.op("dve", lambda e: e.scalar_tensor_tensor(out=dst_t[:], in0=nn[:], scalar=-TWO_PI_HI, in1=ang[:],
                                                         op0=ALU.mult, op1=ALU.add), reads=[bnn, bang], writes=[bdst])
            P.op("dve", lambda e: e.scalar_tensor_tensor(out=dst_t[:], in0=nn[:], scalar=-TWO_PI_LO, in1=dst_t[:],
                                                         op0=ALU.mult, op1=ALU.add), reads=[bnn, bdst], writes=[bdst])
            P.op("dve", lambda e: e.tensor_scalar(out=dst_t[:], in0=dst_t[:], scalar1=phase, scalar2=math.pi,
                                                  op0=ALU.add, op1=ALU.min), reads=[bdst], writes=[bdst])
            P.op("dve", lambda e: e.tensor_scalar(out=dst_t[:], in0=dst_t[:], scalar1=-math.pi, scalar2=None,
                                                  op0=ALU.max), reads=[bdst], writes=[bdst])
            if sign_scale:
                P.op("act", lambda e: e.activation(out=dst_t[:], in_=dst_t[:], func=AF.Sin, scale=sgn[0:64, :]),
                     reads=[bdst, bpsw], writes=[bdst])
            else:
                P.op("act", lambda e: e.activation(out=dst_t[:], in_=dst_t[:], func=AF.Sin), reads=[bdst], writes=[bdst])

        def rope(x, bx, out_t, bout):
            bank, bb = nbank()
            P.op("pe", lambda e: e.matmul(bank[0:64, :TG], lhsT=psw[:], rhs=x[:], start=True, stop=True),
                 reads=[bpsw, bx], writes=[bb])
            t, bt = rt[rot[0] % 2], brt[rot[0] % 2]
            P.op("dve", lambda e: e.tensor_tensor(out=t[:], in0=bank[0:64, :TG], in1=sinT[:], op=ALU.mult),
                 reads=[bb, bsin], writes=[bt])
            P.op("pool", lambda e: e.tensor_tensor(out=x[:], in0=x[:], in1=cosT[:], op=ALU.mult),
                 reads=[bx, bcos], writes=[bx])
            P.op("dve", lambda e: e.tensor_tensor(out=out_t, in0=t[:], in1=x[:], op=ALU.add),
                 reads=[bt, bx], writes=[bout])

        for gi in range(NG):
            t0 = gi * TG
            self.load_h(src, t0, TG, h, bh)
            nm()
            P.dma("sp", lambda e: e.dma_start(out=posi[:], in_=self.din["pos"].ap()[0:1, t0:t0 + TG].partition_broadcast(64)),
                  writes=[bpos])
            P.op("dve", lambda e: e.tensor_copy(out=ang[:], in_=posi[:]), reads=[bpos], writes=[bang])
            P.op("dve", lambda e: e.tensor_scalar(out=ang[:], in0=ang[:], scalar1=self.vec[0:64, self.VOFF["inv_freq"]:self.VOFF["inv_freq"] + 1],
                                                  scalar2=None, op0=ALU.mult), reads=[bang, self.b_vec], writes=[bang])
            sincos(sinT, bsin, 0.0, True)
            sincos(cosT, bcos, math.pi / 2, False)
            for mc in range(9):
                bank, bb = nbank()
                self.lin(ws, w_in, mc, KC, lambda kc: u[:, kc, :], [bu], bank[:, :TG], bb)
                if mc < 4:
                    P.op("act", lambda e, mc=mc, bank=bank: e.activation(out=cq[:, mc, :], in_=bank[:, :TG], func=AF.Copy),
                         reads=[bb], writes=[bcq])
                elif mc < 8:
                    P.op("act", lambda e, mc=mc, bank=bank: e.activation(out=ckv[:, mc - 4, :], in_=bank[:, :TG], func=AF.Copy),
                         reads=[bb], writes=[bckv])
                else:
                    P.op("act", lambda e, bank=bank: e.activation(out=kr[:], in_=bank[0:64, :TG], func=AF.Copy),
                         reads=[bb], writes=[bkr])
            for (cx, bcx, cn, bcn, gname) in ((cq, bcq, cqn, bcqn, "mla_qg"), (ckv, bckv, ckvn, bckvn, "mla_kvg")):
                bank, bb = self.pb[7], self.pbb[7]
                self.colsum(lambda kc, cx=cx: cx[:, kc, :], 4, TG, bank, bb, sq, bsq, extra_reads=[bcx])
                P.op("act", lambda e: e.activation(out=rr[:], in_=bank[:, :TG], func=AF.Sqrt, bias=self.epsb[:],
                                                   scale=1.0 / 512.0), reads=[bb, self.b_const], writes=[brr])
                P.op("dve", lambda e: e.reciprocal(out=rr[:], in_=rr[:]), reads=[brr], writes=[brr])
                for kc in range(4):
                    P.op("dve", lambda e, kc=kc, cx=cx, cn=cn, gname=gname: e.scalar_tensor_tensor(
                        out=cn[:, kc, :], in0=cx[:, kc, :], scalar=self.v(gname, kc), in1=rr[:], op0=ALU.mult, op1=ALU.mult),
                        reads=[bcx, brr, self.b_vec], writes=[bcn])
            o16, bo16 = nob()
            rope(kr, bkr, o16[0:64, :], bo16)
            P.dma("sp", lambda e, o16=o16: e.dma_start(out=self.KR.ap()[:, t0:t0 + TG], in_=o16[0:64, :]),
                  reads=[bo16], writes=[bscr])
            for hh in range(H):
                bank, bb = nbank()
                self.lin(ws, w_uq, 2 * hh, 4, lambda kc: cqn[:, kc, :], [bcqn], bank[:, :TG], bb)
                o16, bo16 = nob()
                P.op("act", lambda e, o16=o16, bank=bank: e.activation(out=o16[:], in_=bank[:, :TG], func=AF.Copy),
                     reads=[bb], writes=[bo16])
                P.dma("sp", lambda e, o16=o16, hh=hh: e.dma_start(out=self.QN.ap()[hh, :, t0:t0 + TG], in_=o16[:]),
                      reads=[bo16], writes=[bscr])
                bank, bb = nbank()
                self.lin(ws, w_uq, 2 * hh + 1, 4, lambda kc: cqn[:, kc, :], [bcqn], bank[:, :TG], bb)
                qf, bqf = qrf[hh % 2], bqrf[hh % 2]
                P.op("act", lambda e, qf=qf, bank=bank: e.activation(out=qf[:], in_=bank[0:64, :TG], func=AF.Copy),
                     reads=[bb], writes=[bqf])
                o16, bo16 = nob()
                rope(qf, bqf, o16[0:64, :], bo16)
                P.dma("sp", lambda e, o16=o16, hh=hh: e.dma_start(out=self.QR.ap()[hh, :, t0:t0 + TG], in_=o16[0:64, :]),
                      reads=[bo16], writes=[bscr])
                bank, bb = nbank()
                self.lin(ws, w_ukv, 2 * hh, 4, lambda kc: ckvn[:, kc, :], [bckvn], bank[:, :TG], bb)
                o16, bo16 = nob()
                P.op("act", lambda e, o16=o16, bank=bank: e.activation(out=o16[:], in_=bank[:, :TG], func=AF.Copy),
                     reads=[bb], writes=[bo16])
                P.dma("sp", lambda e, o16=o16, hh=hh: e.dma_start(out=self.KN.ap()[hh, :, t0:t0 + TG], in_=o16[:]),
                      reads=[bo16], writes=[bscr])
                wm, bm = ws.get(w_ukv, 2 * hh + 1, kcs=4)
                v_, bv_ = vt[hh % 2], bvt[hh % 2]
                bank, bb = nbank()
                for blk in range(4):
                    for kc in range(4):
                        P.op("pe", lambda e, kc=kc, blk=blk, bank=bank: e.matmul(
                            bank[:, blk * 128:(blk + 1) * 128], lhsT=ckvn[:, kc, blk * 128:(blk + 1) * 128], rhs=wm[:, kc, :],
                            start=(kc == 0), stop=(kc == 3)), reads=[bckvn, bm], writes=[bb])
                P.op("act", lambda e, v_=v_, bank=bank: e.activation(
                    out=v_[:], in_=bank[:, :].rearrange("p (b c) -> p b c", b=4), func=AF.Copy), reads=[bb], writes=[bv_])
                P.dma("sp", lambda e, v_=v_, hh=hh: e.dma_start(
                    out=self.VT.ap()[hh, t0:t0 + TG, :].rearrange("(b p) c -> p b c", p=128), in_=v_[:]),
                    reads=[bv_], writes=[bscr])
        P.barrier()
    with ExitStack() as es:
        sbt = lambda n, s, d: es.enter_context(nc.sbuf_tensor("m2_" + n, list(s), d))
        NKB = S // 128
        kn = sbt("kn", [128, S], BF16); bkn = Buf()
        krs = sbt("krs", [64, S], BF16); bkrs = Buf()
        vsb = sbt("vsb", [128, NKB, 130], BF16); bvsb = Buf()
        qn = [sbt("qn%d" % i, [128, TG], BF16) for i in range(2)]; bqn = [Buf(), Buf()]
        qr = [sbt("qr%d" % i, [64, TG], BF16) for i in range(2)]; bqr = [Buf(), Buf()]
        pT = [sbt("pT%d" % i, [128, TG], BF16) for i in range(3)]; bpT = [Buf() for _ in range(3)]
        dmf = sbt("dmf", [128, 128], F32)
        dm = sbt("dm", [128, 128], BF16); bdm = Buf()
        rec = sbt("rec", [128, 4], F32); brec = Buf()
        on = [sbt("on%d" % i, [128, 128], F32) for i in range(2)]; bon = [Buf(), Buf()]
        oT = [sbt("oT%d" % i, [128, TG], BF16) for i in range(2)]; boT = [Buf(), Buf()]
        P.dma("sp", lambda e: e.dma_start(out=dmf[:], in_=self.din["dmask"].ap()), writes=[bdm])
        P.op("dve", lambda e: e.tensor_copy(out=dm[:], in_=dmf[:]), reads=[bdm], writes=[bdm])
        P.op("pool", lambda e: e.memset(vsb[:, :, 128:130], 1.0), writes=[bvsb])
        P.dma("sp", lambda e: e.dma_start(out=krs[:], in_=self.KR.ap()), reads=[bscr], writes=[bkrs])
        pcount = [0]
        for hh in range(H):
            P.dma("sp", lambda e, hh=hh: e.dma_start(out=kn[:], in_=self.KN.ap()[hh]), reads=[bscr], writes=[bkn])
            P.dma("sp", lambda e, hh=hh: e.dma_start(
                out=vsb[:, :, 0:128], in_=self.VT.ap()[hh].rearrange("(b p) c -> p b c", p=128)),
                reads=[bscr], writes=[bvsb])
            for G in range(NG):
                q0 = G * TG
                qi = (hh * NG + G) % 2
                qn_, bqn_, qr_, bqr_ = qn[qi], bqn[qi], qr[qi], bqr[qi]
                P.dma("sp", lambda e, hh=hh, qn_=qn_: e.dma_start(out=qn_[:], in_=self.QN.ap()[hh, :, q0:q0 + TG]),
                      reads=[bscr], writes=[bqn_])
                P.dma("sp", lambda e, hh=hh, qr_=qr_: e.dma_start(out=qr_[:], in_=self.QR.ap()[hh, :, q0:q0 + TG]),
                      reads=[bscr], writes=[bqr_])
                nkb = 4 * (G + 1)

                def qk(kb):
                    j = kb - 4 * G
                    c0 = max(j, 0) * 128
                    bk = pcount[0] % 2
                    p_, bp_ = pT[pcount[0] % 3], bpT[pcount[0] % 3]
                    pcount[0] += 1
                    bank, bb = self.pb[bk], self.pbb[bk]
                    ks = slice(kb * 128, (kb + 1) * 128)
                    P.op("pe", lambda e: e.matmul(bank[:, c0:TG], lhsT=kn[:, ks], rhs=qn_[:, c0:TG], start=True, stop=False),
                         reads=[bkn, bqn_], writes=[bb])
                    P.op("pe", lambda e: e.matmul(bank[:, c0:TG], lhsT=krs[:, ks], rhs=qr_[:, c0:TG], start=False, stop=True),
                         reads=[bkrs, bqr_], writes=[bb])
                    return (kb, j, c0, bank, bb, p_, bp_)

                def pv(ctx):
                    kb, j, c0, bank, bb, p_, bp_ = ctx
                    P.op("act", lambda e: e.activation(out=p_[:, c0:TG], in_=bank[:, c0:TG], func=AF.Exp, scale=SCALE),
                         reads=[bb], writes=[bp_])
                    if j >= 0:
                        P.op("pool", lambda e: e.tensor_tensor(out=p_[:, c0:c0 + 128], in0=p_[:, c0:c0 + 128], in1=dm[:],
                                                               op=ALU.mult), reads=[bp_, bdm], writes=[bp_])
                    for qb in range(max(j, 0), 4):
                        P.op("pe", lambda e, qb=qb: e.matmul(
                            self.pb[2 + qb][:, 0:130], lhsT=p_[:, qb * 128:(qb + 1) * 128], rhs=vsb[:, kb, :],
                            start=(kb == 0), stop=(kb == 4 * G + qb)), reads=[bp_, bvsb], writes=[self.pbb[2 + qb]])

                prev = qk(0)
                for kb in range(1, nkb):
                    cur = qk(kb)
                    pv(prev)
                    prev = cur
                pv(prev)
                o_, bo_ = oT[(hh * NG + G) % 2], boT[(hh * NG + G) % 2]
                for qb in range(4):
                    P.op("dve", lambda e, qb=qb: e.reciprocal(out=rec[:, qb:qb + 1], in_=self.pb[2 + qb][:, 128:129]),
                         reads=[self.pbb[2 + qb]], writes=[brec])
                    n_, bn_ = on[qb % 2], bon[qb % 2]
                    P.op("act", lambda e, qb=qb, n_=n_: e.activation(out=n_[:], in_=self.pb[2 + qb][:, 0:128], func=AF.Copy,
                                                                     scale=rec[:, qb:qb + 1]),
                         reads=[self.pbb[2 + qb], brec], writes=[bn_])
                    P.op("pe", lambda e, n_=n_, qb=qb: e.transpose(out=self.pb[6][:, qb * 128:(qb + 1) * 128], in_=n_[:],
                                                                   identity=self.ident[:]),
                         reads=[bn_, self.b_const], writes=[self.pbb[6]])
                P.op("act", lambda e, o_=o_: e.activation(out=o_[:], in_=self.pb[6][:, :], func=AF.Copy),
                     reads=[self.pbb[6]], writes=[bo_])
                P.dma("sp", lambda e, o_=o_, hh=hh: e.dma_start(out=self.OT.ap()[:, hh, q0:q0 + TG], in_=o_[:]),
                      reads=[bo_], writes=[bscr])
        P.barrier()
    with ExitStack() as es:
        sbt = lambda n, s, d: es.enter_context(nc.sbuf_tensor("m3_" + n, list(s), d))
        h = sbt("h", [128, KC, TG], F32); bh = Buf()
        ob = sbt("ob", [128, KC, TG], BF16); bob = Buf()
        ws = WS(self, es, KC, "m3w")
        r3 = [0]
        for gi in range(NG):
            t0 = gi * TG
            self.load_h(src, t0, TG, h, bh)
            P.dma("sp", lambda e: e.dma_start(out=ob[:], in_=self.OT.ap()[:, :, t0:t0 + TG]), reads=[bscr], writes=[bob])
            for mc in range(KC):
                bk = r3[0] % 4; r3[0] += 1
                bank, bb = self.pb[bk], self.pbb[bk]
                self.lin(ws, w_o, mc, KC, lambda kc: ob[:, kc, :], [bob], bank[:, :TG], bb)
                P.op("dve", lambda e, mc=mc, bank=bank: e.scalar_tensor_tensor(
                    out=h[:, mc, :], in0=bank[:, :TG], scalar=self.mod(li, 2, mc), in1=h[:, mc, :],
                    op0=ALU.mult, op1=ALU.add), reads=[bb, self.b_mods, bh], writes=[bh])
            self.store_h(dst, t0, TG, h, bh)
        P.barrier()


Model._stage_mla = _mla_stage
```

```python
import math
from contextlib import ExitStack

import numpy as np
import concourse.bass as bass
import concourse.mybir as mybir
from concourse.bass_utils import run_bass_kernel_spmd

F32 = mybir.dt.float32
BF16 = mybir.dt.bfloat16
I32 = mybir.dt.int32
U32 = mybir.dt.uint32
AF = mybir.ActivationFunctionType
ALU = mybir.AluOpType
AX = mybir.AxisListType

D = 2048
KC = 16
DEPTH = 4
EPS = 1e-6
CONV_W = 31
NEG = -1.0e30


class Buf:
    __slots__ = ("w", "r")

    def __init__(self):
        self.w = None
        self.r = {}


class Prog:
    NDMA = 8

    def __init__(self, nc, es):
        self.nc = nc
        self.es = es
        self.E = {"pe": nc.tensor, "dve": nc.vector, "act": nc.scalar, "pool": nc.gpsimd, "sp": nc.sync}
        self.sems = {}
        self.cnt = {}
        for e in self.E:
            self.sems[e] = es.enter_context(nc.semaphore("s_" + e))
            self.cnt[e] = 0
        self.dq = {}
        for q in ("sp", "pool", "act"):
            slots = []
            for i in range(self.NDMA):
                k = "d_%s_%d" % (q, i)
                self.sems[k] = es.enter_context(nc.semaphore(k))
                slots.append(k)
            self.dq[q] = [slots, 0, [0] * self.NDMA]
        self.seen = {e: {} for e in self.E}
        self.ninst = 0

    def sb(self, name, shape, dt):
        return self.es.enter_context(self.nc.sbuf_tensor(name, list(shape), dt))

    def ps(self, name, shape, dt=F32):
        return self.es.enter_context(self.nc.psum_tensor(name, list(shape), dt))

    def _need(self, eng, k, v):
        if eng == "pe" and k == "pe":
            return
        if self.seen[eng].get(k, 0) < v:
            self.E[eng].wait_ge(self.sems[k], v)
            self.seen[eng][k] = v
            self.ninst += 1

    def _deps(self, eng, reads, writes):
        for b in reads:
            if b.w is not None:
                self._need(eng, b.w[0], b.w[1])
        for b in writes:
            if b.w is not None and not b.r:
                self._need(eng, b.w[0], b.w[1])
            for k, v in b.r.items():
                self._need(eng, k, v)

    def _record(self, tok, reads, writes):
        k, v = tok
        for b in reads:
            if b.r.get(k, 0) < v:
                b.r[k] = v
        for b in writes:
            b.w = tok
            b.r = {}

    def op(self, eng, fn, reads=(), writes=()):
        self._deps(eng, reads, writes)
        inst = fn(self.E[eng])
        self.cnt[eng] += 1
        inst.then_inc(self.sems[eng], 1)
        self.ninst += 1
        self._record((eng, self.cnt[eng]), reads, writes)

    def dma(self, q, fn, reads=(), writes=()):
        slots, nxt, counts = self.dq[q]
        s = nxt % self.NDMA
        self.dq[q][1] = nxt + 1
        k = slots[s]
        self._deps(q, reads, writes)
        self._need(q, k, counts[s] * 16)
        inst = fn(self.E[q])
        counts[s] += 1
        inst.then_inc(self.sems[k], 16)
        self.ninst += 1
        self._record((k, counts[s] * 16), reads, writes)

    def wait_all(self, eng, bufs):
        self._deps(eng, bufs, bufs)

    def barrier(self):
        for e in self.E:
            for k in self.E:
                if k != e:
                    self._need(e, k, self.cnt[k])
            for q in self.dq:
                slots, _, counts = self.dq[q]
                for s, k in enumerate(slots):
                    self._need(e, k, counts[s] * 16)


def _bc(ap, shape):
    return ap.to_broadcast(list(shape))


class Model:
    def __init__(self, S, stages):
        self.S = S
        self.stages = stages
        self.nc = bass.Bass("TRN2", target_bir_lowering=False)
        self.din = {}

    def inp(self, name, shape, dt=F32):
        t = self.nc.dram_tensor(name, list(shape), dt, kind="ExternalInput")
        self.din[name] = t
        return t

    def build(self):
        nc = self.nc
        S = self.S
        NB = S // 128
        with ExitStack() as es:
            P = Prog(nc, es)
            self.P = P
            xT = self.inp("xT", [128, KC, S])
            cT = self.inp("cT", [128, KC])
            pos = self.inp("pos", [1, S], I32)
            ada_w = self.inp("ada_w", [DEPTH, KC, 128, 6 * D])
            vecs = self.inp("vecs", [128, self.NV])
            ident_d = self.inp("ident", [128, 128])
            outT = nc.dram_tensor("outT", [128, KC, S], F32, kind="ExternalOutput")
            hS = nc.dram_tensor("hS", [128, KC, S], F32)
            self.hbufs = {"x": [Buf() for _ in range(NB)], "h": [Buf() for _ in range(NB)],
                          "o": [Buf() for _ in range(NB)]}
            self.hten = {"x": xT, "h": hS, "o": outT}
            self.ident = P.sb("ident_s", [128, 128], F32)
            self.b_const = Buf()
            P.dma("sp", lambda e: e.dma_start(out=self.ident[:], in_=ident_d.ap()), writes=[self.b_const])
            self.ones = P.sb("ones", [128, 128], F32)
            P.op("pool", lambda e: e.memset(self.ones[:], 1.0), writes=[self.b_const])
            self.epsb = P.sb("epsb", [128, 1], F32)
            P.op("pool", lambda e: e.memset(self.epsb[:], EPS), writes=[self.b_const])
            self.vec = P.sb("vec", [128, self.NV], F32)
            self.b_vec = Buf()
            P.dma("sp", lambda e: e.dma_start(out=self.vec[:], in_=vecs.ap()), writes=[self.b_vec])
            self.pb = [P.ps("pb%d" % i, [128, 512]) for i in range(8)]
            self.pbb = [Buf() for _ in range(8)]
            self.mods = P.sb("mods", [128, DEPTH, 96], F32)
            self.b_mods = Buf()
            self._stage_mods(cT, ada_w)
            for st in self.stages:
                kind, li = st[0], st[1]
                if kind == "conv":
                    s = li // 3
                    if "conv_w_pw1_%d" % s not in self.din:
                        self.inp("conv_w_pw1_%d" % s, [32, 128, KC, 128])
                        self.inp("conv_w_pw2_%d" % s, [16, 128, KC, 128])
                if kind == "peer":
                    if "peer_wq_%d" % li not in self.din:
                        self.inp("peer_wq_%d" % li, [16, 128, KC, 128])
                        self.inp("peer_skT_%d" % li, [128, 16, 128])
                        self.inp("peer_u_%d" % li, [16384, D])
                        self.inp("peer_v_%d" % li, [16384, D])
                if kind == "hgrn":
                    self.inp("hg_w_qfg", [48, 128, KC, 128])
                    self.inp("hg_w_i", [16, 128, KC, 128])
                    self.inp("hg_w_out", [16, 128, KC, 128])
                if kind in ("hgrn", "mla") and "tri" not in self.din:
                    self.inp("tri", [128, 128])
                if kind == "mla":
                    self.declare_mla()
            for st in self.stages:
                kind, li, src, dst = st
                getattr(self, "_stage_" + kind)(li, src, dst)
            P.wait_all("sp", self.hbufs["o"])
            self.ninst = P.ninst
        return nc

    NV = 0
    VOFF = {}

    def v(self, name, j=0, n=1):
        o = self.VOFF[name] + j
        return self.vec[:, o:o + n]

    def _stage_mods(self, cT, ada_w):
        P = self.P
        with ExitStack() as es:
            P2 = P
            nc = self.nc
            sc = es.enter_context(nc.sbuf_tensor("m_sc", [128, KC, 2], F32))
            c0 = es.enter_context(nc.sbuf_tensor("m_c0", [128, KC], F32))
            wt = [es.enter_context(nc.sbuf_tensor("m_wt%d" % i, [128, KC, 512], F32)) for i in range(2)]
            bsc, bc0 = Buf(), Buf()
            bwt = [Buf(), Buf()]
            P.dma("sp", lambda e: e.dma_start(out=c0[:], in_=cT.ap()), writes=[bc0])
            for j in range(2):
                P.op("act", lambda e, j=j: e.activation(out=sc[:, :, j], in_=c0[:], func=AF.Silu),
                     reads=[bc0], writes=[bsc])
            n = 0
            for li in range(DEPTH):
                for cb in range(6 * D // 512):
                    w = wt[n % 2]
                    bw = bwt[n % 2]
                    n += 1
                    P.dma("sp", lambda e, w=w, li=li, cb=cb: e.dma_start(
                        out=w[:], in_=ada_w.ap()[li, :, :, cb * 512:(cb + 1) * 512].rearrange("k p c -> p k c")),
                        writes=[bw])
                    bank = self.pb[n % 2]
                    bb = self.pbb[n % 2]
                    for j in range(4):
                        for kc in range(KC):
                            P.op("pe", lambda e, w=w, j=j, kc=kc, bank=bank: e.matmul(
                                bank[:, 2 * j:2 * j + 2], lhsT=w[:, kc, j * 128:(j + 1) * 128], rhs=sc[:, kc, :],
                                start=(kc == 0), stop=(kc == KC - 1)),
                                reads=[bw, bsc], writes=[bb])
                    for j in range(4):
                        col = cb * 4 + j
                        P.op("dve", lambda e, j=j, col=col, li=li, bank=bank: e.tensor_tensor(
                            out=self.mods[:, li, col:col + 1], in0=bank[:, 2 * j:2 * j + 1],
                            in1=self.v("ada_b", li * 96 + col), op=ALU.add),
                            reads=[bb, self.b_vec], writes=[self.b_mods])
            P.barrier()

    def mod(self, li, s, kc):
        return self.mods[:, li, s * 16 + kc:s * 16 + kc + 1]

    debug = False
    dumps = None

    def dump(self, name, ap, shape, dt, reads):
        if not self.debug:
            return
        if self.dumps is None:
            self.dumps = {}
        if name in self.dumps:
            return
        t = self.nc.dram_tensor("dbg_" + name, list(shape), dt, kind="ExternalOutput")
        self.dumps[name] = t
        b = Buf()
        self.P.dma("sp", lambda e: e.dma_start(out=t.ap(), in_=ap), reads=reads, writes=[b])
        self.hbufs["o"].append(b)

    def hbl(self, which, t0, n):
        return self.hbufs[which][t0 // 128:(t0 + n) // 128]

    def load_h(self, which, t0, N, h, bh):
        ten = self.hten[which]
        self.P.dma("sp", lambda e: e.dma_start(out=h[:, :, :N], in_=ten.ap()[:, :, t0:t0 + N]),
                   reads=self.hbl(which, t0, N), writes=[bh])

    def store_h(self, which, t0, N, h, bh):
        ten = self.hten[which]
        self.P.dma("sp", lambda e: e.dma_start(out=ten.ap()[:, :, t0:t0 + N], in_=h[:, :, :N]),
                   reads=[bh], writes=self.hbl(which, t0, N))

    def make_A(self, es, li, half):
        P = self.P
        A = es.enter_context(self.nc.sbuf_tensor("A_%d_%d" % (li, half), [128, KC], F32))
        bA = Buf()
        s_sc = 1 + 3 * half
        P.op("dve", lambda e: e.tensor_scalar(out=A[:], in0=self.mods[:, li, s_sc * 16:(s_sc + 1) * 16],
                                              scalar1=1.0, scalar2=None, op0=ALU.add),
             reads=[self.b_mods], writes=[bA])
        P.op("dve", lambda e: e.tensor_tensor(out=A[:], in0=A[:], in1=self.v("norm_g", (li * 2 + half) * 16, 16),
                                              op=ALU.mult),
             reads=[bA, self.b_vec], writes=[bA])
        return A, bA

    def colsum(self, srcs, nk, N, bank, bbank, sq, bsq, square=True, extra_reads=()):
        P = self.P
        for kc in range(nk):
            s = sq[kc % 2]
            bs = bsq[kc % 2]
            if square:
                P.op("act", lambda e, s=s, kc=kc: e.activation(out=s[:, :N], in_=srcs(kc), func=AF.Square),
                     reads=list(extra_reads), writes=[bs])
                rhs = s[:, :N]
                rd = [bs, self.b_const]
            else:
                rhs = srcs(kc)
                rd = list(extra_reads) + [self.b_const]
            P.op("pe", lambda e, rhs=rhs, kc=kc: e.matmul(bank[:, :N], lhsT=self.ones[:], rhs=rhs,
                                                          start=(kc == 0), stop=(kc == nk - 1)),
                 reads=rd, writes=[bbank])

    def norm_mod(self, es, h, bh, N, A, bA, li, half, out, bout, tag):
        P = self.P
        nc = self.nc
        sq = [es.enter_context(nc.sbuf_tensor("%s_sq%d" % (tag, i), [128, 512], F32)) for i in range(2)]
        bsq = [Buf(), Buf()]
        r = es.enter_context(nc.sbuf_tensor("%s_r" % tag, [128, 512], F32))
        br = Buf()
        s_sh = 3 * half

        def run():
            bank, bb = self.pb[7], self.pbb[7]
            self.colsum(lambda kc: h[:, kc, :N], KC, N, bank, bb, sq, bsq, extra_reads=[bh])
            P.op("act", lambda e: e.activation(out=r[:, :N], in_=bank[:, :N], func=AF.Sqrt, bias=self.epsb[:],
                                               scale=1.0 / D), reads=[bb, self.b_const], writes=[br])
            P.op("dve", lambda e: e.reciprocal(out=r[:, :N], in_=r[:, :N]), reads=[br], writes=[br])
            for kc in range(KC):
                s = sq[kc % 2]
                bs = bsq[kc % 2]
                P.op("dve", lambda e, s=s, kc=kc: e.scalar_tensor_tensor(
                    out=s[:, :N], in0=h[:, kc, :N], scalar=A[:, kc:kc + 1], in1=r[:, :N], op0=ALU.mult, op1=ALU.mult),
                    reads=[bh, bA, br], writes=[bs])
                P.op("act", lambda e, s=s, kc=kc: e.activation(
                    out=out[:, kc, :N], in_=s[:, :N], func=AF.Identity, bias=self.mod(li, s_sh, kc), scale=1.0),
                    reads=[bs, self.b_mods], writes=[bout])
        return run

    def linear_fm(self, es, wd, KCin, mlist, x, bx, N, epi, tag, banks=(0, 1, 2, 3), f32=False, x_col0=0):
        P = self.P
        nc = self.nc
        wst = [es.enter_context(nc.sbuf_tensor("%s_ws%d" % (tag, i), [128, KCin, 128], F32)) for i in range(2)]
        bws = [Buf(), Buf()]
        wdb = None
        if not f32:
            wbf = [es.enter_context(nc.sbuf_tensor("%s_wb%d" % (tag, i), [128, KCin, 128], BF16)) for i in range(2)]
            bwb = [Buf(), Buf()]
            ntile = max(mlist) + 1
            wdb = nc.dram_tensor("wb_" + tag, [ntile, 128, KCin, 128], BF16)
            bdst = Buf()
            for t in range(ntile):
                k = t % 2
                P.dma("sp", lambda e, k=k, t=t: e.dma_start(out=wst[k][:], in_=wd.ap()[t]), writes=[bws[k]])
                if t % 2 == 0:
                    P.op("act", lambda e, k=k: e.activation(out=wbf[k][:], in_=wst[k][:], func=AF.Copy),
                         reads=[bws[k]], writes=[bwb[k]])
                else:
                    P.op("dve", lambda e, k=k: e.tensor_copy(out=wbf[k][:], in_=wst[k][:]), reads=[bws[k]], writes=[bwb[k]])
                P.dma("sp", lambda e, k=k, t=t: e.dma_start(out=wdb.ap()[t], in_=wbf[k][:]), reads=[bwb[k]], writes=[bdst])

        def run(mlist=mlist, x=x, bx=bx, N=N, epi=epi, x_col0=x_col0):
            for i, mc in enumerate(mlist):
                st = self._wrot
                self._wrot += 1
                w, bw = wst[st % 2], bws[st % 2]
                if f32:
                    P.dma("sp", lambda e, w=w, mc=mc: e.dma_start(out=w[:], in_=wd.ap()[mc]), writes=[bw])
                    wm, bm = w, bw
                else:
                    wm, bm = wbf[st % 2], bwb[st % 2]
                    P.dma("sp", lambda e, wm=wm, mc=mc: e.dma_start(out=wm[:], in_=wdb.ap()[mc]), reads=[bdst], writes=[bm])
                bk = banks[st % len(banks)]
                bank, bb = self.pb[bk], self.pbb[bk]
                for kc in range(KCin):
                    P.op("pe", lambda e, wm=wm, kc=kc, bank=bank: e.matmul(
                        bank[:, :N], lhsT=wm[:, kc, :], rhs=x[:, kc, x_col0:x_col0 + N],
                        start=(kc == 0), stop=(kc == KCin - 1)),
                        reads=[bm, bx], writes=[bb])
                epi(i, mc, bank, bb)
        return run

    _wrot = 0

    def _stage_conv(self, li, src, dst):
        P = self.P
        nc = self.nc
        S = self.S
        slot = li // 3
        TG = 512
        wd1 = self.din["conv_w_pw1_%d" % slot]
        wd2 = self.din["conv_w_pw2_%d" % slot]
        pre = "conv%d_" % slot
        with ExitStack() as es:
            sbt = lambda n, s, d: es.enter_context(nc.sbuf_tensor("cv%d_" % li + n, list(s), d))
            h = sbt("h", [128, KC, TG], F32); bh = Buf()
            u = sbt("u", [128, KC, TG], BF16); bu = Buf()
            glu = sbt("glu", [128, KC, 30 + TG], F32); bglu = [Buf() for _ in range(KC)]
            y = sbt("y", [128, KC, TG], F32); by = [Buf() for _ in range(KC)]
            z = sbt("z", [128, KC, TG], BF16); bz = Buf()
            sig = [sbt("sig%d" % i, [128, TG], F32) for i in range(2)]; bsig = [Buf(), Buf()]
            mean = sbt("mean", [128, TG], F32); bmean = Buf()
            rstd = sbt("rstd", [128, TG], F32); brstd = Buf()
            tmp = [sbt("tmp%d" % i, [128, TG], F32) for i in range(2)]; btmp = [Buf(), Buf()]
            gb = sbt("gb", [128, KC], F32); bgb = Buf()
            A, bA = self.make_A(es, li, 0)
            P.op("dve", lambda e: e.tensor_tensor(out=gb[:], in0=self.mods[:, li, 32:48], in1=self.v(pre + "b_pw2", 0, 16),
                                                  op=ALU.mult), reads=[self.b_mods, self.b_vec], writes=[bgb])
            P.op("pool", lambda e: e.memset(glu[:, :, 0:30], 0.0), writes=bglu)
            nm = self.norm_mod(es, h, bh, TG, A, bA, li, 0, u, bu, "cvn%d" % li)

            def epi1(i, mc, bank, bb):
                if mc >= KC:
                    j = mc - KC
                    s, bs = sig[j % 2], bsig[j % 2]
                    P.op("act", lambda e: e.activation(out=s[:], in_=bank[:, :TG], func=AF.Sigmoid,
                                                       bias=self.v(pre + "b_pw1", mc), scale=1.0),
                         reads=[bb, self.b_vec], writes=[bs])
                else:
                    j = mc
                    s, bs = sig[j % 2], bsig[j % 2]
                    P.op("dve", lambda e: e.scalar_tensor_tensor(
                        out=glu[:, j, 30:30 + TG], in0=bank[:, :TG], scalar=self.v(pre + "b_pw1", mc), in1=s[:],
                        op0=ALU.add, op1=ALU.mult), reads=[bb, bs, self.b_vec], writes=[bglu[j]])
            ml1 = []
            for j in range(KC):
                ml1 += [KC + j, j]
            lin1 = self.linear_fm(es, wd1, KC, ml1, u, bu, TG, epi1, "cv1_%d" % li)

            def epi2(i, mc, bank, bb):
                P.op("dve", lambda e: e.scalar_tensor_tensor(
                    out=h[:, mc, :], in0=bank[:, :TG], scalar=self.mod(li, 2, mc), in1=h[:, mc, :],
                    op0=ALU.mult, op1=ALU.add), reads=[bb, self.b_mods, bh], writes=[bh])
                P.op("dve", lambda e: e.tensor_scalar(out=h[:, mc, :], in0=h[:, mc, :], scalar1=gb[:, mc:mc + 1],
                                                      scalar2=None, op0=ALU.add), reads=[bh, bgb], writes=[bh])
            lin2 = self.linear_fm(es, wd2, KC, list(range(KC)), z, bz, TG, epi2, "cv2_%d" % li)

            for g in range(S // TG):
                t0 = g * TG
                self.load_h(src, t0, TG, h, bh)
                nm()
                lin1()
                for j in range(KC):
                    P.op("dve", lambda e, j=j: e.tensor_scalar(
                        out=y[:, j, :], in0=glu[:, j, 0:TG], scalar1=self.v(pre + "w_dw", j * CONV_W + 0),
                        scalar2=self.v(pre + "b_dw", j), op0=ALU.mult, op1=ALU.add),
                        reads=[bglu[j], self.b_vec], writes=[by[j]])
                    for w in range(1, CONV_W):
                        P.op("dve", lambda e, j=j, w=w: e.scalar_tensor_tensor(
                            out=y[:, j, :], in0=glu[:, j, w:w + TG], scalar=self.v(pre + "w_dw", j * CONV_W + w),
                            in1=y[:, j, :], op0=ALU.mult, op1=ALU.add),
                            reads=[bglu[j], by[j], self.b_vec], writes=[by[j]])
                    P.op("pool", lambda e, j=j: e.tensor_copy(out=glu[:, j, 0:30], in_=glu[:, j, TG:TG + 30]),
                         reads=[bglu[j]], writes=[bglu[j]])
                self.colsum(lambda kc: y[:, kc, :], KC, TG, self.pb[4], self.pbb[4], tmp, btmp, square=False,
                            extra_reads=by)
                self.colsum(lambda kc: y[:, kc, :], KC, TG, self.pb[5], self.pbb[5], tmp, btmp, square=True,
                            extra_reads=by)
                P.op("act", lambda e: e.activation(out=mean[:], in_=self.pb[4][:, :TG], func=AF.Copy, scale=1.0 / D),
                     reads=[self.pbb[4]], writes=[bmean])
                P.op("dve", lambda e: e.tensor_tensor(out=rstd[:], in0=mean[:], in1=mean[:], op=ALU.mult),
                     reads=[bmean], writes=[brstd])
                P.op("dve", lambda e: e.scalar_tensor_tensor(out=rstd[:], in0=self.pb[5][:, :TG], scalar=1.0 / D,
                                                             in1=rstd[:], op0=ALU.mult, op1=ALU.subtract),
                     reads=[self.pbb[5], brstd], writes=[brstd])
                P.op("act", lambda e: e.activation(out=rstd[:], in_=rstd[:], func=AF.Sqrt, bias=self.epsb[:], scale=1.0),
                     reads=[brstd, self.b_const], writes=[brstd])
                P.op("dve", lambda e: e.reciprocal(out=rstd[:], in_=rstd[:]), reads=[brstd], writes=[brstd])
                for j in range(KC):
                    t, bt = tmp[j % 2], btmp[j % 2]
                    P.op("dve", lambda e, j=j, t=t: e.tensor_tensor(out=t[:], in0=y[:, j, :], in1=mean[:], op=ALU.subtract),
                         reads=[by[j], bmean], writes=[bt])
                    P.op("dve", lambda e, t=t: e.tensor_tensor(out=t[:], in0=t[:], in1=rstd[:], op=ALU.mult),
                         reads=[bt, brstd], writes=[bt])
                    P.op("act", lambda e, j=j, t=t: e.activation(out=z[:, j, :], in_=t[:], func=AF.Silu,
                                                                 bias=self.v(pre + "ln_b", j), scale=self.v(pre + "ln_g", j)),
                         reads=[bt, self.b_vec], writes=[bz])
                lin2()
                self.store_h(dst, t0, TG, h, bh)
            P.barrier()

    def _stage_final(self, li, src, dst):
        P = self.P
        nc = self.nc
        S = self.S
        TG = 512
        with ExitStack() as es:
            sbt = lambda n, s, d: es.enter_context(nc.sbuf_tensor("fn_" + n, list(s), d))
            h = sbt("h", [128, KC, TG], F32); bh = Buf()
            o = sbt("o", [128, KC, TG], F32); bo = Buf()
            sq = [sbt("sq%d" % i, [128, TG], F32) for i in range(2)]; bsq = [Buf(), Buf()]
            r = sbt("r", [128, TG], F32); br = Buf()
            for g in range(S // TG):
                t0 = g * TG
                self.load_h(src, t0, TG, h, bh)
                bank, bb = self.pb[7], self.pbb[7]
                self.colsum(lambda kc: h[:, kc, :], KC, TG, bank, bb, sq, bsq, extra_reads=[bh])
                P.op("act", lambda e: e.activation(out=r[:], in_=bank[:, :TG], func=AF.Sqrt, bias=self.epsb[:],
                                                   scale=1.0 / D), reads=[bb, self.b_const], writes=[br])
                P.op("dve", lambda e: e.reciprocal(out=r[:], in_=r[:]), reads=[br], writes=[br])
                for kc in range(KC):
                    P.op("dve", lambda e, kc=kc: e.scalar_tensor_tensor(
                        out=o[:, kc, :], in0=h[:, kc, :], scalar=self.v("final_g", kc), in1=r[:],
                        op0=ALU.mult, op1=ALU.mult), reads=[bh, br, self.b_vec], writes=[bo])
                self.store_h(dst, t0, TG, o, bo)
            P.barrier()


_VEC_SPEC = [("ada_b", 4 * 96), ("norm_g", 4 * 2 * 16), ("final_g", 16),
             ("conv0_b_pw1", 32), ("conv0_w_dw", 16 * CONV_W), ("conv0_b_dw", 16), ("conv0_ln_g", 16),
             ("conv0_ln_b", 16), ("conv0_b_pw2", 16),
             ("conv1_b_pw1", 32), ("conv1_w_dw", 16 * CONV_W), ("conv1_b_dw", 16), ("conv1_ln_g", 16),
             ("conv1_ln_b", 16), ("conv1_b_pw2", 16),
             ("hg_lb", 4 * 16), ("hg_norm_g", 1), ("mla_qg", 4), ("mla_kvg", 4), ("inv_freq", 1)]
_o = 0
for _n, _w in _VEC_SPEC:
    Model.VOFF[_n] = _o
    _o += _w
Model.NV = _o


def _fvec(v):
    v = np.asarray(v, np.float32).reshape(-1)
    return v.reshape(-1, 128).T


def _wtiles(W):
    K, M = W.shape
    return np.ascontiguousarray(W.reshape(K // 128, 128, M // 128, 128).transpose(2, 1, 0, 3))


def pack_vecs(inp):
    parts = {}
    parts["ada_b"] = np.concatenate([_fvec(inp["ada_b"][i]) for i in range(4)], axis=1)
    parts["norm_g"] = np.concatenate([_fvec(inp["norm_g"][i, j]) for i in range(4) for j in range(2)], axis=1)
    parts["final_g"] = _fvec(inp["final_g"])
    for s in range(2):
        p = "conv%d_" % s
        parts[p + "b_pw1"] = _fvec(inp["conv_b_pw1"][s])
        wdw = np.asarray(inp["conv_w_dw"][s], np.float32)
        parts[p + "w_dw"] = wdw.T.reshape(16, 128, CONV_W).transpose(1, 0, 2).reshape(128, 16 * CONV_W)
        parts[p + "b_dw"] = _fvec(inp["conv_b_dw"][s])
        parts[p + "ln_g"] = _fvec(inp["conv_ln_g"][s])
        parts[p + "ln_b"] = _fvec(inp["conv_ln_b"][s])
        parts[p + "b_pw2"] = _fvec(inp["conv_b_pw2"][s])
    parts["hg_lb"] = np.concatenate([_fvec(inp["hg_lb_logits"][i]) for i in range(4)], axis=1)
    parts["hg_norm_g"] = np.asarray(inp["hg_norm_g"][0], np.float32).reshape(128, 1)
    parts["mla_qg"] = _fvec(inp["mla_q_norm_g"][0])
    parts["mla_kvg"] = _fvec(inp["mla_kv_norm_g"][0])
    invf = np.zeros((128, 1), np.float32)
    invf[:32, 0] = (10000.0 ** (-np.arange(0, 64, 2, dtype=np.float32) / 64.0)).astype(np.float32)
    invf[32:64, 0] = invf[:32, 0]
    parts["inv_freq"] = invf
    cols = []
    for n, w in _VEC_SPEC:
        a = np.asarray(parts[n], np.float32)
        assert a.shape == (128, w), (n, a.shape, w)
        cols.append(a)
    return np.ascontiguousarray(np.concatenate(cols, axis=1))


def pack_inputs(inp, S, needed):
    m = {}
    x = np.asarray(inp["x"], np.float32)[0, :S]
    m["xT"] = np.ascontiguousarray(x.T.reshape(KC, 128, S).transpose(1, 0, 2))
    m["cT"] = np.ascontiguousarray(_fvec(inp["c"][0]))
    m["pos"] = np.ascontiguousarray(np.asarray(inp["positions"], np.int32)[:, :S])
    m["ada_w"] = np.asarray(inp["ada_w"], np.float32).reshape(DEPTH, KC, 128, 6 * D)
    m["vecs"] = pack_vecs(inp)
    m["ident"] = np.eye(128, dtype=np.float32)
    for s in range(2):
        if "conv_w_pw1_%d" % s in needed:
            m["conv_w_pw1_%d" % s] = _wtiles(np.asarray(inp["conv_w_pw1"][s], np.float32))
            m["conv_w_pw2_%d" % s] = _wtiles(np.asarray(inp["conv_w_pw2"][s], np.float32))
    if "tri" in needed:
        m["tri"] = np.triu(np.ones((128, 128), np.float32))
    if "mla_w_in" in needed:
        w = np.asarray(inp["mla_w_in"][0], np.float32)
        wp = np.zeros((D, 9 * 128), np.float32)
        wp[:, :1088] = w
        m["mla_w_in"] = _wtiles(wp)
        wq = np.asarray(inp["mla_w_uq"][0], np.float32).reshape(512, 16, 192)
        wqp = np.zeros((512, 16, 2, 128), np.float32)
        wqp[:, :, 0, :] = wq[:, :, 0:128]
        wqp[:, :, 1, 0:64] = wq[:, :, 128:192]
        m["mla_w_uq"] = _wtiles(wqp.reshape(512, 32 * 128))
        m["mla_w_ukv"] = _wtiles(np.asarray(inp["mla_w_ukv"][0], np.float32))
        m["mla_w_o"] = _wtiles(np.asarray(inp["mla_w_o"][0], np.float32))
        ps = np.zeros((64, 64), np.float32)
        ps[np.arange(32) + 32, np.arange(32)] = 1.0
        ps[np.arange(32), np.arange(32) + 32] = 1.0
        m["pswap"] = ps
        dmk = np.ones((128, 128), np.float32)
        dmk[64:, :64] = 0.0
        m["dmask"] = dmk
        sg = np.ones((128, 1), np.float32)
        sg[:32] = -1.0
        m["rsgn"] = sg
    if "hg_w_qfg" in needed:
        w_in = np.asarray(inp["hg_w_in"][0], np.float32)
        m["hg_w_qfg"] = _wtiles(np.concatenate([w_in[:, 0:D], w_in[:, D:2 * D], w_in[:, 3 * D:4 * D]], axis=1))
        m["hg_w_i"] = _wtiles(w_in[:, 2 * D:3 * D])
        m["hg_w_out"] = _wtiles(np.asarray(inp["hg_w_out"][0], np.float32))
    for li in range(DEPTH):
        if "peer_wq_%d" % li in needed:
            m["peer_wq_%d" % li] = _wtiles(np.asarray(inp["peer_w_q"][li], np.float32))
            sk = np.asarray(inp["peer_sub_keys"][li], np.float32).reshape(16, 128, 128)
            m["peer_skT_%d" % li] = np.ascontiguousarray(sk.transpose(2, 0, 1))
            m["peer_u_%d" % li] = np.asarray(inp["peer_u"][li], np.float32)
            m["peer_v_%d" % li] = np.asarray(inp["peer_v"][li], np.float32)
    return m


def unpack_out(outT, S):
    return np.ascontiguousarray(outT.transpose(2, 1, 0).reshape(S, D))[None]


def run_model(inputs, S, stages, debug=False):
    m = Model(S, stages)
    m.debug = debug
    nc = m.build()
    in_map = pack_inputs(inputs, S, set(m.din.keys()))
    in_map = {k: in_map[k] for k in m.din}
    res = run_bass_kernel_spmd(nc, [in_map], core_ids=[0])
    m.results = res.results[0]
    return unpack_out(res.results[0]["outT"], S), m


FULL_STAGES = [("conv", 0, "x", "h"), ("peer", 0, "h", "h"),
               ("hgrn", 1, "h", "h"), ("peer", 1, "h", "h"),
               ("mla", 2, "h", "h"), ("peer", 2, "h", "h"),
               ("conv", 3, "h", "h"), ("peer", 3, "h", "h"),
               ("final", 0, "h", "o")]


def kernel(**inputs):
    out, _ = run_model(inputs, 8192, FULL_STAGES)
    return out.astype(np.float32)


def _peer_stage(self, li, src, dst):
    P = self.P
    nc = self.nc
    S = self.S
    TG = 256
    NBLK = TG // 128
    R = 8
    wq = self.din["peer_wq_%d" % li]
    skd = self.din["peer_skT_%d" % li]
    utab_f = self.din["peer_u_%d" % li]
    vtab_f = self.din["peer_v_%d" % li]
    if getattr(self, "ubf", None) is None:
        self.ubf = nc.dram_tensor("peer_ubf", [16384, D], BF16)
        self.vbf = nc.dram_tensor("peer_vbf", [16384, D], BF16)
    utab, vtab = self.ubf, self.vbf
    btab = Buf()
    with ExitStack() as es:
        JR = 2
        stg = [es.enter_context(nc.sbuf_tensor("pc%d_s%d" % (li, i), [128, JR, D], F32)) for i in range(2)]
        bstg = [Buf(), Buf()]
        cvt = [es.enter_context(nc.sbuf_tensor("pc%d_c%d" % (li, i), [128, JR, D], BF16)) for i in range(2)]
        bcvt = [Buf(), Buf()]
        n = 0
        for src_t, dst_t in ((utab_f, utab), (vtab_f, vtab)):
            for rt_ in range(16384 // (128 * JR)):
                k = n % 2
                r0 = rt_ * 128 * JR
                P.dma("sp", lambda e, k=k, r0=r0, src_t=src_t: e.dma_start(
                    out=stg[k][:], in_=src_t.ap()[r0:r0 + 128 * JR, :].rearrange("(p j) c -> p j c", j=JR)),
                    writes=[bstg[k]])
                eng = "act" if n % 2 == 0 else "dve"
                if eng == "act":
                    P.op("act", lambda e, k=k: e.activation(out=cvt[k][:], in_=stg[k][:], func=AF.Copy),
                         reads=[bstg[k]], writes=[bcvt[k]])
                else:
                    P.op("dve", lambda e, k=k: e.tensor_copy(out=cvt[k][:], in_=stg[k][:]),
                         reads=[bstg[k]], writes=[bcvt[k]])
                P.dma("sp", lambda e, k=k, r0=r0, dst_t=dst_t: e.dma_start(
                    out=dst_t.ap()[r0:r0 + 128 * JR, :].rearrange("(p j) c -> p j c", j=JR), in_=cvt[k][:]),
                    reads=[bcvt[k]], writes=[btab])
                n += 1
        P.barrier()
    with ExitStack() as es:
        sbt = lambda n, s, d: es.enter_context(nc.sbuf_tensor("pr%d_" % li + n, list(s), d))
        h = sbt("h", [128, KC, TG], F32); bh = Buf()
        u = sbt("u", [128, KC, TG], F32); bu = Buf()
        qT = sbt("qT", [128, KC, TG], F32); bq = Buf()
        skT = sbt("skT", [128, 16, 128], F32); bsk = Buf()
        Ssb = sbt("S", [128, 16, 128], F32); bS = Buf()
        S2 = sbt("S2", [128, 2048], F32); bS2 = Buf()
        m8 = sbt("m8", [128, 16, 16], F32); bm8 = Buf()
        i8 = sbt("i8", [128, 16, 16], U32); bi8 = Buf()
        i8f = sbt("i8f", [128, 16, 16], F32); bi8f = Buf()
        cand = sbt("cand", [128, 8, 256], F32); bcand = Buf()
        bs = sbt("bs", [128, 8, 16], F32); bbs = Buf()
        bp = sbt("bp", [128, 8, 16], U32); bbp = Buf()
        posf = sbt("posf", [128, 128], F32); bposf = Buf()
        af = sbt("af", [128, 128], F32); baf = Buf()
        bf = sbt("bf", [128, 128], F32); bbf = Buf()
        io16 = sbt("io16", [128, 128, 16], F32); bio = Buf()
        oh = sbt("oh", [128, 128, 16], F32); boh = Buf()
        isel = sbt("isel", [128, 128], F32); bisel = Buf()
        jsel = sbt("jsel", [128, 128], F32); bjsel = Buf()
        eidx = sbt("eidx", [128, 128], I32); beidx = Buf()
        gate = sbt("gate", [128, 8, 16], F32); bgate = Buf()
        gsum = sbt("gsum", [128, 8], F32); bgsum = Buf()
        av = sbt("a", [128, 128], F32); bav = Buf()
        t1 = sbt("t1", [128, 128], F32); bt1 = Buf()
        wv = sbt("w", [128, 128], F32); bwv = Buf()
        utok = sbt("utok", [128, D], F32); butok = Buf()
        acc = sbt("acc", [128, D], F32); bacc = Buf()
        scr = sbt("scr", [128, D], BF16); bscr = Buf()
        gb = [sbt("g%d" % i, [128, D], BF16) for i in range(R)]; bgb = [Buf() for _ in range(R)]
        A, bA = self.make_A(es, li, 1)
        identb = sbt("identb", [128, 128], BF16); bidb = Buf()
        P.op("dve", lambda e: e.tensor_copy(out=identb[:], in_=self.ident[:]), reads=[self.b_const], writes=[bidb])
        dg = [sbt("dg%d" % i, [128, 128], BF16) for i in range(4)]; bdg = [Buf() for _ in range(4)]
        P.dma("sp", lambda e: e.dma_start(out=skT[:], in_=skd.ap()), writes=[bsk])
        P.op("pool", lambda e: e.iota(io16[:], pattern=[[0, 128], [1, 16]], base=0, channel_multiplier=0,
                                      allow_small_or_imprecise_dtypes=True), writes=[bio])
        nm = self.norm_mod(es, h, bh, TG, A, bA, li, 1, u, bu, "prn%d" % li)

        def epiq(i, mc, bank, bb):
            P.op("act", lambda e: e.activation(out=qT[:, mc, :], in_=bank[:, :TG], func=AF.Copy),
                 reads=[bb], writes=[bq])
        linq = self.linear_fm(es, wq, KC, list(range(KC)), u, bu, TG, epiq, "prq%d" % li, banks=(0, 1), f32=True)

        def top16_many(items, n, rd):
            ng = len(items)
            bv = [Buf() for _ in range(ng)]
            bi = [Buf() for _ in range(ng)]
            b2 = [Buf() for _ in range(ng)]
            P._deps("dve", [], [bS2])
            for gi_, (s_, v_, i_) in enumerate(items):
                P.op("dve", lambda e, s_=s_, v_=v_: e.max(out=v_[:, 0:8], in_=s_), reads=rd, writes=[bv[gi_]])
            for gi_, (s_, v_, i_) in enumerate(items):
                P.op("dve", lambda e, s_=s_, v_=v_, i_=i_: e.max_index(out=i_[:, 0:8], in_max=v_[:, 0:8], in_values=s_),
                     reads=rd + [bv[gi_]], writes=[bi[gi_]])
            for gi_, (s_, v_, i_) in enumerate(items):
                P.op("dve", lambda e, s_=s_, v_=v_, gi_=gi_: e.match_replace(
                    out=S2[:, gi_ * n:(gi_ + 1) * n], in_to_replace=v_[:, 0:8], in_values=s_, imm_value=NEG),
                    reads=rd + [bv[gi_]], writes=[b2[gi_]])
            for gi_, (s_, v_, i_) in enumerate(items):
                P.op("dve", lambda e, v_=v_, gi_=gi_: e.max(out=v_[:, 8:16], in_=S2[:, gi_ * n:(gi_ + 1) * n]),
                     reads=[b2[gi_]], writes=[bv[gi_]])
            for gi_, (s_, v_, i_) in enumerate(items):
                P.op("dve", lambda e, v_=v_, i_=i_, gi_=gi_: e.max_index(
                    out=i_[:, 8:16], in_max=v_[:, 8:16], in_values=S2[:, gi_ * n:(gi_ + 1) * n]),
                    reads=[b2[gi_], bv[gi_]], writes=[bi[gi_]])
            return bv + bi, b2

        gcount = [0]

        def gather(tab, m):
            k = gcount[0] % R
            gcount[0] += 1
            g, bg = gb[k], bgb[k]
            P.dma("pool", lambda e: e.indirect_dma_start(
                out=g[:, :], out_offset=None, in_=tab[:, :],
                in_offset=bass.IndirectOffsetOnAxis(ap=eidx[:, m:m + 1], axis=0)),
                reads=[beidx], writes=[bg])
            return g, bg

        for gi in range(S // TG):
            t0 = gi * TG
            self.load_h(src, t0, TG, h, bh)
            self.dump("h", h[:], [128, KC, TG], F32, [bh])
            nm()
            self.dump("u", u[:], [128, KC, TG], F32, [bu])
            linq()
            self.dump("qT", qT[:], [128, KC, TG], F32, [bq])
            for blk in range(NBLK):
                c0 = blk * 128
                for g in range(16):
                    bank, bb = self.pb[2 + g // 4], self.pbb[2 + g // 4]
                    P.op("pe", lambda e, g=g, bank=bank: e.matmul(
                        bank[:, (g % 4) * 128:(g % 4 + 1) * 128], lhsT=qT[:, g, c0:c0 + 128], rhs=skT[:, g, :],
                        start=True, stop=True), reads=[bq, bsk], writes=[bb])
                for b4 in range(4):
                    P.op("act", lambda e, b4=b4: e.activation(
                        out=Ssb[:, b4 * 4:(b4 + 1) * 4, :], in_=self.pb[2 + b4][:, :].rearrange("p (g n) -> p g n", g=4),
                        func=AF.Copy), reads=[self.pbb[2 + b4]], writes=[bS])
                P._deps("dve", [], [bm8, bi8])
                tb, tb2 = top16_many([(Ssb[:, g, :], m8[:, g, :], i8[:, g, :]) for g in range(16)], 128, [bS])
                P.op("dve", lambda e: e.tensor_copy(out=i8f[:, 0, 0:1], in_=i8[:, 0, 0:1]), reads=tb + tb2, writes=[bm8, bi8, bS2])
                P.op("dve", lambda e: e.tensor_copy(out=i8f[:], in_=i8[:]), reads=[bi8], writes=[bi8f])
                m8v = m8[:].rearrange("p (h two) k -> p h two k", two=2)
                P.op("dve", lambda e: e.tensor_tensor(
                    out=cand[:].rearrange("p h (a b) -> p h a b", a=16),
                    in0=m8v[:, :, 0, :].unsqueeze(3).to_broadcast([128, 8, 16, 16]),
                    in1=m8v[:, :, 1, :].unsqueeze(2).to_broadcast([128, 8, 16, 16]), op=ALU.add),
                    reads=[bm8], writes=[bcand])
                P._deps("dve", [], [bbs, bbp])
                tb, tb2 = top16_many([(cand[:, hh, :], bs[:, hh, :], bp[:, hh, :]) for hh in range(8)], 256, [bcand])
                P.op("dve", lambda e: e.tensor_copy(out=posf[:, 0:1], in_=bp[:, 0, 0:1]), reads=tb + tb2, writes=[bbs, bbp, bS2])
                P.op("dve", lambda e: e.tensor_copy(out=posf[:], in_=bp[:].rearrange("p h k -> p (h k)")),
                     reads=[bbp], writes=[bposf])
                P.op("dve", lambda e: e.tensor_scalar(out=af[:], in0=posf[:], scalar1=0.0625, scalar2=0.53125,
                                                      op0=ALU.mult, op1=ALU.add), reads=[bposf], writes=[baf])
                P.op("dve", lambda e: e.tensor_scalar(out=af[:], in0=af[:], scalar1=8388608.0, scalar2=None,
                                                      op0=ALU.add), reads=[baf], writes=[baf])
                P.op("dve", lambda e: e.tensor_scalar(out=af[:], in0=af[:], scalar1=-8388609.0, scalar2=None,
                                                      op0=ALU.add), reads=[baf], writes=[baf])
                P.op("dve", lambda e: e.scalar_tensor_tensor(out=bf[:], in0=af[:], scalar=-16.0, in1=posf[:],
                                                             op0=ALU.mult, op1=ALU.add),
                     reads=[baf, bposf], writes=[bbf])
                i8v = i8f[:].rearrange("p (h two) k -> p h two k", two=2)
                for which, sel_idx, dst_t, bdst in ((0, af, isel, bisel), (1, bf, jsel, bjsel)):
                    bsel = baf if which == 0 else bbf
                    P.op("dve", lambda e, sel_idx=sel_idx: e.tensor_tensor(
                        out=oh[:], in0=io16[:], in1=sel_idx[:].unsqueeze(2).to_broadcast([128, 128, 16]),
                        op=ALU.is_equal), reads=[bio, bsel], writes=[boh])
                    P.op("dve", lambda e, which=which: e.tensor_tensor(
                        out=oh[:].rearrange("p (h k) a -> p h k a", h=8),
                        in0=oh[:].rearrange("p (h k) a -> p h k a", h=8),
                        in1=i8v[:, :, which, :].unsqueeze(2).to_broadcast([128, 8, 16, 16]), op=ALU.mult),
                        reads=[boh, bi8f], writes=[boh])
                    P.op("dve", lambda e, dst_t=dst_t: e.tensor_reduce(out=dst_t[:], in_=oh[:], axis=AX.X, op=ALU.add),
                         reads=[boh], writes=[bdst])
                P.op("dve", lambda e: e.scalar_tensor_tensor(out=af[:], in0=isel[:], scalar=128.0, in1=jsel[:],
                                                             op0=ALU.mult, op1=ALU.add),
                     reads=[bisel, bjsel, baf], writes=[baf])
                P.op("dve", lambda e: e.tensor_copy(out=eidx[:], in_=af[:]), reads=[baf], writes=[beidx])
                self.dump("eidx", eidx[:], [128, 128], I32, [beidx])
                self.dump("m8", m8[:], [128, 16, 16], F32, [bm8])
                self.dump("i8f", i8f[:], [128, 16, 16], F32, [bi8f])
                self.dump("bs", bs[:], [128, 8, 16], F32, [bbs])
                self.dump("posf", posf[:], [128, 128], F32, [bposf])
                self.dump("S", Ssb[:], [128, 16, 128], F32, [bS])
                self.dump("isel", isel[:], [128, 128], F32, [bisel])
                self.dump("jsel", jsel[:], [128, 128], F32, [bjsel])
                P.op("dve", lambda e: e.tensor_tensor(out=gate[:], in0=bs[:], in1=bs[:, :, 0:1].to_broadcast([128, 8, 16]),
                                                      op=ALU.subtract), reads=[bbs], writes=[bgate])
                P.op("act", lambda e: e.activation(out=gate[:], in_=gate[:], func=AF.Exp), reads=[bgate], writes=[bgate])
                P.op("dve", lambda e: e.tensor_reduce(out=gsum[:], in_=gate[:], axis=AX.X, op=ALU.add),
                     reads=[bgate], writes=[bgsum])
                P.op("dve", lambda e: e.reciprocal(out=gsum[:], in_=gsum[:]), reads=[bgsum], writes=[bgsum])
                P.op("dve", lambda e: e.tensor_tensor(out=gate[:], in0=gate[:],
                                                      in1=gsum[:].unsqueeze(2).to_broadcast([128, 8, 16]), op=ALU.mult),
                     reads=[bgate, bgsum], writes=[bgate])
                for kc in range(KC):
                    bank, bb = self.pb[2 + kc // 4], self.pbb[2 + kc // 4]
                    P.op("pe", lambda e, kc=kc, bank=bank: e.transpose(
                        out=bank[:, (kc % 4) * 128:(kc % 4 + 1) * 128], in_=u[:, kc, c0:c0 + 128], identity=self.ident[:]),
                        reads=[bu, self.b_const], writes=[bb])
                for b4 in range(4):
                    P.op("act", lambda e, b4=b4: e.activation(out=utok[:, b4 * 512:(b4 + 1) * 512],
                                                              in_=self.pb[2 + b4][:, :], func=AF.Copy),
                         reads=[self.pbb[2 + b4]], writes=[butok])
                for m in range(128):
                    g, bg = gather(utab, m)
                    P.op("dve", lambda e, g=g, m=m: e.scalar_tensor_tensor(
                        out=scr[:], in0=utok[:], scalar=1.0, in1=g[:], op0=ALU.mult, op1=ALU.mult,
                        accum_out=av[:, m:m + 1]), reads=[butok, bg], writes=[bscr, bav])
                P.op("dve", lambda e: e.tensor_tensor(out=t1[:], in0=av[:], in1=av[:], op=ALU.mult), reads=[bav], writes=[bt1])
                P.op("dve", lambda e: e.tensor_scalar(out=t1[:], in0=t1[:], scalar1=0.044715, scalar2=1.0,
                                                      op0=ALU.mult, op1=ALU.add), reads=[bt1], writes=[bt1])
                P.op("dve", lambda e: e.tensor_tensor(out=t1[:], in0=t1[:], in1=av[:], op=ALU.mult),
                     reads=[bt1, bav], writes=[bt1])
                P.op("act", lambda e: e.activation(out=t1[:], in_=t1[:], func=AF.Tanh, scale=0.7978845608028654),
                     reads=[bt1], writes=[bt1])
                P.op("dve", lambda e: e.scalar_tensor_tensor(out=t1[:], in0=t1[:], scalar=1.0, in1=av[:],
                                                             op0=ALU.add, op1=ALU.mult), reads=[bt1, bav], writes=[bt1])
                P.op("dve", lambda e: e.scalar_tensor_tensor(out=wv[:], in0=t1[:], scalar=0.5,
                                                             in1=gate[:].rearrange("p h k -> p (h k)"),
                                                             op0=ALU.mult, op1=ALU.mult),
                     reads=[bt1, bgate], writes=[bwv])
                self.dump("a", av[:], [128, 128], F32, [bav])
                self.dump("w", wv[:], [128, 128], F32, [bwv])
                self.dump("gate", gate[:], [128, 8, 16], F32, [bgate])
                self.dump("utok", utok[:], [128, D], F32, [butok])
                for m in range(128):
                    g, bg = gather(vtab, m)
                    dgt, bdgt = dg[m % 4], bdg[m % 4]
                    P.op("act", lambda e, dgt=dgt, m=m: e.activation(out=dgt[:], in_=identb[:], func=AF.Copy,
                                                                     scale=wv[:, m:m + 1]),
                         reads=[bwv, bidb], writes=[bdgt])
                    for c4 in range(4):
                        P.op("pe", lambda e, dgt=dgt, g=g, c4=c4, m=m: e.matmul(
                            self.pb[2 + c4][:, :], lhsT=dgt[:], rhs=g[:, c4 * 512:(c4 + 1) * 512],
                            start=(m == 0), stop=(m == 127)), reads=[bdgt, bg], writes=[self.pbb[2 + c4]])
                for c4 in range(4):
                    P.op("act", lambda e, c4=c4: e.activation(out=acc[:, c4 * 512:(c4 + 1) * 512], in_=self.pb[2 + c4][:, :],
                                                              func=AF.Copy), reads=[self.pbb[2 + c4]], writes=[bacc])
                self.dump("acc", acc[:], [128, D], F32, [bacc])
                for kc in range(KC):
                    bank, bb = self.pb[2 + kc // 4], self.pbb[2 + kc // 4]
                    P.op("pe", lambda e, kc=kc, bank=bank: e.transpose(
                        out=bank[:, (kc % 4) * 128:(kc % 4 + 1) * 128], in_=acc[:, kc * 128:(kc + 1) * 128],
                        identity=self.ident[:]), reads=[bacc, self.b_const], writes=[bb])
                for kc in range(KC):
                    bank, bb = self.pb[2 + kc // 4], self.pbb[2 + kc // 4]
                    P.op("dve", lambda e, kc=kc, bank=bank: e.scalar_tensor_tensor(
                        out=h[:, kc, c0:c0 + 128], in0=bank[:, (kc % 4) * 128:(kc % 4 + 1) * 128],
                        scalar=self.mod(li, 5, kc), in1=h[:, kc, c0:c0 + 128], op0=ALU.mult, op1=ALU.add),
                        reads=[bb, self.b_mods, bh], writes=[bh])
            self.store_h(dst, t0, TG, h, bh)
        P.barrier()


Model._stage_peer = _peer_stage


class WS:
    def __init__(self, model, es, KCin, tag, nbuf=2):
        nc = model.nc
        self.m = model
        self.n = nbuf
        self.KCin = KCin
        self.wst = [es.enter_context(nc.sbuf_tensor("%s_ws%d" % (tag, i), [128, KCin, 128], F32)) for i in range(nbuf)]
        self.bws = [Buf() for _ in range(nbuf)]
        self.wbf = [es.enter_context(nc.sbuf_tensor("%s_wb%d" % (tag, i), [128, KCin, 128], BF16)) for i in range(nbuf)]
        self.bwb = [Buf() for _ in range(nbuf)]
        self.i = 0
        self.pre = {}

    def preconvert(self, wd, ntiles, kcs, tag):
        P = self.m.P
        nc = self.m.nc
        wdb = nc.dram_tensor("wb_" + tag, [ntiles, 128, kcs, 128], BF16)
        bdst = Buf()
        for t in range(ntiles):
            k = t % self.n
            w, bw, wm, bm = self.wst[k], self.bws[k], self.wbf[k], self.bwb[k]
            P.dma("sp", lambda e, w=w, t=t: e.dma_start(out=w[:, :kcs, :], in_=wd.ap()[t]), writes=[bw])
            if t % 2 == 0:
                P.op("act", lambda e, w=w, wm=wm: e.activation(out=wm[:, :kcs, :], in_=w[:, :kcs, :], func=AF.Copy),
                     reads=[bw], writes=[bm])
            else:
                P.op("dve", lambda e, w=w, wm=wm: e.tensor_copy(out=wm[:, :kcs, :], in_=w[:, :kcs, :]),
                     reads=[bw], writes=[bm])
            P.dma("sp", lambda e, wm=wm, t=t: e.dma_start(out=wdb.ap()[t], in_=wm[:, :kcs, :]), reads=[bm], writes=[bdst])
        self.pre[id(wd)] = (wdb, bdst)

    def get(self, wd, mc, f32=False, kcs=None):
        P = self.m.P
        k = self.i % self.n
        self.i += 1
        w, bw = self.wst[k], self.bws[k]
        kk = self.KCin if kcs is None else kcs
        if id(wd) in self.pre and not f32:
            wdb, bdst = self.pre[id(wd)]
            wm, bm = self.wbf[k], self.bwb[k]
            P.dma("sp", lambda e: e.dma_start(out=wm[:, :kk, :], in_=wdb.ap()[mc]), reads=[bdst], writes=[bm])
            return wm, bm
        P.dma("sp", lambda e: e.dma_start(out=w[:, :kk, :], in_=wd.ap()[mc]), writes=[bw])
        if f32:
            return w, bw
        wm, bm = self.wbf[k], self.bwb[k]
        if self.i % 4 == 0:
            P.op("act", lambda e: e.activation(out=wm[:, :kk, :], in_=w[:, :kk, :], func=AF.Copy), reads=[bw], writes=[bm])
        else:
            P.op("pool", lambda e: e.tensor_copy(out=wm[:, :kk, :], in_=w[:, :kk, :]), reads=[bw], writes=[bm])
        return wm, bm


def _lin(self, ws, wd, mc, kcs, rhs_of, rd, out_ap, bout, f32=False):
    P = self.P
    wm, bm = ws.get(wd, mc, f32=f32, kcs=kcs)
    for kc in range(kcs):
        P.op("pe", lambda e, kc=kc: e.matmul(out_ap, lhsT=wm[:, kc, :], rhs=rhs_of(kc),
                                             start=(kc == 0), stop=(kc == kcs - 1)),
             reads=[bm] + list(rd), writes=[bout])


Model.lin = _lin


def _hgrn_stage(self, li, src, dst):
    P = self.P
    nc = self.nc
    S = self.S
    TG = 256
    NCH = TG // 64
    H = 16
    wqfg = self.din["hg_w_qfg"]
    wi = self.din["hg_w_i"]
    wo = self.din["hg_w_out"]
    with ExitStack() as es:
        sbt = lambda n, s, d: es.enter_context(nc.sbuf_tensor("hg_" + n, list(s), d))
        h = sbt("h", [128, KC, TG], F32); bh = Buf()
        u = sbt("u", [128, KC, TG], BF16); bu = Buf()
        vtok = sbt("vtok", [64, NCH, D], BF16); bvt = [[Buf() for _ in range(H)] for _ in range(NCH)]
        qt = sbt("qt", [128, H, TG], BF16); bqt = [Buf() for _ in range(H)]
        ktb = sbt("ktb", [128, H, TG], BF16); bktb = [Buf() for _ in range(H)]
        ktok = sbt("ktok", [64, NCH, H, 128], BF16); bktok = [[Buf() for _ in range(H)] for _ in range(NCH)]
        gs = sbt("gs", [128, H, TG], BF16); bgs = [Buf() for _ in range(H)]
        oT = sbt("oT", [128, H, TG], F32); boT = [Buf() for _ in range(H)]
        ob = sbt("ob", [128, H, TG], BF16); bob = Buf()
        St = sbt("S", [128, H, 128], F32); bS = [Buf() for _ in range(H)]
        Sb = sbt("Sb", [128, H, 128], BF16); bSb = [Buf() for _ in range(H)]
        elast = sbt("elast", [128, H, NCH], F32); bel = [Buf() for _ in range(H)]
        lb = sbt("lb", [128, H], F32); blb = Buf()
        omlb = sbt("omlb", [128, H], F32)
        ex = sbt("ex", [128, 4, H], F32)
        tf = [sbt("tf%d" % i, [128, TG], F32) for i in range(2)]; btf = [Buf(), Buf()]
        logf = [sbt("logf%d" % i, [128, TG], F32) for i in range(2)]; blogf = [Buf(), Buf()]
        kk = [sbt("kk%d" % i, [128, TG], F32) for i in range(2)]; bkk = [Buf(), Buf()]
        bb_ = [sbt("b%d" % i, [128, TG], F32) for i in range(2)]; bbb = [Buf(), Buf()]
        eb = [sbt("eb%d" % i, [128, TG], F32) for i in range(2)]; beb = [Buf(), Buf()]
        ktf = [sbt("ktf%d" % i, [128, TG], F32) for i in range(2)]; bktf = [Buf(), Buf()]
        att = [sbt("att%d" % i, [64, 64], BF16) for i in range(4)]; batt = [Buf() for _ in range(4)]
        tri = sbt("tri", [64, 64], F32); btri = Buf()
        rs = sbt("rs", [128, TG], F32); brs = Buf()
        A, bA = self.make_A(es, li, 0)
        ws = WS(self, es, KC, "hgw")
        ws.preconvert(wqfg, 48, KC, "hg_qfg")
        ws.preconvert(wi, 16, KC, "hg_i")
        ws.preconvert(wo, 16, KC, "hg_o")
        nm = self.norm_mod(es, h, bh, TG, A, bA, li, 0, u, bu, "hg")
        P.dma("sp", lambda e: e.dma_start(out=tri[:], in_=self.din["tri"].ap()[0:64, 0:64]), writes=[btri])
        lg = self.v("hg_lb", 0, 64).rearrange("p (l k) -> p l k", l=4)
        P.op("act", lambda e: e.activation(out=ex[:], in_=lg, func=AF.Exp), reads=[self.b_vec], writes=[blb])
        P.op("dve", lambda e: e.tensor_tensor(out=omlb[:], in0=ex[:, 0, :], in1=ex[:, 1, :], op=ALU.add), reads=[blb], writes=[blb])
        P.op("dve", lambda e: e.tensor_tensor(out=omlb[:], in0=omlb[:], in1=ex[:, 2, :], op=ALU.add), reads=[blb], writes=[blb])
        P.op("dve", lambda e: e.tensor_tensor(out=omlb[:], in0=omlb[:], in1=ex[:, 3, :], op=ALU.add), reads=[blb], writes=[blb])
        P.op("dve", lambda e: e.reciprocal(out=omlb[:], in_=omlb[:]), reads=[blb], writes=[blb])
        P.op("dve", lambda e: e.tensor_copy(out=lb[:], in_=ex[:, 1, :]), reads=[blb], writes=[blb])
        for l in range(2, li + 1):
            P.op("dve", lambda e, l=l: e.tensor_tensor(out=lb[:], in0=lb[:], in1=ex[:, l, :], op=ALU.add), reads=[blb], writes=[blb])
        P.op("dve", lambda e: e.tensor_tensor(out=lb[:], in0=lb[:], in1=omlb[:], op=ALU.mult), reads=[blb], writes=[blb])
        P.op("dve", lambda e: e.tensor_scalar(out=omlb[:], in0=lb[:], scalar1=-1.0, scalar2=1.0, op0=ALU.mult, op1=ALU.add),
             reads=[blb], writes=[blb])
        P.op("pool", lambda e: e.memset(St[:], 0.0), writes=bS)
        P.op("pool", lambda e: e.memset(Sb[:], 0.0), writes=bSb)
        SCALE = 128.0 ** -0.5
        lrot = [0]

        for gi in range(S // TG):
            t0 = gi * TG
            self.load_h(src, t0, TG, h, bh)
            nm()
            for hh in range(H):
                k2 = hh % 2
                bk = lrot[0] % 2; lrot[0] += 1
                bank, bbk = self.pb[bk], self.pbb[bk]
                self.lin(ws, wqfg, 16 + hh, KC, lambda kc: u[:, kc, :], [bu], bank[:, :TG], bbk)
                t, bt = tf[k2], btf[k2]
                P.op("act", lambda e, t=t, bank=bank: e.activation(out=t[:], in_=bank[:, :TG], func=AF.Sigmoid),
                     reads=[bbk], writes=[bt])
                P.op("dve", lambda e, t=t, hh=hh: e.tensor_scalar(out=t[:], in0=t[:], scalar1=omlb[:, hh:hh + 1],
                                                                   scalar2=lb[:, hh:hh + 1], op0=ALU.mult, op1=ALU.add),
                     reads=[bt, blb], writes=[bt])
                lf, blf = logf[k2], blogf[k2]
                P.op("act", lambda e, t=t, lf=lf: e.activation(out=lf[:], in_=t[:], func=AF.Ln), reads=[bt], writes=[blf])
                kx, bkx = kk[k2], bkk[k2]
                P.op("dve", lambda e, t=t, kx=kx: e.tensor_scalar(out=kx[:], in0=t[:], scalar1=-1.0, scalar2=1.0,
                                                                  op0=ALU.mult, op1=ALU.add), reads=[bt], writes=[bkx])
                bc, bbc = bb_[k2], bbb[k2]
                for c in range(NCH):
                    P.op("dve", lambda e, c=c, bc=bc, lf=lf: e.tensor_tensor_scan(
                        out=bc[:, c * 64:(c + 1) * 64], data0=self.ones[:, 0:64], data1=lf[:, c * 64:(c + 1) * 64],
                        initial=0.0, op0=ALU.mult, op1=ALU.add), reads=[blf, self.b_const], writes=[bbc])
                e1, be1 = eb[k2], beb[k2]
                P.op("act", lambda e, e1=e1, bc=bc: e.activation(out=e1[:], in_=bc[:], func=AF.Exp), reads=[bbc], writes=[be1])
                P.op("dve", lambda e, e1=e1, hh=hh: e.tensor_copy(
                    out=elast[:, hh, :], in_=e1[:].rearrange("p (c t) -> p c t", t=64)[:, :, 63]),
                    reads=[be1], writes=[bel[hh]])
                kf, bkf = ktf[k2], bktf[k2]
                P.op("act", lambda e, kf=kf, bc=bc: e.activation(out=kf[:], in_=bc[:], func=AF.Exp, scale=-1.0),
                     reads=[bbc], writes=[bkf])
                P.op("dve", lambda e, kf=kf, kx=kx: e.tensor_tensor(out=kf[:], in0=kf[:], in1=kx[:], op=ALU.mult),
                     reads=[bkf, bkx], writes=[bkf])
                P.op("act", lambda e, kf=kf, hh=hh: e.activation(out=ktb[:, hh, :], in_=kf[:], func=AF.Copy),
                     reads=[bkf], writes=[bktb[hh]])
                for c in range(NCH):
                    sl = (hh * NCH + c) % 4
                    P.op("pe", lambda e, c=c, kf=kf, sl=sl: e.transpose(
                        out=self.pb[6][0:64, sl * 128:(sl + 1) * 128], in_=kf[:, c * 64:(c + 1) * 64], identity=self.ident[:]),
                        reads=[bkf, self.b_const], writes=[self.pbb[6]])
                    P.op("act", lambda e, c=c, hh=hh, sl=sl: e.activation(
                        out=ktok[0:64, c, hh, :], in_=self.pb[6][0:64, sl * 128:(sl + 1) * 128], func=AF.Copy),
                        reads=[self.pbb[6]], writes=[bktok[c][hh]])
                bk = lrot[0] % 2; lrot[0] += 1
                bank, bbk = self.pb[bk], self.pbb[bk]
                self.lin(ws, wqfg, hh, KC, lambda kc: u[:, kc, :], [bu], bank[:, :TG], bbk)
                P.op("dve", lambda e, hh=hh, bank=bank, e1=e1: e.scalar_tensor_tensor(
                    out=qt[:, hh, :], in0=bank[:, :TG], scalar=SCALE, in1=e1[:], op0=ALU.mult, op1=ALU.mult),
                    reads=[bbk, be1], writes=[bqt[hh]])
                bk = lrot[0] % 2; lrot[0] += 1
                bank, bbk = self.pb[bk], self.pbb[bk]
                self.lin(ws, wqfg, 32 + hh, KC, lambda kc: u[:, kc, :], [bu], bank[:, :TG], bbk)
                P.op("act", lambda e, hh=hh, bank=bank: e.activation(out=gs[:, hh, :], in_=bank[:, :TG], func=AF.Silu),
                     reads=[bbk], writes=[bgs[hh]])
                wm, bm = ws.get(wi, hh)
                for c in range(NCH):
                    bk2 = 2 + (hh * NCH + c) % 2
                    bank, bbk = self.pb[bk2], self.pbb[bk2]
                    for kc in range(KC):
                        P.op("pe", lambda e, kc=kc, c=c, bank=bank: e.matmul(
                            bank[0:64, 0:128], lhsT=u[:, kc, c * 64:(c + 1) * 64], rhs=wm[:, kc, :],
                            start=(kc == 0), stop=(kc == KC - 1)), reads=[bu, bm], writes=[bbk])
                    P.op("act", lambda e, c=c, hh=hh, bank=bank: e.activation(
                        out=vtok[0:64, c, hh * 128:(hh + 1) * 128], in_=bank[0:64, 0:128], func=AF.Copy),
                        reads=[bbk], writes=[bvt[c][hh]])
            import os
            HG_STOP = int(os.environ.get("HG_STOP", "9"))
            for c in range(NCH if HG_STOP >= 2 else 0):
                cs = slice(c * 64, (c + 1) * 64)
                for hh in range(H):
                    sa = so = ss = 0
                    pA, pO, pS = self.pb[hh % 2], self.pb[2 + hh % 2], self.pb[4 + hh % 2]
                    bpA = [self.pbb[hh % 2]]; bpO = [self.pbb[2 + hh % 2]]; bpS = [self.pbb[4 + hh % 2]]
                    at, bat = att[hh % 4], batt[hh % 4]
                    P.op("pe", lambda e, hh=hh, sa=sa, pA=pA: e.matmul(
                        pA[0:64, sa * 64:(sa + 1) * 64], lhsT=ktb[:, hh, cs], rhs=qt[:, hh, cs], start=True, stop=True),
                        reads=[bktb[hh], bqt[hh]], writes=[bpA[sa]])
                    P.op("dve", lambda e, at=at, sa=sa, pA=pA: e.tensor_tensor(
                        out=at[:], in0=pA[0:64, sa * 64:(sa + 1) * 64], in1=tri[:], op=ALU.mult),
                        reads=[bpA[sa], btri], writes=[bat])
                    P.op("pe", lambda e, hh=hh, so=so, at=at, pO=pO: e.matmul(
                        pO[:, so * 64:(so + 1) * 64], lhsT=vtok[0:64, c, hh * 128:(hh + 1) * 128], rhs=at[:],
                        start=True, stop=False), reads=[bvt[c][hh], bat], writes=[bpO[so]])
                    P.op("pe", lambda e, hh=hh, so=so, pO=pO: e.matmul(
                        pO[:, so * 64:(so + 1) * 64], lhsT=Sb[:, hh, :], rhs=qt[:, hh, cs],
                        start=False, stop=True), reads=[bSb[hh], bqt[hh]], writes=[bpO[so]])
                    P.op("act", lambda e, hh=hh, so=so, pO=pO: e.activation(
                        out=oT[:, hh, cs], in_=pO[:, so * 64:(so + 1) * 64], func=AF.Copy),
                        reads=[bpO[so]], writes=[boT[hh]])
                    P.op("pe", lambda e, hh=hh, ss=ss, pS=pS: e.matmul(
                        pS[:, ss * 128:(ss + 1) * 128], lhsT=ktok[0:64, c, hh, :], rhs=vtok[0:64, c, hh * 128:(hh + 1) * 128],
                        start=True, stop=True), reads=[bktok[c][hh], bvt[c][hh]], writes=[bpS[ss]])
                    P.op("dve", lambda e, hh=hh, ss=ss, pS=pS: e.tensor_tensor(
                        out=St[:, hh, :], in0=pS[:, ss * 128:(ss + 1) * 128], in1=St[:, hh, :], op=ALU.add),
                        reads=[bpS[ss], bS[hh]], writes=[bS[hh]])
                    P.op("dve", lambda e, hh=hh: e.tensor_scalar(
                        out=St[:, hh, :], in0=St[:, hh, :], scalar1=elast[:, hh, c:c + 1], scalar2=None, op0=ALU.mult),
                        reads=[bS[hh], bel[hh]], writes=[bS[hh]])
                    P.op("act", lambda e, hh=hh: e.activation(out=Sb[:, hh, :], in_=St[:, hh, :], func=AF.Copy),
                         reads=[bS[hh]], writes=[bSb[hh]])
            for hh in range(H if HG_STOP >= 3 else 0):
                t, bt = tf[hh % 2], btf[hh % 2]
                P.op("act", lambda e, t=t, hh=hh: e.activation(out=t[:], in_=oT[:, hh, :], func=AF.Square),
                     reads=[boT[hh]], writes=[bt])
                bank, bbk = self.pb[7], self.pbb[7]
                P.op("pe", lambda e, t=t: e.matmul(bank[:, :TG], lhsT=self.ones[:], rhs=t[:], start=True, stop=True),
                     reads=[bt, self.b_const], writes=[bbk])
                P.op("act", lambda e: e.activation(out=rs[:], in_=bank[:, :TG], func=AF.Sqrt, bias=self.epsb[:],
                                                   scale=1.0 / 128.0), reads=[bbk, self.b_const], writes=[brs])
                P.op("dve", lambda e: e.reciprocal(out=rs[:], in_=rs[:]), reads=[brs], writes=[brs])
                P.op("dve", lambda e, t=t, hh=hh: e.tensor_tensor(out=t[:], in0=oT[:, hh, :], in1=rs[:], op=ALU.mult),
                     reads=[boT[hh], brs, bt], writes=[bt])
                P.op("dve", lambda e, t=t, hh=hh: e.scalar_tensor_tensor(
                    out=ob[:, hh, :], in0=t[:], scalar=self.v("hg_norm_g", 0), in1=gs[:, hh, :], op0=ALU.mult, op1=ALU.mult),
                    reads=[bt, self.b_vec, bgs[hh]], writes=[bob])
            for mc in range(KC if HG_STOP >= 4 else 0):
                bk = lrot[0] % 2; lrot[0] += 1
                bank, bbk = self.pb[bk], self.pbb[bk]
                self.lin(ws, wo, mc, KC, lambda kc: ob[:, kc, :], [bob], bank[:, :TG], bbk)
                P.op("dve", lambda e, mc=mc, bank=bank: e.scalar_tensor_tensor(
                    out=h[:, mc, :], in0=bank[:, :TG], scalar=self.mod(li, 2, mc), in1=h[:, mc, :],
                    op0=ALU.mult, op1=ALU.add), reads=[bbk, self.b_mods, bh], writes=[bh])
            self.store_h(dst, t0, TG, h, bh)
        P.barrier()


Model._stage_hgrn = _hgrn_stage


def _declare_mla(self):
    S = self.S
    nc = self.nc
    self.inp("mla_w_in", [9, 128, KC, 128])
    self.inp("mla_w_uq", [32, 128, 4, 128])
    self.inp("mla_w_ukv", [32, 128, 4, 128])
    self.inp("mla_w_o", [16, 128, KC, 128])
    self.inp("pswap", [64, 64])
    self.inp("dmask", [128, 128])
    self.inp("rsgn", [128, 1])
    self.QN = nc.dram_tensor("mla_QN", [16, 128, S], BF16)
    self.QR = nc.dram_tensor("mla_QR", [16, 64, S], BF16)
    self.KN = nc.dram_tensor("mla_KN", [16, 128, S], BF16)
    self.KR = nc.dram_tensor("mla_KR", [64, S], BF16)
    self.VT = nc.dram_tensor("mla_VT", [16, S, 128], BF16)
    self.OT = nc.dram_tensor("mla_OT", [128, 16, S], BF16)


Model.declare_mla = _declare_mla

TWO_PI_HI = 6.28125
TWO_PI_LO = 0.0019353071795864769


def _mla_stage(self, li, src, dst):
    P = self.P
    nc = self.nc
    S = self.S
    TG = 512
    H = 16
    NG = S // TG
    w_in, w_uq, w_ukv, w_o = (self.din[k] for k in ("mla_w_in", "mla_w_uq", "mla_w_ukv", "mla_w_o"))
    bscr = Buf()
    SCALE = 192.0 ** -0.5
    with ExitStack() as es:
        sbt = lambda n, s, d: es.enter_context(nc.sbuf_tensor("m1_" + n, list(s), d))
        h = sbt("h", [128, KC, TG], F32); bh = Buf()
        u = sbt("u", [128, KC, TG], BF16); bu = Buf()
        cq = sbt("cq", [128, 4, TG], F32); bcq = Buf()
        ckv = sbt("ckv", [128, 4, TG], F32); bckv = Buf()
        cqn = sbt("cqn", [128, 4, TG], BF16); bcqn = Buf()
        ckvn = sbt("ckvn", [128, 4, TG], BF16); bckvn = Buf()
        kr = sbt("kr", [64, TG], F32); bkr = Buf()
        posi = sbt("posi", [64, TG], I32); bpos = Buf()
        ang = sbt("ang", [64, TG], F32); bang = Buf()
        nn = sbt("nn", [64, TG], F32); bnn = Buf()
        cosT = sbt("cos", [64, TG], F32); bcos = Buf()
        sinT = sbt("sin", [64, TG], F32); bsin = Buf()
        psw = sbt("psw", [64, 64], F32); bpsw = Buf()
        sgn = sbt("sgn", [128, 1], F32)
        rt = [sbt("rt%d" % i, [64, TG], F32) for i in range(2)]; brt = [Buf(), Buf()]
        qrf = [sbt("qrf%d" % i, [64, TG], F32) for i in range(2)]; bqrf = [Buf(), Buf()]
        ob16 = [sbt("ob%d" % i, [128, TG], BF16) for i in range(4)]; bob16 = [Buf() for _ in range(4)]
        vt = [sbt("vt%d" % i, [128, 4, 128], BF16) for i in range(2)]; bvt = [Buf(), Buf()]
        sq = [sbt("sq%d" % i, [128, TG], F32) for i in range(2)]; bsq = [Buf(), Buf()]
        rr = sbt("rr", [128, TG], F32); brr = Buf()
        A, bA = self.make_A(es, li, 0)
        ws = WS(self, es, KC, "m1w")
        ws.preconvert(w_in, 9, KC, "m_in")
        ws.preconvert(w_uq, 32, 4, "m_uq")
        ws.preconvert(w_ukv, 32, 4, "m_ukv")
        nm = self.norm_mod(es, h, bh, TG, A, bA, li, 0, u, bu, "m1n")
        P.dma("sp", lambda e: e.dma_start(out=psw[:], in_=self.din["pswap"].ap()), writes=[bpsw])
        P.dma("sp", lambda e: e.dma_start(out=sgn[:], in_=self.din["rsgn"].ap()), writes=[bpsw])
        rot = [0]
        orot = [0]

        def nbank():
            b = rot[0] % 4
            rot[0] += 1
            return self.pb[b], self.pbb[b]

        def nob():
            k = orot[0] % 4
            orot[0] += 1
            return ob16[k], bob16[k]

        def sincos(dst_t, bdst, phase, sign_scale):
            P.op("dve", lambda e: e.tensor_scalar(out=nn[:], in0=ang[:], scalar1=phase, scalar2=1.0 / (2 * math.pi),
                                                  op0=ALU.add, op1=ALU.mult), reads=[bang], writes=[bnn])
            P.op("dve", lambda e: e.tensor_scalar(out=nn[:], in0=nn[:], scalar1=8388608.0, scalar2=None, op0=ALU.add),
                 reads=[bnn], writes=[bnn])
            P.op("dve", lambda e: e.tensor_scalar(out=nn[:], in0=nn[:], scalar1=-8388608.0, scalar2=None, op0=ALU.add),
                 reads=[bnn], writes=[bnn])
            P.op("dve", lambda e: e.scalar_tensor_tensor(out=dst_t[:], in0=nn[:], scalar=-TWO_PI_HI, in1=ang[:],
                                                         op0=ALU.mult, op1=ALU.add), reads=[bnn, bang], writes=[bdst])
            P.op("dve", lambda e: e.scalar_tensor_tensor(out=dst_t[:], in0=nn[:], scalar=-TWO_PI_LO, in1=dst_t[:],
                                                         op0=ALU.mult, op1=ALU.add), reads=[bnn, bdst], writes=[bdst])
            P.op("dve", lambda e: e.tensor_scalar(out=dst_t[:], in0=dst_t[:], scalar1=phase, scalar2=math.pi,
                                                  op0=ALU.add, op1=ALU.min), reads=[bdst], writes=[bdst])
            P.op("dve", lambda e: e.tensor_scalar(out=dst_t[:], in0=dst_t[:], scalar1=-math.pi, scalar2=None,
                                                  op0=ALU.max), reads=[bdst], writes=[bdst])
            if sign_scale:
                P.op("act", lambda e: e.activation(out=dst_t[:], in_=dst_t[:], func=AF.Sin, scale=sgn[0:64, :]),
                     reads=[bdst, bpsw], writes=[bdst])
            else:
                P.op("act", lambda e: e.activation(out=dst_t[:], in_=dst_t[:], func=AF.Sin), reads=[bdst], writes=[bdst])

        def rope(x, bx, out_t, bout):
            bank, bb = nbank()
            P.op("pe", lambda e: e.matmul(bank[0:64, :TG], lhsT=psw[:], rhs=x[:], start=True, stop=True),
                 reads=[bpsw, bx], writes=[bb])
            t, bt = rt[rot[0] % 2], brt[rot[0] % 2]
            P.op("dve", lambda e: e.tensor_tensor(out=t[:], in0=bank[0:64, :TG], in1=sinT[:], op=ALU.mult),
                 reads=[bb, bsin], writes=[bt])
            P.op("pool", lambda e: e.tensor_tensor(out=x[:], in0=x[:], in1=cosT[:], op=ALU.mult),
                 reads=[bx, bcos], writes=[bx])
            P.op("dve", lambda e: e.tensor_tensor(out=out_t, in0=t[:], in1=x[:], op=ALU.add),
                 reads=[bt, bx], writes=[bout])

        for gi in range(NG):
            t0 = gi * TG
            self.load_h(src, t0, TG, h, bh)
            nm()
            P.dma("sp", lambda e: e.dma_start(out=posi[:], in_=self.din["pos"].ap()[0:1, t0:t0 + TG].partition_broadcast(64)),
                  writes=[bpos])
            P.op("dve", lambda e: e.tensor_copy(out=ang[:], in_=posi[:]), reads=[bpos], writes=[bang])
            P.op("dve", lambda e: e.tensor_scalar(out=ang[:], in0=ang[:], scalar1=self.vec[0:64, self.VOFF["inv_freq"]:self.VOFF["inv_freq"] + 1],
                                                  scalar2=None, op0=ALU.mult), reads=[bang, self.b_vec], writes=[bang])
            sincos(sinT, bsin, 0.0, True)
            sincos(cosT, bcos, math.pi / 2, False)
            for mc in range(9):
                bank, bb = nbank()
                self.lin(ws, w_in, mc, KC, lambda kc: u[:, kc, :], [bu], bank[:, :TG], bb)
                if mc < 4:
                    P.op("act", lambda e, mc=mc, bank=bank: e.activation(out=cq[:, mc, :], in_=bank[:, :TG], func=AF.Copy),
                         reads=[bb], writes=[bcq])
                elif mc < 8:
                    P.op("act", lambda e, mc=mc, bank=bank: e.activation(out=ckv[:, mc - 4, :], in_=bank[:, :TG], func=AF.Copy),
                         reads=[bb], writes=[bckv])
                else:
                    P.op("act", lambda e, bank=bank: e.activation(out=kr[:], in_=bank[0:64, :TG], func=AF.Copy),
                         reads=[bb], writes=[bkr])
            for (cx, bcx, cn, bcn, gname) in ((cq, bcq, cqn, bcqn, "mla_qg"), (ckv, bckv, ckvn, bckvn, "mla_kvg")):
                bank, bb = self.pb[7], self.pbb[7]
                self.colsum(lambda kc, cx=cx: cx[:, kc, :], 4, TG, bank, bb, sq, bsq, extra_reads=[bcx])
                P.op("act", lambda e: e.activation(out=rr[:], in_=bank[:, :TG], func=AF.Sqrt, bias=self.epsb[:],
                                                   scale=1.0 / 512.0), reads=[bb, self.b_const], writes=[brr])
                P.op("dve", lambda e: e.reciprocal(out=rr[:], in_=rr[:]), reads=[brr], writes=[brr])
                for kc in range(4):
                    P.op("dve", lambda e, kc=kc, cx=cx, cn=cn, gname=gname: e.scalar_tensor_tensor(
                        out=cn[:, kc, :], in0=cx[:, kc, :], scalar=self.v(gname, kc), in1=rr[:], op0=ALU.mult, op1=ALU.mult),
                        reads=[bcx, brr, self.b_vec], writes=[bcn])
            o16, bo16 = nob()
            rope(kr, bkr, o16[0:64, :], bo16)
            P.dma("sp", lambda e, o16=o16: e.dma_start(out=self.KR.ap()[:, t0:t0 + TG], in_=o16[0:64, :]),
                  reads=[bo16], writes=[bscr])
            for hh in range(H):
                bank, bb = nbank()
                self.lin(ws, w_uq, 2 * hh, 4, lambda kc: cqn[:, kc, :], [bcqn], bank[:, :TG], bb)
                o16, bo16 = nob()
                P.op("act", lambda e, o16=o16, bank=bank: e.activation(out=o16[:], in_=bank[:, :TG], func=AF.Copy),
                     reads=[bb], writes=[bo16])
                P.dma("sp", lambda e, o16=o16, hh=hh: e.dma_start(out=self.QN.ap()[hh, :, t0:t0 + TG], in_=o16[:]),
                      reads=[bo16], writes=[bscr])
                bank, bb = nbank()
                self.lin(ws, w_uq, 2 * hh + 1, 4, lambda kc: cqn[:, kc, :], [bcqn], bank[:, :TG], bb)
                qf, bqf = qrf[hh % 2], bqrf[hh % 2]
                P.op("act", lambda e, qf=qf, bank=bank: e.activation(out=qf[:], in_=bank[0:64, :TG], func=AF.Copy),
                     reads=[bb], writes=[bqf])
                o16, bo16 = nob()
                rope(qf, bqf, o16[0:64, :], bo16)
                P.dma("sp", lambda e, o16=o16, hh=hh: e.dma_start(out=self.QR.ap()[hh, :, t0:t0 + TG], in_=o16[0:64, :]),
                      reads=[bo16], writes=[bscr])
                bank, bb = nbank()
                self.lin(ws, w_ukv, 2 * hh, 4, lambda kc: ckvn[:, kc, :], [bckvn], bank[:, :TG], bb)
                o16, bo16 = nob()
                P.op("act", lambda e, o16=o16, bank=bank: e.activation(out=o16[:], in_=bank[:, :TG], func=AF.Copy),
                     reads=[bb], writes=[bo16])
                P.dma("sp", lambda e, o16=o16, hh=hh: e.dma_start(out=self.KN.ap()[hh, :, t0:t0 + TG], in_=o16[:]),
                      reads=[bo16], writes=[bscr])
                wm, bm = ws.get(w_ukv, 2 * hh + 1, kcs=4)
                v_, bv_ = vt[hh % 2], bvt[hh % 2]
                bank, bb = nbank()
                for blk in range(4):
                    for kc in range(4):
                        P.op("pe", lambda e, kc=kc, blk=blk, bank=bank: e.matmul(
                            bank[:, blk * 128:(blk + 1) * 128], lhsT=ckvn[:, kc, blk * 128:(blk + 1) * 128], rhs=wm[:, kc, :],
                            start=(kc == 0), stop=(kc == 3)), reads=[bckvn, bm], writes=[bb])
                P.op("act", lambda e, v_=v_, bank=bank: e.activation(
                    out=v_[:], in_=bank[:, :].rearrange("p (b c) -> p b c", b=4), func=AF.Copy), reads=[bb], writes=[bv_])
                P.dma("sp", lambda e, v_=v_, hh=hh: e.dma_start(
                    out=self.VT.ap()[hh, t0:t0 + TG, :].rearrange("(b p) c -> p b c", p=128), in_=v_[:]),
                    reads=[bv_], writes=[bscr])
        P.barrier()
    with ExitStack() as es:
        sbt = lambda n, s, d: es.enter_context(nc.sbuf_tensor("m2_" + n, list(s), d))
        NKB = S // 128
        kn = sbt("kn", [128, S], BF16); bkn = Buf()
        krs = sbt("krs", [64, S], BF16); bkrs = Buf()
        vsb = sbt("vsb", [128, NKB, 130], BF16); bvsb = Buf()
        qn = [sbt("qn%d" % i, [128, TG], BF16) for i in range(2)]; bqn = [Buf(), Buf()]
        qr = [sbt("qr%d" % i, [64, TG], BF16) for i in range(2)]; bqr = [Buf(), Buf()]
        pT = [sbt("pT%d" % i, [128, TG], BF16) for i in range(3)]; bpT = [Buf() for _ in range(3)]
        dmf = sbt("dmf", [128, 128], F32)
        dm = sbt("dm", [128, 128], BF16); bdm = Buf()
        rec = sbt("rec", [128, 4], F32); brec = Buf()
        on = [sbt("on%d" % i, [128, 128], F32) for i in range(2)]; bon = [Buf(), Buf()]
        oT = [sbt("oT%d" % i, [128, TG], BF16) for i in range(2)]; boT = [Buf(), Buf()]
        P.dma("sp", lambda e: e.dma_start(out=dmf[:], in_=self.din["dmask"].ap()), writes=[bdm])
        P.op("dve", lambda e: e.tensor_copy(out=dm[:], in_=dmf[:]), reads=[bdm], writes=[bdm])
        P.op("pool", lambda e: e.memset(vsb[:, :, 128:130], 1.0), writes=[bvsb])
        P.dma("sp", lambda e: e.dma_start(out=krs[:], in_=self.KR.ap()), reads=[bscr], writes=[bkrs])
        pcount = [0]
        for hh in range(H):
            P.dma("sp", lambda e, hh=hh: e.dma_start(out=kn[:], in_=self.KN.ap()[hh]), reads=[bscr], writes=[bkn])
            P.dma("sp", lambda e, hh=hh: e.dma_start(
                out=vsb[:, :, 0:128], in_=self.VT.ap()[hh].rearrange("(b p) c -> p b c", p=128)),
                reads=[bscr], writes=[bvsb])
            for G in range(NG):
                q0 = G * TG
                qi = (hh * NG + G) % 2
                qn_, bqn_, qr_, bqr_ = qn[qi], bqn[qi], qr[qi], bqr[qi]
                P.dma("sp", lambda e, hh=hh, qn_=qn_: e.dma_start(out=qn_[:], in_=self.QN.ap()[hh, :, q0:q0 + TG]),
                      reads=[bscr], writes=[bqn_])
                P.dma("sp", lambda e, hh=hh, qr_=qr_: e.dma_start(out=qr_[:], in_=self.QR.ap()[hh, :, q0:q0 + TG]),
                      reads=[bscr], writes=[bqr_])
                nkb = 4 * (G + 1)

                def qk(kb):
                    j = kb - 4 * G
                    c0 = max(j, 0) * 128
                    bk = pcount[0] % 2
                    p_, bp_ = pT[pcount[0] % 3], bpT[pcount[0] % 3]
                    pcount[0] += 1
                    bank, bb = self.pb[bk], self.pbb[bk]
                    ks = slice(kb * 128, (kb + 1) * 128)
                    P.op("pe", lambda e: e.matmul(bank[:, c0:TG], lhsT=kn[:, ks], rhs=qn_[:, c0:TG], start=True, stop=False),
                         reads=[bkn, bqn_], writes=[bb])
                    P.op("pe", lambda e: e.matmul(bank[:, c0:TG], lhsT=krs[:, ks], rhs=qr_[:, c0:TG], start=False, stop=True),
                         reads=[bkrs, bqr_], writes=[bb])
                    return (kb, j, c0, bank, bb, p_, bp_)

                def pv(ctx):
                    kb, j, c0, bank, bb, p_, bp_ = ctx
                    P.op("act", lambda e: e.activation(out=p_[:, c0:TG], in_=bank[:, c0:TG], func=AF.Exp, scale=SCALE),
                         reads=[bb], writes=[bp_])
                    if j >= 0:
                        P.op("pool", lambda e: e.tensor_tensor(out=p_[:, c0:c0 + 128], in0=p_[:, c0:c0 + 128], in1=dm[:],
                                                               op=ALU.mult), reads=[bp_, bdm], writes=[bp_])
                    for qb in range(max(j, 0), 4):
                        P.op("pe", lambda e, qb=qb: e.matmul(
                            self.pb[2 + qb][:, 0:130], lhsT=p_[:, qb * 128:(qb + 1) * 128], rhs=vsb[:, kb, :],
                            start=(kb == 0), stop=(kb == 4 * G + qb)), reads=[bp_, bvsb], writes=[self.pbb[2 + qb]])

                prev = qk(0)
                for kb in range(1, nkb):
                    cur = qk(kb)
                    pv(prev)
                    prev = cur
                pv(prev)
                o_, bo_ = oT[(hh * NG + G) % 2], boT[(hh * NG + G) % 2]
                for qb in range(4):
                    P.op("dve", lambda e, qb=qb: e.reciprocal(out=rec[:, qb:qb + 1], in_=self.pb[2 + qb][:, 128:129]),
                         reads=[self.pbb[2 + qb]], writes=[brec])
                    n_, bn_ = on[qb % 2], bon[qb % 2]
                    P.op("act", lambda e, qb=qb, n_=n_: e.activation(out=n_[:], in_=self.pb[2 + qb][:, 0:128], func=AF.Copy,
                                                                     scale=rec[:, qb:qb + 1]),
                         reads=[self.pbb[2 + qb], brec], writes=[bn_])
                    P.op("pe", lambda e, n_=n_, qb=qb: e.transpose(out=self.pb[6][:, qb * 128:(qb + 1) * 128], in_=n_[:],
                                                                   identity=self.ident[:]),
                         reads=[bn_, self.b_const], writes=[self.pbb[6]])
                P.op("act", lambda e, o_=o_: e.activation(out=o_[:], in_=self.pb[6][:, :], func=AF.Copy),
                     reads=[self.pbb[6]], writes=[bo_])
                P.dma("sp", lambda e, o_=o_, hh=hh: e.dma_start(out=self.OT.ap()[:, hh, q0:q0 + TG], in_=o_[:]),
                      reads=[bo_], writes=[bscr])
        P.barrier()
    with ExitStack() as es:
        sbt = lambda n, s, d: es.enter_context(nc.sbuf_tensor("m3_" + n, list(s), d))
        h = sbt("h", [128, KC, TG], F32); bh = Buf()
        ob = sbt("ob", [128, KC, TG], BF16); bob = Buf()
        ws = WS(self, es, KC, "m3w")
        ws.preconvert(w_o, 16, KC, "m_o")
        r3 = [0]
        for gi in range(NG):
            t0 = gi * TG
            self.load_h(src, t0, TG, h, bh)
            P.dma("sp", lambda e: e.dma_start(out=ob[:], in_=self.OT.ap()[:, :, t0:t0 + TG]), reads=[bscr], writes=[bob])
            for mc in range(KC):
                bk = r3[0] % 4; r3[0] += 1
                bank, bb = self.pb[bk], self.pbb[bk]
                self.lin(ws, w_o, mc, KC, lambda kc: ob[:, kc, :], [bob], bank[:, :TG], bb)
                P.op("dve", lambda e, mc=mc, bank=bank: e.scalar_tensor_tensor(
                    out=h[:, mc, :], in0=bank[:, :TG], scalar=self.mod(li, 2, mc), in1=h[:, mc, :],
                    op0=ALU.mult, op1=ALU.add), reads=[bb, self.b_mods, bh], writes=[bh])
            self.store_h(dst, t0, TG, h, bh)
        P.barrier()


Model._stage_mla = _mla_stage
```

```python
import math
from contextlib import ExitStack

import numpy as np
import concourse.bass as bass
import concourse.mybir as mybir
from concourse.bass_utils import run_bass_kernel_spmd

F32 = mybir.dt.float32
BF16 = mybir.dt.bfloat16
I32 = mybir.dt.int32
U32 = mybir.dt.uint32
AF = mybir.ActivationFunctionType
ALU = mybir.AluOpType
AX = mybir.AxisListType

D = 2048
KC = 16
DEPTH = 4
EPS = 1e-6
CONV_W = 31
NEG = -1.0e30


class Buf:
    __slots__ = ("w", "r")

    def __init__(self):
        self.w = None
        self.r = {}


class Prog:
    NDMA = 8

    def __init__(self, nc, es):
        self.nc = nc
        self.es = es
        self.E = {"pe": nc.tensor, "dve": nc.vector, "act": nc.scalar, "pool": nc.gpsimd, "sp": nc.sync}
        self.sems = {}
        self.cnt = {}
        for e in self.E:
            self.sems[e] = es.enter_context(nc.semaphore("s_" + e))
            self.cnt[e] = 0
        self.dq = {}
        for q in ("sp", "pool", "act"):
            slots = []
            for i in range(self.NDMA):
                k = "d_%s_%d" % (q, i)
                self.sems[k] = es.enter_context(nc.semaphore(k))
                slots.append(k)
            self.dq[q] = [slots, 0, [0] * self.NDMA]
        self.seen = {e: {} for e in self.E}
        self.ninst = 0

    def sb(self, name, shape, dt):
        return self.es.enter_context(self.nc.sbuf_tensor(name, list(shape), dt))

    def ps(self, name, shape, dt=F32):
        return self.es.enter_context(self.nc.psum_tensor(name, list(shape), dt))

    def _need(self, eng, k, v):
        if eng == "pe" and k == "pe":
            return
        if self.seen[eng].get(k, 0) < v:
            self.E[eng].wait_ge(self.sems[k], v)
            self.seen[eng][k] = v
            self.ninst += 1

    def _deps(self, eng, reads, writes):
        for b in reads:
            if b.w is not None:
                self._need(eng, b.w[0], b.w[1])
        for b in writes:
            if b.w is not None and not b.r:
                self._need(eng, b.w[0], b.w[1])
            for k, v in b.r.items():
                self._need(eng, k, v)

    def _record(self, tok, reads, writes):
        k, v = tok
        for b in reads:
            if b.r.get(k, 0) < v:
                b.r[k] = v
        for b in writes:
            b.w = tok
            b.r = {}

    def op(self, eng, fn, reads=(), writes=()):
        self._deps(eng, reads, writes)
        inst = fn(self.E[eng])
        self.cnt[eng] += 1
        inst.then_inc(self.sems[eng], 1)
        self.ninst += 1
        self._record((eng, self.cnt[eng]), reads, writes)

    def dma(self, q, fn, reads=(), writes=()):
        slots, nxt, counts = self.dq[q]
        s = nxt % self.NDMA
        self.dq[q][1] = nxt + 1
        k = slots[s]
        self._deps(q, reads, writes)
        self._need(q, k, counts[s] * 16)
        inst = fn(self.E[q])
        counts[s] += 1
        inst.then_inc(self.sems[k], 16)
        self.ninst += 1
        self._record((k, counts[s] * 16), reads, writes)

    def wait_all(self, eng, bufs):
        self._deps(eng, bufs, bufs)

    def barrier(self):
        for e in self.E:
            for k in self.E:
                if k != e:
                    self._need(e, k, self.cnt[k])
            for q in self.dq:
                slots, _, counts = self.dq[q]
                for s, k in enumerate(slots):
                    self._need(e, k, counts[s] * 16)


def _bc(ap, shape):
    return ap.to_broadcast(list(shape))


class Model:
    def __init__(self, S, stages):
        self.S = S
        self.stages = stages
        self.nc = bass.Bass("TRN2", target_bir_lowering=False)
        self.din = {}

    def inp(self, name, shape, dt=F32):
        t = self.nc.dram_tensor(name, list(shape), dt, kind="ExternalInput")
        self.din[name] = t
        return t

    def build(self):
        nc = self.nc
        S = self.S
        NB = S // 128
        with ExitStack() as es:
            P = Prog(nc, es)
            self.P = P
            xT = self.inp("xT", [128, KC, S])
            cT = self.inp("cT", [128, KC])
            pos = self.inp("pos", [1, S], I32)
            ada_w = self.inp("ada_w", [DEPTH, KC, 128, 6 * D])
            vecs = self.inp("vecs", [128, self.NV])
            ident_d = self.inp("ident", [128, 128])
            outT = nc.dram_tensor("outT", [128, KC, S], F32, kind="ExternalOutput")
            hS = nc.dram_tensor("hS", [128, KC, S], F32)
            self.hbufs = {"x": [Buf() for _ in range(NB)], "h": [Buf() for _ in range(NB)],
                          "o": [Buf() for _ in range(NB)]}
            self.hten = {"x": xT, "h": hS, "o": outT}
            self.ident = P.sb("ident_s", [128, 128], F32)
            self.b_const = Buf()
            P.dma("sp", lambda e: e.dma_start(out=self.ident[:], in_=ident_d.ap()), writes=[self.b_const])
            self.ones = P.sb("ones", [128, 128], F32)
            P.op("pool", lambda e: e.memset(self.ones[:], 1.0), writes=[self.b_const])
            self.epsb = P.sb("epsb", [128, 1], F32)
            P.op("pool", lambda e: e.memset(self.epsb[:], EPS), writes=[self.b_const])
            self.vec = P.sb("vec", [128, self.NV], F32)
            self.b_vec = Buf()
            P.dma("sp", lambda e: e.dma_start(out=self.vec[:], in_=vecs.ap()), writes=[self.b_vec])
            self.pb = [P.ps("pb%d" % i, [128, 512]) for i in range(8)]
            self.pbb = [Buf() for _ in range(8)]
            self.mods = P.sb("mods", [128, DEPTH, 96], F32)
            self.b_mods = Buf()
            self._stage_mods(cT, ada_w)
            for st in self.stages:
                kind, li = st[0], st[1]
                if kind == "conv":
                    s = li // 3
                    if "conv_w_pw1_%d" % s not in self.din:
                        self.inp("conv_w_pw1_%d" % s, [32, 128, KC, 128])
                        self.inp("conv_w_pw2_%d" % s, [16, 128, KC, 128])
                if kind == "peer":
                    if "peer_wq_%d" % li not in self.din:
                        self.inp("peer_wq_%d" % li, [16, 128, KC, 128])
                        self.inp("peer_skT_%d" % li, [128, 16, 128])
                        self.inp("peer_u_%d" % li, [16384, D])
                        self.inp("peer_v_%d" % li, [16384, D])
                if kind == "hgrn":
                    self.inp("hg_w_qfg", [48, 128, KC, 128])
                    self.inp("hg_w_i", [16, 128, KC, 128])
                    self.inp("hg_w_out", [16, 128, KC, 128])
                if kind in ("hgrn", "mla") and "tri" not in self.din:
                    self.inp("tri", [128, 128])
                if kind == "mla":
                    self.declare_mla()
            for st in self.stages:
                kind, li, src, dst = st
                getattr(self, "_stage_" + kind)(li, src, dst)
            P.wait_all("sp", self.hbufs["o"])
            self.ninst = P.ninst
        return nc

    NV = 0
    VOFF = {}

    def v(self, name, j=0, n=1):
        o = self.VOFF[name] + j
        return self.vec[:, o:o + n]

    def _stage_mods(self, cT, ada_w):
        P = self.P
        with ExitStack() as es:
            P2 = P
            nc = self.nc
            sc = es.enter_context(nc.sbuf_tensor("m_sc", [128, KC, 2], F32))
            c0 = es.enter_context(nc.sbuf_tensor("m_c0", [128, KC], F32))
            wt = [es.enter_context(nc.sbuf_tensor("m_wt%d" % i, [128, KC, 512], F32)) for i in range(2)]
            bsc, bc0 = Buf(), Buf()
            bwt = [Buf(), Buf()]
            P.dma("sp", lambda e: e.dma_start(out=c0[:], in_=cT.ap()), writes=[bc0])
            for j in range(2):
                P.op("act", lambda e, j=j: e.activation(out=sc[:, :, j], in_=c0[:], func=AF.Silu),
                     reads=[bc0], writes=[bsc])
            n = 0
            for li in range(DEPTH):
                for cb in range(6 * D // 512):
                    w = wt[n % 2]
                    bw = bwt[n % 2]
                    n += 1
                    P.dma("sp", lambda e, w=w, li=li, cb=cb: e.dma_start(
                        out=w[:], in_=ada_w.ap()[li, :, :, cb * 512:(cb + 1) * 512].rearrange("k p c -> p k c")),
                        writes=[bw])
                    bank = self.pb[n % 2]
                    bb = self.pbb[n % 2]
                    for j in range(4):
                        for kc in range(KC):
                            P.op("pe", lambda e, w=w, j=j, kc=kc, bank=bank: e.matmul(
                                bank[:, 2 * j:2 * j + 2], lhsT=w[:, kc, j * 128:(j + 1) * 128], rhs=sc[:, kc, :],
                                start=(kc == 0), stop=(kc == KC - 1)),
                                reads=[bw, bsc], writes=[bb])
                    for j in range(4):
                        col = cb * 4 + j
                        P.op("dve", lambda e, j=j, col=col, li=li, bank=bank: e.tensor_tensor(
                            out=self.mods[:, li, col:col + 1], in0=bank[:, 2 * j:2 * j + 1],
                            in1=self.v("ada_b", li * 96 + col), op=ALU.add),
                            reads=[bb, self.b_vec], writes=[self.b_mods])
            P.barrier()

    def mod(self, li, s, kc):
        return self.mods[:, li, s * 16 + kc:s * 16 + kc + 1]

    debug = False
    dumps = None

    def dump(self, name, ap, shape, dt, reads):
        if not self.debug:
            return
        if self.dumps is None:
            self.dumps = {}
        if name in self.dumps:
            return
        t = self.nc.dram_tensor("dbg_" + name, list(shape), dt, kind="ExternalOutput")
        self.dumps[name] = t
        b = Buf()
        self.P.dma("sp", lambda e: e.dma_start(out=t.ap(), in_=ap), reads=reads, writes=[b])
        self.hbufs["o"].append(b)

    def hbl(self, which, t0, n):
        return self.hbufs[which][t0 // 128:(t0 + n) // 128]

    def load_h(self, which, t0, N, h, bh):
        ten = self.hten[which]
        self.P.dma("sp", lambda e: e.dma_start(out=h[:, :, :N], in_=ten.ap()[:, :, t0:t0 + N]),
                   reads=self.hbl(which, t0, N), writes=[bh])

    def store_h(self, which, t0, N, h, bh):
        ten = self.hten[which]
        self.P.dma("sp", lambda e: e.dma_start(out=ten.ap()[:, :, t0:t0 + N], in_=h[:, :, :N]),
                   reads=[bh], writes=self.hbl(which, t0, N))

    def make_A(self, es, li, half):
        P = self.P
        A = es.enter_context(self.nc.sbuf_tensor("A_%d_%d" % (li, half), [128, KC], F32))
        bA = Buf()
        s_sc = 1 + 3 * half
        P.op("dve", lambda e: e.tensor_scalar(out=A[:], in0=self.mods[:, li, s_sc * 16:(s_sc + 1) * 16],
                                              scalar1=1.0, scalar2=None, op0=ALU.add),
             reads=[self.b_mods], writes=[bA])
        P.op("dve", lambda e: e.tensor_tensor(out=A[:], in0=A[:], in1=self.v("norm_g", (li * 2 + half) * 16, 16),
                                              op=ALU.mult),
             reads=[bA, self.b_vec], writes=[bA])
        return A, bA

    def colsum(self, srcs, nk, N, bank, bbank, sq, bsq, square=True, extra_reads=()):
        P = self.P
        for kc in range(nk):
            s = sq[kc % 2]
            bs = bsq[kc % 2]
            if square:
                P.op("act", lambda e, s=s, kc=kc: e.activation(out=s[:, :N], in_=srcs(kc), func=AF.Square),
                     reads=list(extra_reads), writes=[bs])
                rhs = s[:, :N]
                rd = [bs, self.b_const]
            else:
                rhs = srcs(kc)
                rd = list(extra_reads) + [self.b_const]
            P.op("pe", lambda e, rhs=rhs, kc=kc: e.matmul(bank[:, :N], lhsT=self.ones[:], rhs=rhs,
                                                          start=(kc == 0), stop=(kc == nk - 1)),
                 reads=rd, writes=[bbank])

    def norm_mod(self, es, h, bh, N, A, bA, li, half, out, bout, tag):
        P = self.P
        nc = self.nc
        sq = [es.enter_context(nc.sbuf_tensor("%s_sq%d" % (tag, i), [128, 512], F32)) for i in range(2)]
        bsq = [Buf(), Buf()]
        r = es.enter_context(nc.sbuf_tensor("%s_r" % tag, [128, 512], F32))
        br = Buf()
        s_sh = 3 * half

        def run():
            bank, bb = self.pb[7], self.pbb[7]
            self.colsum(lambda kc: h[:, kc, :N], KC, N, bank, bb, sq, bsq, extra_reads=[bh])
            P.op("act", lambda e: e.activation(out=r[:, :N], in_=bank[:, :N], func=AF.Sqrt, bias=self.epsb[:],
                                               scale=1.0 / D), reads=[bb, self.b_const], writes=[br])
            P.op("dve", lambda e: e.reciprocal(out=r[:, :N], in_=r[:, :N]), reads=[br], writes=[br])
            for kc in range(KC):
                s = sq[kc % 2]
                bs = bsq[kc % 2]
                P.op("dve", lambda e, s=s, kc=kc: e.scalar_tensor_tensor(
                    out=s[:, :N], in0=h[:, kc, :N], scalar=A[:, kc:kc + 1], in1=r[:, :N], op0=ALU.mult, op1=ALU.mult),
                    reads=[bh, bA, br], writes=[bs])
                P.op("act", lambda e, s=s, kc=kc: e.activation(
                    out=out[:, kc, :N], in_=s[:, :N], func=AF.Identity, bias=self.mod(li, s_sh, kc), scale=1.0),
                    reads=[bs, self.b_mods], writes=[bout])
        return run

    def linear_fm(self, es, wd, KCin, mlist, x, bx, N, epi, tag, banks=(0, 1, 2, 3), f32=False, x_col0=0):
        P = self.P
        nc = self.nc
        wst = [es.enter_context(nc.sbuf_tensor("%s_ws%d" % (tag, i), [128, KCin, 128], F32)) for i in range(2)]
        bws = [Buf(), Buf()]
        wdb = None
        if not f32:
            wbf = [es.enter_context(nc.sbuf_tensor("%s_wb%d" % (tag, i), [128, KCin, 128], BF16)) for i in range(2)]
            bwb = [Buf(), Buf()]
            ntile = max(mlist) + 1
            wdb = nc.dram_tensor("wb_" + tag, [ntile, 128, KCin, 128], BF16)
            bdst = Buf()
            for t in range(ntile):
                k = t % 2
                P.dma("sp", lambda e, k=k, t=t: e.dma_start(out=wst[k][:], in_=wd.ap()[t]), writes=[bws[k]])
                if t % 2 == 0:
                    P.op("act", lambda e, k=k: e.activation(out=wbf[k][:], in_=wst[k][:], func=AF.Copy),
                         reads=[bws[k]], writes=[bwb[k]])
                else:
                    P.op("dve", lambda e, k=k: e.tensor_copy(out=wbf[k][:], in_=wst[k][:]), reads=[bws[k]], writes=[bwb[k]])
                P.dma("sp", lambda e, k=k, t=t: e.dma_start(out=wdb.ap()[t], in_=wbf[k][:]), reads=[bwb[k]], writes=[bdst])

        def run(mlist=mlist, x=x, bx=bx, N=N, epi=epi, x_col0=x_col0):
            for i, mc in enumerate(mlist):
                st = self._wrot
                self._wrot += 1
                w, bw = wst[st % 2], bws[st % 2]
                if f32:
                    P.dma("sp", lambda e, w=w, mc=mc: e.dma_start(out=w[:], in_=wd.ap()[mc]), writes=[bw])
                    wm, bm = w, bw
                else:
                    wm, bm = wbf[st % 2], bwb[st % 2]
                    P.dma("sp", lambda e, wm=wm, mc=mc: e.dma_start(out=wm[:], in_=wdb.ap()[mc]), reads=[bdst], writes=[bm])
                bk = banks[st % len(banks)]
                bank, bb = self.pb[bk], self.pbb[bk]
                for kc in range(KCin):
                    P.op("pe", lambda e, wm=wm, kc=kc, bank=bank: e.matmul(
                        bank[:, :N], lhsT=wm[:, kc, :], rhs=x[:, kc, x_col0:x_col0 + N],
                        start=(kc == 0), stop=(kc == KCin - 1)),
                        reads=[bm, bx], writes=[bb])
                epi(i, mc, bank, bb)
        return run

    _wrot = 0

    def _stage_conv(self, li, src, dst):
        P = self.P
        nc = self.nc
        S = self.S
        slot = li // 3
        TG = 512
        wd1 = self.din["conv_w_pw1_%d" % slot]
        wd2 = self.din["conv_w_pw2_%d" % slot]
        pre = "conv%d_" % slot
        with ExitStack() as es:
            sbt = lambda n, s, d: es.enter_context(nc.sbuf_tensor("cv%d_" % li + n, list(s), d))
            h = sbt("h", [128, KC, TG], F32); bh = Buf()
            u = sbt("u", [128, KC, TG], BF16); bu = Buf()
            glu = sbt("glu", [128, KC, 30 + TG], F32); bglu = [Buf() for _ in range(KC)]
            y = sbt("y", [128, KC, TG], F32); by = [Buf() for _ in range(KC)]
            z = sbt("z", [128, KC, TG], BF16); bz = Buf()
            sig = [sbt("sig%d" % i, [128, TG], F32) for i in range(2)]; bsig = [Buf(), Buf()]
            mean = sbt("mean", [128, TG], F32); bmean = Buf()
            rstd = sbt("rstd", [128, TG], F32); brstd = Buf()
            tmp = [sbt("tmp%d" % i, [128, TG], F32) for i in range(2)]; btmp = [Buf(), Buf()]
            gb = sbt("gb", [128, KC], F32); bgb = Buf()
            A, bA = self.make_A(es, li, 0)
            P.op("dve", lambda e: e.tensor_tensor(out=gb[:], in0=self.mods[:, li, 32:48], in1=self.v(pre + "b_pw2", 0, 16),
                                                  op=ALU.mult), reads=[self.b_mods, self.b_vec], writes=[bgb])
            P.op("pool", lambda e: e.memset(glu[:, :, 0:30], 0.0), writes=bglu)
            nm = self.norm_mod(es, h, bh, TG, A, bA, li, 0, u, bu, "cvn%d" % li)

            def epi1(i, mc, bank, bb):
                if mc >= KC:
                    j = mc - KC
                    s, bs = sig[j % 2], bsig[j % 2]
                    P.op("act", lambda e: e.activation(out=s[:], in_=bank[:, :TG], func=AF.Sigmoid,
                                                       bias=self.v(pre + "b_pw1", mc), scale=1.0),
                         reads=[bb, self.b_vec], writes=[bs])
                else:
                    j = mc
                    s, bs = sig[j % 2], bsig[j % 2]
                    P.op("dve", lambda e: e.scalar_tensor_tensor(
                        out=glu[:, j, 30:30 + TG], in0=bank[:, :TG], scalar=self.v(pre + "b_pw1", mc), in1=s[:],
                        op0=ALU.add, op1=ALU.mult), reads=[bb, bs, self.b_vec], writes=[bglu[j]])
            ml1 = []
            for j in range(KC):
                ml1 += [KC + j, j]
            lin1 = self.linear_fm(es, wd1, KC, ml1, u, bu, TG, epi1, "cv1_%d" % li)

            def epi2(i, mc, bank, bb):
                P.op("dve", lambda e: e.scalar_tensor_tensor(
                    out=h[:, mc, :], in0=bank[:, :TG], scalar=self.mod(li, 2, mc), in1=h[:, mc, :],
                    op0=ALU.mult, op1=ALU.add), reads=[bb, self.b_mods, bh], writes=[bh])
                P.op("dve", lambda e: e.tensor_scalar(out=h[:, mc, :], in0=h[:, mc, :], scalar1=gb[:, mc:mc + 1],
                                                      scalar2=None, op0=ALU.add), reads=[bh, bgb], writes=[bh])
            lin2 = self.linear_fm(es, wd2, KC, list(range(KC)), z, bz, TG, epi2, "cv2_%d" % li)

            for g in range(S // TG):
                t0 = g * TG
                self.load_h(src, t0, TG, h, bh)
                nm()
                lin1()
                for j in range(KC):
                    P.op("dve", lambda e, j=j: e.tensor_scalar(
                        out=y[:, j, :], in0=glu[:, j, 0:TG], scalar1=self.v(pre + "w_dw", j * CONV_W + 0),
                        scalar2=self.v(pre + "b_dw", j), op0=ALU.mult, op1=ALU.add),
                        reads=[bglu[j], self.b_vec], writes=[by[j]])
                    for w in range(1, CONV_W):
                        P.op("dve", lambda e, j=j, w=w: e.scalar_tensor_tensor(
                            out=y[:, j, :], in0=glu[:, j, w:w + TG], scalar=self.v(pre + "w_dw", j * CONV_W + w),
                            in1=y[:, j, :], op0=ALU.mult, op1=ALU.add),
                            reads=[bglu[j], by[j], self.b_vec], writes=[by[j]])
                    P.op("pool", lambda e, j=j: e.tensor_copy(out=glu[:, j, 0:30], in_=glu[:, j, TG:TG + 30]),
                         reads=[bglu[j]], writes=[bglu[j]])
                self.colsum(lambda kc: y[:, kc, :], KC, TG, self.pb[4], self.pbb[4], tmp, btmp, square=False,
                            extra_reads=by)
                self.colsum(lambda kc: y[:, kc, :], KC, TG, self.pb[5], self.pbb[5], tmp, btmp, square=True,
                            extra_reads=by)
                P.op("act", lambda e: e.activation(out=mean[:], in_=self.pb[4][:, :TG], func=AF.Copy, scale=1.0 / D),
                     reads=[self.pbb[4]], writes=[bmean])
                P.op("dve", lambda e: e.tensor_tensor(out=rstd[:], in0=mean[:], in1=mean[:], op=ALU.mult),
                     reads=[bmean], writes=[brstd])
                P.op("dve", lambda e: e.scalar_tensor_tensor(out=rstd[:], in0=self.pb[5][:, :TG], scalar=1.0 / D,
                                                             in1=rstd[:], op0=ALU.mult, op1=ALU.subtract),
                     reads=[self.pbb[5], brstd], writes=[brstd])
                P.op("act", lambda e: e.activation(out=rstd[:], in_=rstd[:], func=AF.Sqrt, bias=self.epsb[:], scale=1.0),
                     reads=[brstd, self.b_const], writes=[brstd])
                P.op("dve", lambda e: e.reciprocal(out=rstd[:], in_=rstd[:]), reads=[brstd], writes=[brstd])
                for j in range(KC):
                    t, bt = tmp[j % 2], btmp[j % 2]
                    P.op("dve", lambda e, j=j, t=t: e.tensor_tensor(out=t[:], in0=y[:, j, :], in1=mean[:], op=ALU.subtract),
                         reads=[by[j], bmean], writes=[bt])
                    P.op("dve", lambda e, t=t: e.tensor_tensor(out=t[:], in0=t[:], in1=rstd[:], op=ALU.mult),
                         reads=[bt, brstd], writes=[bt])
                    P.op("act", lambda e, j=j, t=t: e.activation(out=z[:, j, :], in_=t[:], func=AF.Silu,
                                                                 bias=self.v(pre + "ln_b", j), scale=self.v(pre + "ln_g", j)),
                         reads=[bt, self.b_vec], writes=[bz])
                lin2()
                self.store_h(dst, t0, TG, h, bh)
            P.barrier()

    def _stage_final(self, li, src, dst):
        P = self.P
        nc = self.nc
        S = self.S
        TG = 512
        with ExitStack() as es:
            sbt = lambda n, s, d: es.enter_context(nc.sbuf_tensor("fn_" + n, list(s), d))
            h = sbt("h", [128, KC, TG], F32); bh = Buf()
            o = sbt("o", [128, KC, TG], F32); bo = Buf()
            sq = [sbt("sq%d" % i, [128, TG], F32) for i in range(2)]; bsq = [Buf(), Buf()]
            r = sbt("r", [128, TG], F32); br = Buf()
            for g in range(S // TG):
                t0 = g * TG
                self.load_h(src, t0, TG, h, bh)
                bank, bb = self.pb[7], self.pbb[7]
                self.colsum(lambda kc: h[:, kc, :], KC, TG, bank, bb, sq, bsq, extra_reads=[bh])
                P.op("act", lambda e: e.activation(out=r[:], in_=bank[:, :TG], func=AF.Sqrt, bias=self.epsb[:],
                                                   scale=1.0 / D), reads=[bb, self.b_const], writes=[br])
                P.op("dve", lambda e: e.reciprocal(out=r[:], in_=r[:]), reads=[br], writes=[br])
                for kc in range(KC):
                    P.op("dve", lambda e, kc=kc: e.scalar_tensor_tensor(
                        out=o[:, kc, :], in0=h[:, kc, :], scalar=self.v("final_g", kc), in1=r[:],
                        op0=ALU.mult, op1=ALU.mult), reads=[bh, br, self.b_vec], writes=[bo])
                self.store_h(dst, t0, TG, o, bo)
            P.barrier()


_VEC_SPEC = [("ada_b", 4 * 96), ("norm_g", 4 * 2 * 16), ("final_g", 16),
             ("conv0_b_pw1", 32), ("conv0_w_dw", 16 * CONV_W), ("conv0_b_dw", 16), ("conv0_ln_g", 16),
             ("conv0_ln_b", 16), ("conv0_b_pw2", 16),
             ("conv1_b_pw1", 32), ("conv1_w_dw", 16 * CONV_W), ("conv1_b_dw", 16), ("conv1_ln_g", 16),
             ("conv1_ln_b", 16), ("conv1_b_pw2", 16),
             ("hg_lb", 4 * 16), ("hg_norm_g", 1), ("mla_qg", 4), ("mla_kvg", 4), ("inv_freq", 1)]
_o = 0
for _n, _w in _VEC_SPEC:
    Model.VOFF[_n] = _o
    _o += _w
Model.NV = _o


def _fvec(v):
    v = np.asarray(v, np.float32).reshape(-1)
    return v.reshape(-1, 128).T


def _wtiles(W):
    K, M = W.shape
    return np.ascontiguousarray(W.reshape(K // 128, 128, M // 128, 128).transpose(2, 1, 0, 3))


def pack_vecs(inp):
    parts = {}
    parts["ada_b"] = np.concatenate([_fvec(inp["ada_b"][i]) for i in range(4)], axis=1)
    parts["norm_g"] = np.concatenate([_fvec(inp["norm_g"][i, j]) for i in range(4) for j in range(2)], axis=1)
    parts["final_g"] = _fvec(inp["final_g"])
    for s in range(2):
        p = "conv%d_" % s
        parts[p + "b_pw1"] = _fvec(inp["conv_b_pw1"][s])
        wdw = np.asarray(inp["conv_w_dw"][s], np.float32)
        parts[p + "w_dw"] = wdw.T.reshape(16, 128, CONV_W).transpose(1, 0, 2).reshape(128, 16 * CONV_W)
        parts[p + "b_dw"] = _fvec(inp["conv_b_dw"][s])
        parts[p + "ln_g"] = _fvec(inp["conv_ln_g"][s])
        parts[p + "ln_b"] = _fvec(inp["conv_ln_b"][s])
        parts[p + "b_pw2"] = _fvec(inp["conv_b_pw2"][s])
    parts["hg_lb"] = np.concatenate([_fvec(inp["hg_lb_logits"][i]) for i in range(4)], axis=1)
    parts["hg_norm_g"] = np.asarray(inp["hg_norm_g"][0], np.float32).reshape(128, 1)
    parts["mla_qg"] = _fvec(inp["mla_q_norm_g"][0])
    parts["mla_kvg"] = _fvec(inp["mla_kv_norm_g"][0])
    invf = np.zeros((128, 1), np.float32)
    invf[:32, 0] = (10000.0 ** (-np.arange(0, 64, 2, dtype=np.float32) / 64.0)).astype(np.float32)
    invf[32:64, 0] = invf[:32, 0]
    parts["inv_freq"] = invf
    cols = []
    for n, w in _VEC_SPEC:
        a = np.asarray(parts[n], np.float32)
        assert a.shape == (128, w), (n, a.shape, w)
        cols.append(a)
    return np.ascontiguousarray(np.concatenate(cols, axis=1))


def pack_inputs(inp, S, needed):
    m = {}
    x = np.asarray(inp["x"], np.float32)[0, :S]
    m["xT"] = np.ascontiguousarray(x.T.reshape(KC, 128, S).transpose(1, 0, 2))
    m["cT"] = np.ascontiguousarray(_fvec(inp["c"][0]))
    m["pos"] = np.ascontiguousarray(np.asarray(inp["positions"], np.int32)[:, :S])
    m["ada_w"] = np.asarray(inp["ada_w"], np.float32).reshape(DEPTH, KC, 128, 6 * D)
    m["vecs"] = pack_vecs(inp)
    m["ident"] = np.eye(128, dtype=np.float32)
    for s in range(2):
        if "conv_w_pw1_%d" % s in needed:
            m["conv_w_pw1_%d" % s] = _wtiles(np.asarray(inp["conv_w_pw1"][s], np.float32))
            m["conv_w_pw2_%d" % s] = _wtiles(np.asarray(inp["conv_w_pw2"][s], np.float32))
    if "tri" in needed:
        m["tri"] = np.triu(np.ones((128, 128), np.float32))
    if "mla_w_in" in needed:
        w = np.asarray(inp["mla_w_in"][0], np.float32)
        wp = np.zeros((D, 9 * 128), np.float32)
        wp[:, :1088] = w
        m["mla_w_in"] = _wtiles(wp)
        wq = np.asarray(inp["mla_w_uq"][0], np.float32).reshape(512, 16, 192)
        wqp = np.zeros((512, 16, 2, 128), np.float32)
        wqp[:, :, 0, :] = wq[:, :, 0:128]
        wqp[:, :, 1, 0:64] = wq[:, :, 128:192]
        m["mla_w_uq"] = _wtiles(wqp.reshape(512, 32 * 128))
        m["mla_w_ukv"] = _wtiles(np.asarray(inp["mla_w_ukv"][0], np.float32))
        m["mla_w_o"] = _wtiles(np.asarray(inp["mla_w_o"][0], np.float32))
        ps = np.zeros((64, 64), np.float32)
        ps[np.arange(32) + 32, np.arange(32)] = 1.0
        ps[np.arange(32), np.arange(32) + 32] = 1.0
        m["pswap"] = ps
        dmk = np.ones((128, 128), np.float32)
        dmk[64:, :64] = 0.0
        m["dmask"] = dmk
        sg = np.ones((128, 1), np.float32)
        sg[:32] = -1.0
        m["rsgn"] = sg
    if "hg_w_qfg" in needed:
        w_in = np.asarray(inp["hg_w_in"][0], np.float32)
        m["hg_w_qfg"] = _wtiles(np.concatenate([w_in[:, 0:D], w_in[:, D:2 * D], w_in[:, 3 * D:4 * D]], axis=1))
        m["hg_w_i"] = _wtiles(w_in[:, 2 * D:3 * D])
        m["hg_w_out"] = _wtiles(np.asarray(inp["hg_w_out"][0], np.float32))
    for li in range(DEPTH):
        if "peer_wq_%d" % li in needed:
            m["peer_wq_%d" % li] = _wtiles(np.asarray(inp["peer_w_q"][li], np.float32))
            sk = np.asarray(inp["peer_sub_keys"][li], np.float32).reshape(16, 128, 128)
            m["peer_skT_%d" % li] = np.ascontiguousarray(sk.transpose(2, 0, 1))
            m["peer_u_%d" % li] = np.asarray(inp["peer_u"][li], np.float32)
            m["peer_v_%d" % li] = np.asarray(inp["peer_v"][li], np.float32)
    return m


def unpack_out(outT, S):
    return np.ascontiguousarray(outT.transpose(2, 1, 0).reshape(S, D))[None]


def run_model(inputs, S, stages, debug=False):
    m = Model(S, stages)
    m.debug = debug
    nc = m.build()
    in_map = pack_inputs(inputs, S, set(m.din.keys()))
    in_map = {k: in_map[k] for k in m.din}
    res = run_bass_kernel_spmd(nc, [in_map], core_ids=[0])
    m.results = res.results[0]
    return unpack_out(res.results[0]["outT"], S), m


FULL_STAGES = [("conv", 0, "x", "h"), ("peer", 0, "h", "h"),
               ("hgrn", 1, "h", "h"), ("peer", 1, "h", "h"),
               ("mla", 2, "h", "h"), ("peer", 2, "h", "h"),
               ("conv", 3, "h", "h"), ("peer", 3, "h", "h"),
               ("final", 0, "h", "o")]


def kernel(**inputs):
    out, _ = run_model(inputs, 8192, FULL_STAGES)
    return out.astype(np.float32)


def _peer_stage(self, li, src, dst):
    P = self.P
    nc = self.nc
    S = self.S
    TG = 256
    NBLK = TG // 128
    R = 8
    wq = self.din["peer_wq_%d" % li]
    skd = self.din["peer_skT_%d" % li]
    utab_f = self.din["peer_u_%d" % li]
    vtab_f = self.din["peer_v_%d" % li]
    if getattr(self, "ubf", None) is None:
        self.ubf = nc.dram_tensor("peer_ubf", [16384, D], BF16)
        self.vbf = nc.dram_tensor("peer_vbf", [16384, D], BF16)
    utab, vtab = self.ubf, self.vbf
    btab = Buf()
    with ExitStack() as es:
        JR = 2
        stg = [es.enter_context(nc.sbuf_tensor("pc%d_s%d" % (li, i), [128, JR, D], F32)) for i in range(2)]
        bstg = [Buf(), Buf()]
        cvt = [es.enter_context(nc.sbuf_tensor("pc%d_c%d" % (li, i), [128, JR, D], BF16)) for i in range(2)]
        bcvt = [Buf(), Buf()]
        n = 0
        for src_t, dst_t in ((utab_f, utab), (vtab_f, vtab)):
            for rt_ in range(16384 // (128 * JR)):
                k = n % 2
                r0 = rt_ * 128 * JR
                P.dma("sp", lambda e, k=k, r0=r0, src_t=src_t: e.dma_start(
                    out=stg[k][:], in_=src_t.ap()[r0:r0 + 128 * JR, :].rearrange("(p j) c -> p j c", j=JR)),
                    writes=[bstg[k]])
                eng = "act" if n % 2 == 0 else "dve"
                if eng == "act":
                    P.op("act", lambda e, k=k: e.activation(out=cvt[k][:], in_=stg[k][:], func=AF.Copy),
                         reads=[bstg[k]], writes=[bcvt[k]])
                else:
                    P.op("dve", lambda e, k=k: e.tensor_copy(out=cvt[k][:], in_=stg[k][:]),
                         reads=[bstg[k]], writes=[bcvt[k]])
                P.dma("sp", lambda e, k=k, r0=r0, dst_t=dst_t: e.dma_start(
                    out=dst_t.ap()[r0:r0 + 128 * JR, :].rearrange("(p j) c -> p j c", j=JR), in_=cvt[k][:]),
                    reads=[bcvt[k]], writes=[btab])
                n += 1
        P.barrier()
    with ExitStack() as es:
        sbt = lambda n, s, d: es.enter_context(nc.sbuf_tensor("pr%d_" % li + n, list(s), d))
        h = sbt("h", [128, KC, TG], F32); bh = Buf()
        u = sbt("u", [128, KC, TG], F32); bu = Buf()
        qT = sbt("qT", [128, KC, TG], F32); bq = Buf()
        ub = sbt("ub", [128, KC, TG], BF16); bub = Buf()
        skT = sbt("skT", [128, 16, 128], F32); bsk = Buf()
        Ssb = sbt("S", [128, 16, 128], F32); bS = Buf()
        S2 = sbt("S2", [128, 2048], F32); bS2 = Buf()
        m8 = sbt("m8", [128, 16, 16], F32); bm8 = Buf()
        i8 = sbt("i8", [128, 16, 16], U32); bi8 = Buf()
        i8f = sbt("i8f", [128, 16, 16], F32); bi8f = Buf()
        cand = sbt("cand", [128, 8, 256], F32); bcand = Buf()
        bs = sbt("bs", [128, 8, 16], F32); bbs = Buf()
        bp = sbt("bp", [128, 8, 16], U32); bbp = Buf()
        posf = sbt("posf", [128, 128], F32); bposf = Buf()
        af = sbt("af", [128, 128], F32); baf = Buf()
        bf = sbt("bf", [128, 128], F32); bbf = Buf()
        io16 = sbt("io16", [128, 128, 16], F32); bio = Buf()
        oh = sbt("oh", [128, 128, 16], F32); boh = Buf()
        isel = sbt("isel", [128, 128], F32); bisel = Buf()
        jsel = sbt("jsel", [128, 128], F32); bjsel = Buf()
        eidx = sbt("eidx", [128, 128], I32); beidx = Buf()
        gate = sbt("gate", [128, 8, 16], F32); bgate = Buf()
        gsum = sbt("gsum", [128, 8], F32); bgsum = Buf()
        av = sbt("a", [128, 128], F32); bav = Buf()
        t1 = sbt("t1", [128, 128], F32); bt1 = Buf()
        wv = sbt("w", [128, 128], F32); bwv = Buf()
        utok = sbt("utok", [128, D], F32); butok = Buf()
        acc = sbt("acc", [128, D], F32); bacc = Buf()
        scr = sbt("scr", [128, D], BF16); bscr = Buf()
        gb = [sbt("g%d" % i, [128, D], BF16) for i in range(R)]; bgb = [Buf() for _ in range(R)]
        A, bA = self.make_A(es, li, 1)
        identb = sbt("identb", [128, 128], BF16); bidb = Buf()
        P.op("dve", lambda e: e.tensor_copy(out=identb[:], in_=self.ident[:]), reads=[self.b_const], writes=[bidb])
        dg = [sbt("dg%d" % i, [128, 128], BF16) for i in range(4)]; bdg = [Buf() for _ in range(4)]
        P.dma("sp", lambda e: e.dma_start(out=skT[:], in_=skd.ap()), writes=[bsk])
        P.op("pool", lambda e: e.iota(io16[:], pattern=[[0, 128], [1, 16]], base=0, channel_multiplier=0,
                                      allow_small_or_imprecise_dtypes=True), writes=[bio])
        nm = self.norm_mod(es, h, bh, TG, A, bA, li, 1, u, bu, "prn%d" % li)

        def epiq(i, mc, bank, bb):
            P.op("act", lambda e: e.activation(out=qT[:, mc, :], in_=bank[:, :TG], func=AF.Copy),
                 reads=[bb], writes=[bq])
        linq = self.linear_fm(es, wq, KC, list(range(KC)), ub, bub, TG, epiq, "prq%d" % li, banks=(0, 1), f32=False)

        def top16_many(items, n, rd):
            ng = len(items)
            bv = [Buf() for _ in range(ng)]
            bi = [Buf() for _ in range(ng)]
            b2 = [Buf() for _ in range(ng)]
            P._deps("dve", [], [bS2])
            for gi_, (s_, v_, i_) in enumerate(items):
                P.op("dve", lambda e, s_=s_, v_=v_: e.max(out=v_[:, 0:8], in_=s_), reads=rd, writes=[bv[gi_]])
            for gi_, (s_, v_, i_) in enumerate(items):
                P.op("dve", lambda e, s_=s_, v_=v_, i_=i_: e.max_index(out=i_[:, 0:8], in_max=v_[:, 0:8], in_values=s_),
                     reads=rd + [bv[gi_]], writes=[bi[gi_]])
            for gi_, (s_, v_, i_) in enumerate(items):
                P.op("dve", lambda e, s_=s_, v_=v_, gi_=gi_: e.match_replace(
                    out=S2[:, gi_ * n:(gi_ + 1) * n], in_to_replace=v_[:, 0:8], in_values=s_, imm_value=NEG),
                    reads=rd + [bv[gi_]], writes=[b2[gi_]])
            for gi_, (s_, v_, i_) in enumerate(items):
                P.op("dve", lambda e, v_=v_, gi_=gi_: e.max(out=v_[:, 8:16], in_=S2[:, gi_ * n:(gi_ + 1) * n]),
                     reads=[b2[gi_]], writes=[bv[gi_]])
            for gi_, (s_, v_, i_) in enumerate(items):
                P.op("dve", lambda e, v_=v_, i_=i_, gi_=gi_: e.max_index(
                    out=i_[:, 8:16], in_max=v_[:, 8:16], in_values=S2[:, gi_ * n:(gi_ + 1) * n]),
                    reads=[b2[gi_], bv[gi_]], writes=[bi[gi_]])
            return bv + bi, b2

        gcount = [0]

        def gather(tab, m):
            k = gcount[0] % R
            gcount[0] += 1
            g, bg = gb[k], bgb[k]
            P.dma("pool", lambda e: e.indirect_dma_start(
                out=g[:, :], out_offset=None, in_=tab[:, :],
                in_offset=bass.IndirectOffsetOnAxis(ap=eidx[:, m:m + 1], axis=0)),
                reads=[beidx], writes=[bg])
            return g, bg

        for gi in range(S // TG):
            t0 = gi * TG
            self.load_h(src, t0, TG, h, bh)
            self.dump("h", h[:], [128, KC, TG], F32, [bh])
            nm()
            self.dump("u", u[:], [128, KC, TG], F32, [bu])
            P.op("act", lambda e: e.activation(out=ub[:], in_=u[:], func=AF.Copy), reads=[bu], writes=[bub])
            linq()
            self.dump("qT", qT[:], [128, KC, TG], F32, [bq])
            for blk in range(NBLK):
                c0 = blk * 128
                for g in range(16):
                    bank, bb = self.pb[2 + g // 4], self.pbb[2 + g // 4]
                    P.op("pe", lambda e, g=g, bank=bank: e.matmul(
                        bank[:, (g % 4) * 128:(g % 4 + 1) * 128], lhsT=qT[:, g, c0:c0 + 128], rhs=skT[:, g, :],
                        start=True, stop=True), reads=[bq, bsk], writes=[bb])
                for b4 in range(4):
                    P.op("act", lambda e, b4=b4: e.activation(
                        out=Ssb[:, b4 * 4:(b4 + 1) * 4, :], in_=self.pb[2 + b4][:, :].rearrange("p (g n) -> p g n", g=4),
                        func=AF.Copy), reads=[self.pbb[2 + b4]], writes=[bS])
                P._deps("dve", [], [bm8, bi8])
                tb, tb2 = top16_many([(Ssb[:, g, :], m8[:, g, :], i8[:, g, :]) for g in range(16)], 128, [bS])
                P.op("dve", lambda e: e.tensor_copy(out=i8f[:, 0, 0:1], in_=i8[:, 0, 0:1]), reads=tb + tb2, writes=[bm8, bi8, bS2])
                P.op("dve", lambda e: e.tensor_copy(out=i8f[:], in_=i8[:]), reads=[bi8], writes=[bi8f])
                m8v = m8[:].rearrange("p (h two) k -> p h two k", two=2)
                P.op("dve", lambda e: e.tensor_tensor(
                    out=cand[:].rearrange("p h (a b) -> p h a b", a=16),
                    in0=m8v[:, :, 0, :].unsqueeze(3).to_broadcast([128, 8, 16, 16]),
                    in1=m8v[:, :, 1, :].unsqueeze(2).to_broadcast([128, 8, 16, 16]), op=ALU.add),
                    reads=[bm8], writes=[bcand])
                P._deps("dve", [], [bbs, bbp])
                tb, tb2 = top16_many([(cand[:, hh, :], bs[:, hh, :], bp[:, hh, :]) for hh in range(8)], 256, [bcand])
                P.op("dve", lambda e: e.tensor_copy(out=posf[:, 0:1], in_=bp[:, 0, 0:1]), reads=tb + tb2, writes=[bbs, bbp, bS2])
                P.op("dve", lambda e: e.tensor_copy(out=posf[:], in_=bp[:].rearrange("p h k -> p (h k)")),
                     reads=[bbp], writes=[bposf])
                P.op("dve", lambda e: e.tensor_scalar(out=af[:], in0=posf[:], scalar1=0.0625, scalar2=0.53125,
                                                      op0=ALU.mult, op1=ALU.add), reads=[bposf], writes=[baf])
                P.op("dve", lambda e: e.tensor_scalar(out=af[:], in0=af[:], scalar1=8388608.0, scalar2=None,
                                                      op0=ALU.add), reads=[baf], writes=[baf])
                P.op("dve", lambda e: e.tensor_scalar(out=af[:], in0=af[:], scalar1=-8388609.0, scalar2=None,
                                                      op0=ALU.add), reads=[baf], writes=[baf])
                P.op("dve", lambda e: e.scalar_tensor_tensor(out=bf[:], in0=af[:], scalar=-16.0, in1=posf[:],
                                                             op0=ALU.mult, op1=ALU.add),
                     reads=[baf, bposf], writes=[bbf])
                i8v = i8f[:].rearrange("p (h two) k -> p h two k", two=2)
                for which, sel_idx, dst_t, bdst in ((0, af, isel, bisel), (1, bf, jsel, bjsel)):
                    bsel = baf if which == 0 else bbf
                    P.op("dve", lambda e, sel_idx=sel_idx: e.tensor_tensor(
                        out=oh[:], in0=io16[:], in1=sel_idx[:].unsqueeze(2).to_broadcast([128, 128, 16]),
                        op=ALU.is_equal), reads=[bio, bsel], writes=[boh])
                    P.op("dve", lambda e, which=which: e.tensor_tensor(
                        out=oh[:].rearrange("p (h k) a -> p h k a", h=8),
                        in0=oh[:].rearrange("p (h k) a -> p h k a", h=8),
                        in1=i8v[:, :, which, :].unsqueeze(2).to_broadcast([128, 8, 16, 16]), op=ALU.mult),
                        reads=[boh, bi8f], writes=[boh])
                    P.op("dve", lambda e, dst_t=dst_t: e.tensor_reduce(out=dst_t[:], in_=oh[:], axis=AX.X, op=ALU.add),
                         reads=[boh], writes=[bdst])
                P.op("dve", lambda e: e.scalar_tensor_tensor(out=af[:], in0=isel[:], scalar=128.0, in1=jsel[:],
                                                             op0=ALU.mult, op1=ALU.add),
                     reads=[bisel, bjsel, baf], writes=[baf])
                P.op("dve", lambda e: e.tensor_copy(out=eidx[:], in_=af[:]), reads=[baf], writes=[beidx])
                self.dump("eidx", eidx[:], [128, 128], I32, [beidx])
                self.dump("m8", m8[:], [128, 16, 16], F32, [bm8])
                self.dump("i8f", i8f[:], [128, 16, 16], F32, [bi8f])
                self.dump("bs", bs[:], [128, 8, 16], F32, [bbs])
                self.dump("posf", posf[:], [128, 128], F32, [bposf])
                self.dump("S", Ssb[:], [128, 16, 128], F32, [bS])
                self.dump("isel", isel[:], [128, 128], F32, [bisel])
                self.dump("jsel", jsel[:], [128, 128], F32, [bjsel])
                P.op("dve", lambda e: e.tensor_tensor(out=gate[:], in0=bs[:], in1=bs[:, :, 0:1].to_broadcast([128, 8, 16]),
                                                      op=ALU.subtract), reads=[bbs], writes=[bgate])
                P.op("act", lambda e: e.activation(out=gate[:], in_=gate[:], func=AF.Exp), reads=[bgate], writes=[bgate])
                P.op("dve", lambda e: e.tensor_reduce(out=gsum[:], in_=gate[:], axis=AX.X, op=ALU.add),
                     reads=[bgate], writes=[bgsum])
                P.op("dve", lambda e: e.reciprocal(out=gsum[:], in_=gsum[:]), reads=[bgsum], writes=[bgsum])
                P.op("dve", lambda e: e.tensor_tensor(out=gate[:], in0=gate[:],
                                                      in1=gsum[:].unsqueeze(2).to_broadcast([128, 8, 16]), op=ALU.mult),
                     reads=[bgate, bgsum], writes=[bgate])
                for kc in range(KC):
                    bank, bb = self.pb[2 + kc // 4], self.pbb[2 + kc // 4]
                    P.op("pe", lambda e, kc=kc, bank=bank: e.transpose(
                        out=bank[:, (kc % 4) * 128:(kc % 4 + 1) * 128], in_=u[:, kc, c0:c0 + 128], identity=self.ident[:]),
                        reads=[bu, self.b_const], writes=[bb])
                for b4 in range(4):
                    P.op("act", lambda e, b4=b4: e.activation(out=utok[:, b4 * 512:(b4 + 1) * 512],
                                                              in_=self.pb[2 + b4][:, :], func=AF.Copy),
                         reads=[self.pbb[2 + b4]], writes=[butok])
                for m in range(128):
                    g, bg = gather(utab, m)
                    P.op("dve", lambda e, g=g, m=m: e.scalar_tensor_tensor(
                        out=scr[:], in0=utok[:], scalar=1.0, in1=g[:], op0=ALU.mult, op1=ALU.mult,
                        accum_out=av[:, m:m + 1]), reads=[butok, bg], writes=[bscr, bav])
                P.op("dve", lambda e: e.tensor_tensor(out=t1[:], in0=av[:], in1=av[:], op=ALU.mult), reads=[bav], writes=[bt1])
                P.op("dve", lambda e: e.tensor_scalar(out=t1[:], in0=t1[:], scalar1=0.044715, scalar2=1.0,
                                                      op0=ALU.mult, op1=ALU.add), reads=[bt1], writes=[bt1])
                P.op("dve", lambda e: e.tensor_tensor(out=t1[:], in0=t1[:], in1=av[:], op=ALU.mult),
                     reads=[bt1, bav], writes=[bt1])
                P.op("act", lambda e: e.activation(out=t1[:], in_=t1[:], func=AF.Tanh, scale=0.7978845608028654),
                     reads=[bt1], writes=[bt1])
                P.op("dve", lambda e: e.scalar_tensor_tensor(out=t1[:], in0=t1[:], scalar=1.0, in1=av[:],
                                                             op0=ALU.add, op1=ALU.mult), reads=[bt1, bav], writes=[bt1])
                P.op("dve", lambda e: e.scalar_tensor_tensor(out=wv[:], in0=t1[:], scalar=0.5,
                                                             in1=gate[:].rearrange("p h k -> p (h k)"),
                                                             op0=ALU.mult, op1=ALU.mult),
                     reads=[bt1, bgate], writes=[bwv])
                self.dump("a", av[:], [128, 128], F32, [bav])
                self.dump("w", wv[:], [128, 128], F32, [bwv])
                self.dump("gate", gate[:], [128, 8, 16], F32, [bgate])
                self.dump("utok", utok[:], [128, D], F32, [butok])
                for m in range(128):
                    g, bg = gather(vtab, m)
                    dgt, bdgt = dg[m % 4], bdg[m % 4]
                    P.op("act", lambda e, dgt=dgt, m=m: e.activation(out=dgt[:], in_=identb[:], func=AF.Copy,
                                                                     scale=wv[:, m:m + 1]),
                         reads=[bwv, bidb], writes=[bdgt])
                    for c4 in range(4):
                        P.op("pe", lambda e, dgt=dgt, g=g, c4=c4, m=m: e.matmul(
                            self.pb[2 + c4][:, :], lhsT=dgt[:], rhs=g[:, c4 * 512:(c4 + 1) * 512],
                            start=(m == 0), stop=(m == 127)), reads=[bdgt, bg], writes=[self.pbb[2 + c4]])
                for c4 in range(4):
                    P.op("act", lambda e, c4=c4: e.activation(out=acc[:, c4 * 512:(c4 + 1) * 512], in_=self.pb[2 + c4][:, :],
                                                              func=AF.Copy), reads=[self.pbb[2 + c4]], writes=[bacc])
                self.dump("acc", acc[:], [128, D], F32, [bacc])
                for kc in range(KC):
                    bank, bb = self.pb[2 + kc // 4], self.pbb[2 + kc // 4]
                    P.op("pe", lambda e, kc=kc, bank=bank: e.transpose(
                        out=bank[:, (kc % 4) * 128:(kc % 4 + 1) * 128], in_=acc[:, kc * 128:(kc + 1) * 128],
                        identity=self.ident[:]), reads=[bacc, self.b_const], writes=[bb])
                for kc in range(KC):
                    bank, bb = self.pb[2 + kc // 4], self.pbb[2 + kc // 4]
                    P.op("dve", lambda e, kc=kc, bank=bank: e.scalar_tensor_tensor(
                        out=h[:, kc, c0:c0 + 128], in0=bank[:, (kc % 4) * 128:(kc % 4 + 1) * 128],
                        scalar=self.mod(li, 5, kc), in1=h[:, kc, c0:c0 + 128], op0=ALU.mult, op1=ALU.add),
                        reads=[bb, self.b_mods, bh], writes=[bh])
            self.store_h(dst, t0, TG, h, bh)
        P.barrier()


Model._stage_peer = _peer_stage


class WS:
    def __init__(self, model, es, KCin, tag, nbuf=2):
        nc = model.nc
        self.m = model
        self.n = nbuf
        self.KCin = KCin
        self.wst = [es.enter_context(nc.sbuf_tensor("%s_ws%d" % (tag, i), [128, KCin, 128], F32)) for i in range(nbuf)]
        self.bws = [Buf() for _ in range(nbuf)]
        self.wbf = [es.enter_context(nc.sbuf_tensor("%s_wb%d" % (tag, i), [128, KCin, 128], BF16)) for i in range(nbuf)]
        self.bwb = [Buf() for _ in range(nbuf)]
        self.i = 0
        self.pre = {}

    def preconvert(self, wd, ntiles, kcs, tag):
        P = self.m.P
        nc = self.m.nc
        wdb = nc.dram_tensor("wb_" + tag, [ntiles, 128, kcs, 128], BF16)
        bdst = Buf()
        for t in range(ntiles):
            k = t % self.n
            w, bw, wm, bm = self.wst[k], self.bws[k], self.wbf[k], self.bwb[k]
            P.dma("sp", lambda e, w=w, t=t: e.dma_start(out=w[:, :kcs, :], in_=wd.ap()[t]), writes=[bw])
            if t % 2 == 0:
                P.op("act", lambda e, w=w, wm=wm: e.activation(out=wm[:, :kcs, :], in_=w[:, :kcs, :], func=AF.Copy),
                     reads=[bw], writes=[bm])
            else:
                P.op("dve", lambda e, w=w, wm=wm: e.tensor_copy(out=wm[:, :kcs, :], in_=w[:, :kcs, :]),
                     reads=[bw], writes=[bm])
            P.dma("sp", lambda e, wm=wm, t=t: e.dma_start(out=wdb.ap()[t], in_=wm[:, :kcs, :]), reads=[bm], writes=[bdst])
        self.pre[id(wd)] = (wdb, bdst)

    def get(self, wd, mc, f32=False, kcs=None):
        P = self.m.P
        k = self.i % self.n
        self.i += 1
        w, bw = self.wst[k], self.bws[k]
        kk = self.KCin if kcs is None else kcs
        if id(wd) in self.pre and not f32:
            wdb, bdst = self.pre[id(wd)]
            wm, bm = self.wbf[k], self.bwb[k]
            P.dma("sp", lambda e: e.dma_start(out=wm[:, :kk, :], in_=wdb.ap()[mc]), reads=[bdst], writes=[bm])
            return wm, bm
        P.dma("sp", lambda e: e.dma_start(out=w[:, :kk, :], in_=wd.ap()[mc]), writes=[bw])
        if f32:
            return w, bw
        wm, bm = self.wbf[k], self.bwb[k]
        if self.i % 4 == 0:
            P.op("act", lambda e: e.activation(out=wm[:, :kk, :], in_=w[:, :kk, :], func=AF.Copy), reads=[bw], writes=[bm])
        else:
            P.op("pool", lambda e: e.tensor_copy(out=wm[:, :kk, :], in_=w[:, :kk, :]), reads=[bw], writes=[bm])
        return wm, bm


def _lin(self, ws, wd, mc, kcs, rhs_of, rd, out_ap, bout, f32=False):
    P = self.P
    wm, bm = ws.get(wd, mc, f32=f32, kcs=kcs)
    for kc in range(kcs):
        P.op("pe", lambda e, kc=kc: e.matmul(out_ap, lhsT=wm[:, kc, :], rhs=rhs_of(kc),
                                             start=(kc == 0), stop=(kc == kcs - 1)),
             reads=[bm] + list(rd), writes=[bout])


Model.lin = _lin


def _hgrn_stage(self, li, src, dst):
    P = self.P
    nc = self.nc
    S = self.S
    TG = 256
    NCH = TG // 64
    H = 16
    wqfg = self.din["hg_w_qfg"]
    wi = self.din["hg_w_i"]
    wo = self.din["hg_w_out"]
    with ExitStack() as es:
        sbt = lambda n, s, d: es.enter_context(nc.sbuf_tensor("hg_" + n, list(s), d))
        h = sbt("h", [128, KC, TG], F32); bh = Buf()
        u = sbt("u", [128, KC, TG], BF16); bu = Buf()
        vtok = sbt("vtok", [64, NCH, D], BF16); bvt = [[Buf() for _ in range(H)] for _ in range(NCH)]
        qt = sbt("qt", [128, H, TG], BF16); bqt = [Buf() for _ in range(H)]
        ktb = sbt("ktb", [128, H, TG], BF16); bktb = [Buf() for _ in range(H)]
        ktok = sbt("ktok", [64, NCH, H, 128], BF16); bktok = [[Buf() for _ in range(H)] for _ in range(NCH)]
        gs = sbt("gs", [128, H, TG], BF16); bgs = [Buf() for _ in range(H)]
        oT = sbt("oT", [128, H, TG], F32); boT = [Buf() for _ in range(H)]
        ob = sbt("ob", [128, H, TG], BF16); bob = Buf()
        St = sbt("S", [128, H, 128], F32); bS = [Buf() for _ in range(H)]
        Sb = sbt("Sb", [128, H, 128], BF16); bSb = [Buf() for _ in range(H)]
        elast = sbt("elast", [128, H, NCH], F32); bel = [Buf() for _ in range(H)]
        lb = sbt("lb", [128, H], F32); blb = Buf()
        omlb = sbt("omlb", [128, H], F32)
        ex = sbt("ex", [128, 4, H], F32)
        tf = [sbt("tf%d" % i, [128, TG], F32) for i in range(2)]; btf = [Buf(), Buf()]
        logf = [sbt("logf%d" % i, [128, TG], F32) for i in range(2)]; blogf = [Buf(), Buf()]
        kk = [sbt("kk%d" % i, [128, TG], F32) for i in range(2)]; bkk = [Buf(), Buf()]
        bb_ = [sbt("b%d" % i, [128, TG], F32) for i in range(2)]; bbb = [Buf(), Buf()]
        eb = [sbt("eb%d" % i, [128, TG], F32) for i in range(2)]; beb = [Buf(), Buf()]
        ktf = [sbt("ktf%d" % i, [128, TG], F32) for i in range(2)]; bktf = [Buf(), Buf()]
        att = [sbt("att%d" % i, [64, 64], BF16) for i in range(4)]; batt = [Buf() for _ in range(4)]
        tri = sbt("tri", [64, 64], F32); btri = Buf()
        rs = sbt("rs", [128, TG], F32); brs = Buf()
        A, bA = self.make_A(es, li, 0)
        ws = WS(self, es, KC, "hgw")
        ws.preconvert(wqfg, 48, KC, "hg_qfg")
        ws.preconvert(wi, 16, KC, "hg_i")
        ws.preconvert(wo, 16, KC, "hg_o")
        nm = self.norm_mod(es, h, bh, TG, A, bA, li, 0, u, bu, "hg")
        P.dma("sp", lambda e: e.dma_start(out=tri[:], in_=self.din["tri"].ap()[0:64, 0:64]), writes=[btri])
        lg = self.v("hg_lb", 0, 64).rearrange("p (l k) -> p l k", l=4)
        P.op("act", lambda e: e.activation(out=ex[:], in_=lg, func=AF.Exp), reads=[self.b_vec], writes=[blb])
        P.op("dve", lambda e: e.tensor_tensor(out=omlb[:], in0=ex[:, 0, :], in1=ex[:, 1, :], op=ALU.add), reads=[blb], writes=[blb])
        P.op("dve", lambda e: e.tensor_tensor(out=omlb[:], in0=omlb[:], in1=ex[:, 2, :], op=ALU.add), reads=[blb], writes=[blb])
        P.op("dve", lambda e: e.tensor_tensor(out=omlb[:], in0=omlb[:], in1=ex[:, 3, :], op=ALU.add), reads=[blb], writes=[blb])
        P.op("dve", lambda e: e.reciprocal(out=omlb[:], in_=omlb[:]), reads=[blb], writes=[blb])
        P.op("dve", lambda e: e.tensor_copy(out=lb[:], in_=ex[:, 1, :]), reads=[blb], writes=[blb])
        for l in range(2, li + 1):
            P.op("dve", lambda e, l=l: e.tensor_tensor(out=lb[:], in0=lb[:], in1=ex[:, l, :], op=ALU.add), reads=[blb], writes=[blb])
        P.op("dve", lambda e: e.tensor_tensor(out=lb[:], in0=lb[:], in1=omlb[:], op=ALU.mult), reads=[blb], writes=[blb])
        P.op("dve", lambda e: e.tensor_scalar(out=omlb[:], in0=lb[:], scalar1=-1.0, scalar2=1.0, op0=ALU.mult, op1=ALU.add),
             reads=[blb], writes=[blb])
        P.op("pool", lambda e: e.memset(St[:], 0.0), writes=bS)
        P.op("pool", lambda e: e.memset(Sb[:], 0.0), writes=bSb)
        SCALE = 128.0 ** -0.5
        lrot = [0]

        for gi in range(S // TG):
            t0 = gi * TG
            self.load_h(src, t0, TG, h, bh)
            nm()
            for hh in range(H):
                k2 = hh % 2
                bk = lrot[0] % 2; lrot[0] += 1
                bank, bbk = self.pb[bk], self.pbb[bk]
                self.lin(ws, wqfg, 16 + hh, KC, lambda kc: u[:, kc, :], [bu], bank[:, :TG], bbk)
                t, bt = tf[k2], btf[k2]
                P.op("act", lambda e, t=t, bank=bank: e.activation(out=t[:], in_=bank[:, :TG], func=AF.Sigmoid),
                     reads=[bbk], writes=[bt])
                P.op("dve", lambda e, t=t, hh=hh: e.tensor_scalar(out=t[:], in0=t[:], scalar1=omlb[:, hh:hh + 1],
                                                                   scalar2=lb[:, hh:hh + 1], op0=ALU.mult, op1=ALU.add),
                     reads=[bt, blb], writes=[bt])
                lf, blf = logf[k2], blogf[k2]
                P.op("act", lambda e, t=t, lf=lf: e.activation(out=lf[:], in_=t[:], func=AF.Ln), reads=[bt], writes=[blf])
                kx, bkx = kk[k2], bkk[k2]
                P.op("dve", lambda e, t=t, kx=kx: e.tensor_scalar(out=kx[:], in0=t[:], scalar1=-1.0, scalar2=1.0,
                                                                  op0=ALU.mult, op1=ALU.add), reads=[bt], writes=[bkx])
                bc, bbc = bb_[k2], bbb[k2]
                for c in range(NCH):
                    P.op("dve", lambda e, c=c, bc=bc, lf=lf: e.tensor_tensor_scan(
                        out=bc[:, c * 64:(c + 1) * 64], data0=self.ones[:, 0:64], data1=lf[:, c * 64:(c + 1) * 64],
                        initial=0.0, op0=ALU.mult, op1=ALU.add), reads=[blf, self.b_const], writes=[bbc])
                e1, be1 = eb[k2], beb[k2]
                P.op("act", lambda e, e1=e1, bc=bc: e.activation(out=e1[:], in_=bc[:], func=AF.Exp), reads=[bbc], writes=[be1])
                P.op("dve", lambda e, e1=e1, hh=hh: e.tensor_copy(
                    out=elast[:, hh, :], in_=e1[:].rearrange("p (c t) -> p c t", t=64)[:, :, 63]),
                    reads=[be1], writes=[bel[hh]])
                kf, bkf = ktf[k2], bktf[k2]
                P.op("act", lambda e, kf=kf, bc=bc: e.activation(out=kf[:], in_=bc[:], func=AF.Exp, scale=-1.0),
                     reads=[bbc], writes=[bkf])
                P.op("dve", lambda e, kf=kf, kx=kx: e.tensor_tensor(out=kf[:], in0=kf[:], in1=kx[:], op=ALU.mult),
                     reads=[bkf, bkx], writes=[bkf])
                P.op("act", lambda e, kf=kf, hh=hh: e.activation(out=ktb[:, hh, :], in_=kf[:], func=AF.Copy),
                     reads=[bkf], writes=[bktb[hh]])
                for c in range(NCH):
                    sl = (hh * NCH + c) % 4
                    P.op("pe", lambda e, c=c, kf=kf, sl=sl: e.transpose(
                        out=self.pb[6][0:64, sl * 128:(sl + 1) * 128], in_=kf[:, c * 64:(c + 1) * 64], identity=self.ident[:]),
                        reads=[bkf, self.b_const], writes=[self.pbb[6]])
                    P.op("act", lambda e, c=c, hh=hh, sl=sl: e.activation(
                        out=ktok[0:64, c, hh, :], in_=self.pb[6][0:64, sl * 128:(sl + 1) * 128], func=AF.Copy),
                        reads=[self.pbb[6]], writes=[bktok[c][hh]])
                bk = lrot[0] % 2; lrot[0] += 1
                bank, bbk = self.pb[bk], self.pbb[bk]
                self.lin(ws, wqfg, hh, KC, lambda kc: u[:, kc, :], [bu], bank[:, :TG], bbk)
                P.op("dve", lambda e, hh=hh, bank=bank, e1=e1: e.scalar_tensor_tensor(
                    out=qt[:, hh, :], in0=bank[:, :TG], scalar=SCALE, in1=e1[:], op0=ALU.mult, op1=ALU.mult),
                    reads=[bbk, be1], writes=[bqt[hh]])
                bk = lrot[0] % 2; lrot[0] += 1
                bank, bbk = self.pb[bk], self.pbb[bk]
                self.lin(ws, wqfg, 32 + hh, KC, lambda kc: u[:, kc, :], [bu], bank[:, :TG], bbk)
                P.op("act", lambda e, hh=hh, bank=bank: e.activation(out=gs[:, hh, :], in_=bank[:, :TG], func=AF.Silu),
                     reads=[bbk], writes=[bgs[hh]])
                wm, bm = ws.get(wi, hh)
                for c in range(NCH):
                    bk2 = 2 + (hh * NCH + c) % 2
                    bank, bbk = self.pb[bk2], self.pbb[bk2]
                    for kc in range(KC):
                        P.op("pe", lambda e, kc=kc, c=c, bank=bank: e.matmul(
                            bank[0:64, 0:128], lhsT=u[:, kc, c * 64:(c + 1) * 64], rhs=wm[:, kc, :],
                            start=(kc == 0), stop=(kc == KC - 1)), reads=[bu, bm], writes=[bbk])
                    P.op("act", lambda e, c=c, hh=hh, bank=bank: e.activation(
                        out=vtok[0:64, c, hh * 128:(hh + 1) * 128], in_=bank[0:64, 0:128], func=AF.Copy),
                        reads=[bbk], writes=[bvt[c][hh]])
            import os
            HG_STOP = int(os.environ.get("HG_STOP", "9"))
            for c in range(NCH if HG_STOP >= 2 else 0):
                cs = slice(c * 64, (c + 1) * 64)
                for hh in range(H):
                    sa = so = ss = 0
                    pA, pO, pS = self.pb[hh % 2], self.pb[2 + hh % 2], self.pb[4 + hh % 2]
                    bpA = [self.pbb[hh % 2]]; bpO = [self.pbb[2 + hh % 2]]; bpS = [self.pbb[4 + hh % 2]]
                    at, bat = att[hh % 4], batt[hh % 4]
                    P.op("pe", lambda e, hh=hh, sa=sa, pA=pA: e.matmul(
                        pA[0:64, sa * 64:(sa + 1) * 64], lhsT=ktb[:, hh, cs], rhs=qt[:, hh, cs], start=True, stop=True),
                        reads=[bktb[hh], bqt[hh]], writes=[bpA[sa]])
                    P.op("dve", lambda e, at=at, sa=sa, pA=pA: e.tensor_tensor(
                        out=at[:], in0=pA[0:64, sa * 64:(sa + 1) * 64], in1=tri[:], op=ALU.mult),
                        reads=[bpA[sa], btri], writes=[bat])
                    P.op("pe", lambda e, hh=hh, so=so, at=at, pO=pO: e.matmul(
                        pO[:, so * 64:(so + 1) * 64], lhsT=vtok[0:64, c, hh * 128:(hh + 1) * 128], rhs=at[:],
                        start=True, stop=False), reads=[bvt[c][hh], bat], writes=[bpO[so]])
                    P.op("pe", lambda e, hh=hh, so=so, pO=pO: e.matmul(
                        pO[:, so * 64:(so + 1) * 64], lhsT=Sb[:, hh, :], rhs=qt[:, hh, cs],
                        start=False, stop=True), reads=[bSb[hh], bqt[hh]], writes=[bpO[so]])
                    P.op("act", lambda e, hh=hh, so=so, pO=pO: e.activation(
                        out=oT[:, hh, cs], in_=pO[:, so * 64:(so + 1) * 64], func=AF.Copy),
                        reads=[bpO[so]], writes=[boT[hh]])
                    P.op("pe", lambda e, hh=hh, ss=ss, pS=pS: e.matmul(
                        pS[:, ss * 128:(ss + 1) * 128], lhsT=ktok[0:64, c, hh, :], rhs=vtok[0:64, c, hh * 128:(hh + 1) * 128],
                        start=True, stop=True), reads=[bktok[c][hh], bvt[c][hh]], writes=[bpS[ss]])
                    P.op("dve", lambda e, hh=hh, ss=ss, pS=pS: e.tensor_tensor(
                        out=St[:, hh, :], in0=pS[:, ss * 128:(ss + 1) * 128], in1=St[:, hh, :], op=ALU.add),
                        reads=[bpS[ss], bS[hh]], writes=[bS[hh]])
                    P.op("dve", lambda e, hh=hh: e.tensor_scalar(
                        out=St[:, hh, :], in0=St[:, hh, :], scalar1=elast[:, hh, c:c + 1], scalar2=None, op0=ALU.mult),
                        reads=[bS[hh], bel[hh]], writes=[bS[hh]])
                    P.op("act", lambda e, hh=hh: e.activation(out=Sb[:, hh, :], in_=St[:, hh, :], func=AF.Copy),
                         reads=[bS[hh]], writes=[bSb[hh]])
            for hh in range(H if HG_STOP >= 3 else 0):
                t, bt = tf[hh % 2], btf[hh % 2]
                P.op("act", lambda e, t=t, hh=hh: e.activation(out=t[:], in_=oT[:, hh, :], func=AF.Square),
                     reads=[boT[hh]], writes=[bt])
                bank, bbk = self.pb[7], self.pbb[7]
                P.op("pe", lambda e, t=t: e.matmul(bank[:, :TG], lhsT=self.ones[:], rhs=t[:], start=True, stop=True),
                     reads=[bt, self.b_const], writes=[bbk])
                P.op("act", lambda e: e.activation(out=rs[:], in_=bank[:, :TG], func=AF.Sqrt, bias=self.epsb[:],
                                                   scale=1.0 / 128.0), reads=[bbk, self.b_const], writes=[brs])
                P.op("dve", lambda e: e.reciprocal(out=rs[:], in_=rs[:]), reads=[brs], writes=[brs])
                P.op("dve", lambda e, t=t, hh=hh: e.tensor_tensor(out=t[:], in0=oT[:, hh, :], in1=rs[:], op=ALU.mult),
                     reads=[boT[hh], brs, bt], writes=[bt])
                P.op("dve", lambda e, t=t, hh=hh: e.scalar_tensor_tensor(
                    out=ob[:, hh, :], in0=t[:], scalar=self.v("hg_norm_g", 0), in1=gs[:, hh, :], op0=ALU.mult, op1=ALU.mult),
                    reads=[bt, self.b_vec, bgs[hh]], writes=[bob])
            for mc in range(KC if HG_STOP >= 4 else 0):
                bk = lrot[0] % 2; lrot[0] += 1
                bank, bbk = self.pb[bk], self.pbb[bk]
                self.lin(ws, wo, mc, KC, lambda kc: ob[:, kc, :], [bob], bank[:, :TG], bbk)
                P.op("dve", lambda e, mc=mc, bank=bank: e.scalar_tensor_tensor(
                    out=h[:, mc, :], in0=bank[:, :TG], scalar=self.mod(li, 2, mc), in1=h[:, mc, :],
                    op0=ALU.mult, op1=ALU.add), reads=[bbk, self.b_mods, bh], writes=[bh])
            self.store_h(dst, t0, TG, h, bh)
        P.barrier()


Model._stage_hgrn = _hgrn_stage


def _declare_mla(self):
    S = self.S
    nc = self.nc
    self.inp("mla_w_in", [9, 128, KC, 128])
    self.inp("mla_w_uq", [32, 128, 4, 128])
    self.inp("mla_w_ukv", [32, 128, 4, 128])
    self.inp("mla_w_o", [16, 128, KC, 128])
    self.inp("pswap", [64, 64])
    self.inp("dmask", [128, 128])
    self.inp("rsgn", [128, 1])
    self.QN = nc.dram_tensor("mla_QN", [16, 128, S], BF16)
    self.QR = nc.dram_tensor("mla_QR", [16, 64, S], BF16)
    self.KN = nc.dram_tensor("mla_KN", [16, 128, S], BF16)
    self.KR = nc.dram_tensor("mla_KR", [64, S], BF16)
    self.VT = nc.dram_tensor("mla_VT", [16, S, 128], BF16)
    self.OT = nc.dram_tensor("mla_OT", [128, 16, S], BF16)


Model.declare_mla = _declare_mla

TWO_PI_HI = 6.28125
TWO_PI_LO = 0.0019353071795864769


def _mla_stage(self, li, src, dst):
    P = self.P
    nc = self.nc
    S = self.S
    TG = 512
    H = 16
    NG = S // TG
    w_in, w_uq, w_ukv, w_o = (self.din[k] for k in ("mla_w_in", "mla_w_uq", "mla_w_ukv", "mla_w_o"))
    bscr = Buf()
    SCALE = 192.0 ** -0.5
    with ExitStack() as es:
        sbt = lambda n, s, d: es.enter_context(nc.sbuf_tensor("m1_" + n, list(s), d))
        h = sbt("h", [128, KC, TG], F32); bh = Buf()
        u = sbt("u", [128, KC, TG], BF16); bu = Buf()
        cq = sbt("cq", [128, 4, TG], F32); bcq = Buf()
        ckv = sbt("ckv", [128, 4, TG], F32); bckv = Buf()
        cqn = sbt("cqn", [128, 4, TG], BF16); bcqn = Buf()
        ckvn = sbt("ckvn", [128, 4, TG], BF16); bckvn = Buf()
        kr = sbt("kr", [64, TG], F32); bkr = Buf()
        posi = sbt("posi", [64, TG], I32); bpos = Buf()
        ang = sbt("ang", [64, TG], F32); bang = Buf()
        nn = sbt("nn", [64, TG], F32); bnn = Buf()
        cosT = sbt("cos", [64, TG], F32); bcos = Buf()
        sinT = sbt("sin", [64, TG], F32); bsin = Buf()
        psw = sbt("psw", [64, 64], F32); bpsw = Buf()
        sgn = sbt("sgn", [128, 1], F32)
        rt = [sbt("rt%d" % i, [64, TG], F32) for i in range(2)]; brt = [Buf(), Buf()]
        qrf = [sbt("qrf%d" % i, [64, TG], F32) for i in range(2)]; bqrf = [Buf(), Buf()]
        ob16 = [sbt("ob%d" % i, [128, TG], BF16) for i in range(4)]; bob16 = [Buf() for _ in range(4)]
        vt = [sbt("vt%d" % i, [128, 4, 128], BF16) for i in range(2)]; bvt = [Buf(), Buf()]
        sq = [sbt("sq%d" % i, [128, TG], F32) for i in range(2)]; bsq = [Buf(), Buf()]
        rr = sbt("rr", [128, TG], F32); brr = Buf()
        A, bA = self.make_A(es, li, 0)
        ws = WS(self, es, KC, "m1w")
        ws.preconvert(w_in, 9, KC, "m_in")
        ws.preconvert(w_uq, 32, 4, "m_uq")
        ws.preconvert(w_ukv, 32, 4, "m_ukv")
        nm = self.norm_mod(es, h, bh, TG, A, bA, li, 0, u, bu, "m1n")
        P.dma("sp", lambda e: e.dma_start(out=psw[:], in_=self.din["pswap"].ap()), writes=[bpsw])
        P.dma("sp", lambda e: e.dma_start(out=sgn[:], in_=self.din["rsgn"].ap()), writes=[bpsw])
        rot = [0]
        orot = [0]

        def nbank():
            b = rot[0] % 4
            rot[0] += 1
            return self.pb[b], self.pbb[b]

        def nob():
            k = orot[0] % 4
            orot[0] += 1
            return ob16[k], bob16[k]

        def sincos(dst_t, bdst, phase, sign_scale):
            P.op("dve", lambda e: e.tensor_scalar(out=nn[:], in0=ang[:], scalar1=phase, scalar2=1.0 / (2 * math.pi),
                                                  op0=ALU.add, op1=ALU.mult), reads=[bang], writes=[bnn])
            P.op("dve", lambda e: e.tensor_scalar(out=nn[:], in0=nn[:], scalar1=8388608.0, scalar2=None, op0=ALU.add),
                 reads=[bnn], writes=[bnn])
            P.op("dve", lambda e: e.tensor_scalar(out=nn[:], in0=nn[:], scalar1=-8388608.0, scalar2=None, op0=ALU.add),
                 reads=[bnn], writes=[bnn])
            P.op("dve", lambda e: e.scalar_tensor_tensor(out=dst_t[:], in0=nn[:], scalar=-TWO_PI_HI, in1=ang[:],
                                                         op0=ALU.mult, op1=ALU.add), reads=[bnn, bang], writes=[bdst])
            P.op("dve", lambda e: e.scalar_tensor_tensor(out=dst_t[:], in0=nn[:], scalar=-TWO_PI_LO, in1=dst_t[:],
                                                         op0=ALU.mult, op1=ALU.add), reads=[bnn, bdst], writes=[bdst])
            P.op("dve", lambda e: e.tensor_scalar(out=dst_t[:], in0=dst_t[:], scalar1=phase, scalar2=math.pi,
                                                  op0=ALU.add, op1=ALU.min), reads=[bdst], writes=[bdst])
            P.op("dve", lambda e: e.tensor_scalar(out=dst_t[:], in0=dst_t[:], scalar1=-math.pi, scalar2=None,
                                                  op0=ALU.max), reads=[bdst], writes=[bdst])
            if sign_scale:
                P.op("act", lambda e: e.activation(out=dst_t[:], in_=dst_t[:], func=AF.Sin, scale=sgn[0:64, :]),
                     reads=[bdst, bpsw], writes=[bdst])
            else:
                P.op("act", lambda e: e.activation(out=dst_t[:], in_=dst_t[:], func=AF.Sin), reads=[bdst], writes=[bdst])

        def rope(x, bx, out_t, bout):
            bank, bb = nbank()
            P.op("pe", lambda e: e.matmul(bank[0:64, :TG], lhsT=psw[:], rhs=x[:], start=True, stop=True),
                 reads=[bpsw, bx], writes=[bb])
            t, bt = rt[rot[0] % 2], brt[rot[0] % 2]
            P.op("dve", lambda e: e.tensor_tensor(out=t[:], in0=bank[0:64, :TG], in1=sinT[:], op=ALU.mult),
                 reads=[bb, bsin], writes=[bt])
            P.op("pool", lambda e: e.tensor_tensor(out=x[:], in0=x[:], in1=cosT[:], op=ALU.mult),
                 reads=[bx, bcos], writes=[bx])
            P.op("dve", lambda e: e.tensor_tensor(out=out_t, in0=t[:], in1=x[:], op=ALU.add),
                 reads=[bt, bx], writes=[bout])

        for gi in range(NG):
            t0 = gi * TG
            self.load_h(src, t0, TG, h, bh)
            nm()
            P.dma("sp", lambda e: e.dma_start(out=posi[:], in_=self.din["pos"].ap()[0:1, t0:t0 + TG].partition_broadcast(64)),
                  writes=[bpos])
            P.op("dve", lambda e: e.tensor_copy(out=ang[:], in_=posi[:]), reads=[bpos], writes=[bang])
            P.op("dve", lambda e: e.tensor_scalar(out=ang[:], in0=ang[:], scalar1=self.vec[0:64, self.VOFF["inv_freq"]:self.VOFF["inv_freq"] + 1],
                                                  scalar2=None, op0=ALU.mult), reads=[bang, self.b_vec], writes=[bang])
            sincos(sinT, bsin, 0.0, True)
            sincos(cosT, bcos, math.pi / 2, False)
            for mc in range(9):
                bank, bb = nbank()
                self.lin(ws, w_in, mc, KC, lambda kc: u[:, kc, :], [bu], bank[:, :TG], bb)
                if mc < 4:
                    P.op("act", lambda e, mc=mc, bank=bank: e.activation(out=cq[:, mc, :], in_=bank[:, :TG], func=AF.Copy),
                         reads=[bb], writes=[bcq])
                elif mc < 8:
                    P.op("act", lambda e, mc=mc, bank=bank: e.activation(out=ckv[:, mc - 4, :], in_=bank[:, :TG], func=AF.Copy),
                         reads=[bb], writes=[bckv])
                else:
                    P.op("act", lambda e, bank=bank: e.activation(out=kr[:], in_=bank[0:64, :TG], func=AF.Copy),
                         reads=[bb], writes=[bkr])
            for (cx, bcx, cn, bcn, gname) in ((cq, bcq, cqn, bcqn, "mla_qg"), (ckv, bckv, ckvn, bckvn, "mla_kvg")):
                bank, bb = self.pb[7], self.pbb[7]
                self.colsum(lambda kc, cx=cx: cx[:, kc, :], 4, TG, bank, bb, sq, bsq, extra_reads=[bcx])
                P.op("act", lambda e: e.activation(out=rr[:], in_=bank[:, :TG], func=AF.Sqrt, bias=self.epsb[:],
                                                   scale=1.0 / 512.0), reads=[bb, self.b_const], writes=[brr])
                P.op("dve", lambda e: e.reciprocal(out=rr[:], in_=rr[:]), reads=[brr], writes=[brr])
                for kc in range(4):
                    P.op("dve", lambda e, kc=kc, cx=cx, cn=cn, gname=gname: e.scalar_tensor_tensor(
                        out=cn[:, kc, :], in0=cx[:, kc, :], scalar=self.v(gname, kc), in1=rr[:], op0=ALU.mult, op1=ALU.mult),
                        reads=[bcx, brr, self.b_vec], writes=[bcn])
            o16, bo16 = nob()
            rope(kr, bkr, o16[0:64, :], bo16)
            P.dma("sp", lambda e, o16=o16: e.dma_start(out=self.KR.ap()[:, t0:t0 + TG], in_=o16[0:64, :]),
                  reads=[bo16], writes=[bscr])
            for hh in range(H):
                bank, bb = nbank()
                self.lin(ws, w_uq, 2 * hh, 4, lambda kc: cqn[:, kc, :], [bcqn], bank[:, :TG], bb)
                o16, bo16 = nob()
                P.op("act", lambda e, o16=o16, bank=bank: e.activation(out=o16[:], in_=bank[:, :TG], func=AF.Copy),
                     reads=[bb], writes=[bo16])
                P.dma("sp", lambda e, o16=o16, hh=hh: e.dma_start(out=self.QN.ap()[hh, :, t0:t0 + TG], in_=o16[:]),
                      reads=[bo16], writes=[bscr])
                bank, bb = nbank()
                self.lin(ws, w_uq, 2 * hh + 1, 4, lambda kc: cqn[:, kc, :], [bcqn], bank[:, :TG], bb)
                qf, bqf = qrf[hh % 2], bqrf[hh % 2]
                P.op("act", lambda e, qf=qf, bank=bank: e.activation(out=qf[:], in_=bank[0:64, :TG], func=AF.Copy),
                     reads=[bb], writes=[bqf])
                o16, bo16 = nob()
                rope(qf, bqf, o16[0:64, :], bo16)
                P.dma("sp", lambda e, o16=o16, hh=hh: e.dma_start(out=self.QR.ap()[hh, :, t0:t0 + TG], in_=o16[0:64, :]),
                      reads=[bo16], writes=[bscr])
                bank, bb = nbank()
                self.lin(ws, w_ukv, 2 * hh, 4, lambda kc: ckvn[:, kc, :], [bckvn], bank[:, :TG], bb)
                o16, bo16 = nob()
                P.op("act", lambda e, o16=o16, bank=bank: e.activation(out=o16[:], in_=bank[:, :TG], func=AF.Copy),
                     reads=[bb], writes=[bo16])
                P.dma("sp", lambda e, o16=o16, hh=hh: e.dma_start(out=self.KN.ap()[hh, :, t0:t0 + TG], in_=o16[:]),
                      reads=[bo16], writes=[bscr])
                wm, bm = ws.get(w_ukv, 2 * hh + 1, kcs=4)
                v_, bv_ = vt[hh % 2], bvt[hh % 2]
                bank, bb = nbank()
                for blk in range(4):
                    for kc in range(4):
                        P.op("pe", lambda e, kc=kc, blk=blk, bank=bank: e.matmul(
                            bank[:, blk * 128:(blk + 1) * 128], lhsT=ckvn[:, kc, blk * 128:(blk + 1) * 128], rhs=wm[:, kc, :],
                            start=(kc == 0), stop=(kc == 3)), reads=[bckvn, bm], writes=[bb])
                P.op("act", lambda e, v_=v_, bank=bank: e.activation(
                    out=v_[:], in_=bank[:, :].rearrange("p (b c) -> p b c", b=4), func=AF.Copy), reads=[bb], writes=[bv_])
                P.dma("sp", lambda e, v_=v_, hh=hh: e.dma_start(
                    out=self.VT.ap()[hh, t0:t0 + TG, :].rearrange("(b p) c -> p b c", p=128), in_=v_[:]),
                    reads=[bv_], writes=[bscr])
        P.barrier()
    with ExitStack() as es:
        sbt = lambda n, s, d: es.enter_context(nc.sbuf_tensor("m2_" + n, list(s), d))
        NKB = S // 128
        kn = sbt("kn", [128, S], BF16); bkn = Buf()
        krs = sbt("krs", [64, S], BF16); bkrs = Buf()
        vsb = sbt("vsb", [128, NKB, 130], BF16); bvsb = Buf()
        qn = [sbt("qn%d" % i, [128, TG], BF16) for i in range(2)]; bqn = [Buf(), Buf()]
        qr = [sbt("qr%d" % i, [64, TG], BF16) for i in range(2)]; bqr = [Buf(), Buf()]
        pT = [sbt("pT%d" % i, [128, TG], BF16) for i in range(3)]; bpT = [Buf() for _ in range(3)]
        dmf = sbt("dmf", [128, 128], F32)
        dm = sbt("dm", [128, 128], BF16); bdm = Buf()
        rec = sbt("rec", [128, 4], F32); brec = Buf()
        on = [sbt("on%d" % i, [128, 128], F32) for i in range(2)]; bon = [Buf(), Buf()]
        oT = [sbt("oT%d" % i, [128, TG], BF16) for i in range(2)]; boT = [Buf(), Buf()]
        P.dma("sp", lambda e: e.dma_start(out=dmf[:], in_=self.din["dmask"].ap()), writes=[bdm])
        P.op("dve", lambda e: e.tensor_copy(out=dm[:], in_=dmf[:]), reads=[bdm], writes=[bdm])
        P.op("pool", lambda e: e.memset(vsb[:, :, 128:130], 1.0), writes=[bvsb])
        P.dma("sp", lambda e: e.dma_start(out=krs[:], in_=self.KR.ap()), reads=[bscr], writes=[bkrs])
        pcount = [0]
        for hh in range(H):
            P.dma("sp", lambda e, hh=hh: e.dma_start(out=kn[:], in_=self.KN.ap()[hh]), reads=[bscr], writes=[bkn])
            P.dma("sp", lambda e, hh=hh: e.dma_start(
                out=vsb[:, :, 0:128], in_=self.VT.ap()[hh].rearrange("(b p) c -> p b c", p=128)),
                reads=[bscr], writes=[bvsb])
            for G in range(NG):
                q0 = G * TG
                qi = (hh * NG + G) % 2
                qn_, bqn_, qr_, bqr_ = qn[qi], bqn[qi], qr[qi], bqr[qi]
                P.dma("sp", lambda e, hh=hh, qn_=qn_: e.dma_start(out=qn_[:], in_=self.QN.ap()[hh, :, q0:q0 + TG]),
                      reads=[bscr], writes=[bqn_])
                P.dma("sp", lambda e, hh=hh, qr_=qr_: e.dma_start(out=qr_[:], in_=self.QR.ap()[hh, :, q0:q0 + TG]),
                      reads=[bscr], writes=[bqr_])
                nkb = 4 * (G + 1)

                def qk(kb):
                    j = kb - 4 * G
                    c0 = max(j, 0) * 128
                    bk = pcount[0] % 2
                    p_, bp_ = pT[pcount[0] % 3], bpT[pcount[0] % 3]
                    pcount[0] += 1
                    bank, bb = self.pb[bk], self.pbb[bk]
                    ks = slice(kb * 128, (kb + 1) * 128)
                    P.op("pe", lambda e: e.matmul(bank[:, c0:TG], lhsT=kn[:, ks], rhs=qn_[:, c0:TG], start=True, stop=False),
                         reads=[bkn, bqn_], writes=[bb])
                    P.op("pe", lambda e: e.matmul(bank[:, c0:TG], lhsT=krs[:, ks], rhs=qr_[:, c0:TG], start=False, stop=True),
                         reads=[bkrs, bqr_], writes=[bb])
                    return (kb, j, c0, bank, bb, p_, bp_)

                def pv(ctx):
                    kb, j, c0, bank, bb, p_, bp_ = ctx
                    P.op("act", lambda e: e.activation(out=p_[:, c0:TG], in_=bank[:, c0:TG], func=AF.Exp, scale=SCALE),
                         reads=[bb], writes=[bp_])
                    if j >= 0:
                        P.op("pool", lambda e: e.tensor_tensor(out=p_[:, c0:c0 + 128], in0=p_[:, c0:c0 + 128], in1=dm[:],
                                                               op=ALU.mult), reads=[bp_, bdm], writes=[bp_])
                    for qb in range(max(j, 0), 4):
                        P.op("pe", lambda e, qb=qb: e.matmul(
                            self.pb[2 + qb][:, 0:130], lhsT=p_[:, qb * 128:(qb + 1) * 128], rhs=vsb[:, kb, :],
                            start=(kb == 0), stop=(kb == 4 * G + qb)), reads=[bp_, bvsb], writes=[self.pbb[2 + qb]])

                prev = qk(0)
                for kb in range(1, nkb):
                    cur = qk(kb)
                    pv(prev)
                    prev = cur
                pv(prev)
                o_, bo_ = oT[(hh * NG + G) % 2], boT[(hh * NG + G) % 2]
                for qb in range(4):
                    P.op("dve", lambda e, qb=qb: e.reciprocal(out=rec[:, qb:qb + 1], in_=self.pb[2 + qb][:, 128:129]),
                         reads=[self.pbb[2 + qb]], writes=[brec])
                    n_, bn_ = on[qb % 2], bon[qb % 2]
                    P.op("act", lambda e, qb=qb, n_=n_: e.activation(out=n_[:], in_=self.pb[2 + qb][:, 0:128], func=AF.Copy,
                                                                     scale=rec[:, qb:qb + 1]),
                         reads=[self.pbb[2 + qb], brec], writes=[bn_])
                    P.op("pe", lambda e, n_=n_, qb=qb: e.transpose(out=self.pb[6][:, qb * 128:(qb + 1) * 128], in_=n_[:],
                                                                   identity=self.ident[:]),
                         reads=[bn_, self.b_const], writes=[self.pbb[6]])
                P.op("act", lambda e, o_=o_: e.activation(out=o_[:], in_=self.pb[6][:, :], func=AF.Copy),
                     reads=[self.pbb[6]], writes=[bo_])
                P.dma("sp", lambda e, o_=o_, hh=hh: e.dma_start(out=self.OT.ap()[:, hh, q0:q0 + TG], in_=o_[:]),
                      reads=[bo_], writes=[bscr])
        P.barrier()
    with ExitStack() as es:
        sbt = lambda n, s, d: es.enter_context(nc.sbuf_tensor("m3_" + n, list(s), d))
        h = sbt("h", [128, KC, TG], F32); bh = Buf()
        ob = sbt("ob", [128, KC, TG], BF16); bob = Buf()
        ws = WS(self, es, KC, "m3w")
        ws.preconvert(w_o, 16, KC, "m_o")
        r3 = [0]
        for gi in range(NG):
            t0 = gi * TG
            self.load_h(src, t0, TG, h, bh)
            P.dma("sp", lambda e: e.dma_start(out=ob[:], in_=self.OT.ap()[:, :, t0:t0 + TG]), reads=[bscr], writes=[bob])
            for mc in range(KC):
                bk = r3[0] % 4; r3[0] += 1
                bank, bb = self.pb[bk], self.pbb[bk]
                self.lin(ws, w_o, mc, KC, lambda kc: ob[:, kc, :], [bob], bank[:, :TG], bb)
                P.op("dve", lambda e, mc=mc, bank=bank: e.scalar_tensor_tensor(
                    out=h[:, mc, :], in0=bank[:, :TG], scalar=self.mod(li, 2, mc), in1=h[:, mc, :],
                    op0=ALU.mult, op1=ALU.add), reads=[bb, self.b_mods, bh], writes=[bh])
            self.store_h(dst, t0, TG, h, bh)
        P.barrier()


Model._stage_mla = _mla_stage
```
